# Optimizing a Trainium2 kernel written in Bass

```python
import jax
import jax.numpy as jnp
from jax import lax
import numpy as np

D_MODEL = 2048
BATCH = 4
SEQ = 2048
DEPTH = 4

GRID_W = 64
CTX_LEN = 256
D_FF = 5632
CONV_DIM = 1024
CONV_WIDTH = 31
GMLP_DIM = 1024
GMLP_CHUNK = 128
GMLP_GROUPS = 4
N_HEADS = 16
HEAD_DIM = 64
ATTN_DIM = N_HEADS * HEAD_DIM
WIN_ROWS = 8
WIN_COLS = 16
Q_BLOCK_COLS = 16
N_BRANCH = 3
N_MOD = 9
EPS = 1e-6
NEG_INF = -1e9

O_CONV = 0
O_GMLP = O_CONV + 2 * CONV_DIM
O_Q = O_GMLP + 2 * GMLP_DIM
O_K = O_Q + ATTN_DIM
O_V = O_K + ATTN_DIM
O_GATE = O_V + ATTN_DIM
IN_COLS = O_GATE + N_BRANCH * D_MODEL

kernel_name = 'hybrid_conv_gmlp_natten_prefix_trunk'


def rms_norm(x, g):
    xf = x.astype(jnp.float32)
    y = xf * lax.rsqrt(jnp.mean(xf * xf, axis=-1, keepdims=True) + EPS)
    return (y * g.astype(jnp.float32)).astype(x.dtype)


def layer_norm(x, g, b):
    xf = x.astype(jnp.float32)
    mu = jnp.mean(xf, axis=-1, keepdims=True)
    var = jnp.mean(jnp.square(xf - mu), axis=-1, keepdims=True)
    y = (xf - mu) * lax.rsqrt(var + EPS) * g.astype(jnp.float32) + b.astype(jnp.float32)
    return y.astype(x.dtype)


def modulate(h, shift, scale):
    return h * (1 + scale) + shift


def swiglu(h, w_gu, w_down):
    gu = h @ w_gu
    return (jax.nn.silu(gu[..., :D_FF]) * gu[..., D_FF:]) @ w_down


def ffn_sublayer(s, mod, j, norm_g, w_gu, w_down):
    h = modulate(rms_norm(s, norm_g), mod[3 * j], mod[3 * j + 1])
    return s + 0.5 * mod[3 * j + 2] * swiglu(h, w_gu, w_down)


def heads(t):
    return t.reshape(t.shape[0], t.shape[1], N_HEADS, HEAD_DIM)


def conv_module(a, dw, db, ln_g, ln_b):
    a = a[..., :CONV_DIM] * jax.nn.sigmoid(a[..., CONV_DIM:])
    y = lax.conv_general_dilated(
        a, dw[:, None, :].astype(a.dtype), window_strides=(1,),
        padding=[(CONV_WIDTH // 2, CONV_WIDTH // 2)],
        dimension_numbers=('NWC', 'WIO', 'NWC'), feature_group_count=CONV_DIM)
    return jax.nn.silu(layer_norm(y + db, ln_g, ln_b))


def gmlp_module(z, ln_g, ln_b, ws, bs):
    z = jax.nn.gelu(z)
    u, v = z[..., :GMLP_DIM], z[..., GMLP_DIM:]
    v = layer_norm(v, ln_g, ln_b)
    b, L, _ = v.shape
    v = v.reshape(b, L // GMLP_CHUNK, GMLP_CHUNK, GMLP_GROUPS, GMLP_DIM // GMLP_GROUPS)
    s = jnp.einsum('gpq,bnqgc->bnpgc', ws, v) + bs.T[None, None, :, :, None]
    return u * s.reshape(b, L, GMLP_DIM)


def neighbourhood_index(rows):
    kr = min(WIN_ROWS, rows)
    n_cb = GRID_W // Q_BLOCK_COLS
    kw = Q_BLOCK_COLS + WIN_COLS
    r = np.arange(rows)
    row_start = np.clip(r - kr // 2, 0, rows - kr)
    key_rows = row_start[:, None] + np.arange(kr)[None, :]
    qcol = np.arange(GRID_W).reshape(n_cb, Q_BLOCK_COLS)
    col_start = np.clip(qcol - WIN_COLS // 2, 0, GRID_W - WIN_COLS)
    blk_start = np.clip(qcol[:, 0] - WIN_COLS // 2, 0, GRID_W - kw)
    key_cols = blk_start[:, None] + np.arange(kw)[None, :]
    flat = key_rows[:, None, :, None] * GRID_W + key_cols[None, :, None, :]
    kc = key_cols[:, None, :]
    col_ok = (kc >= col_start[:, :, None]) & (kc < col_start[:, :, None] + WIN_COLS)
    row_off = key_rows - r[:, None] + WIN_ROWS - 1
    col_off = np.clip(kc - qcol[:, :, None], -(WIN_COLS - 1), WIN_COLS - 1) + WIN_COLS - 1
    return kr, kw, flat.reshape(-1), col_ok, row_off, col_off


def neighbourhood_attention(q, k, v, k_ctx, v_ctx, rpb):
    b, L, h, dh = q.shape
    rows = L // GRID_W
    n_cb = GRID_W // Q_BLOCK_COLS
    kr, kw, flat, col_ok, row_off, col_off = neighbourhood_index(rows)
    n_loc = kr * kw
    qb = q.reshape(b, rows, n_cb, Q_BLOCK_COLS, h, dh) * (dh ** -0.5)
    kg = jnp.take(k, flat, axis=1).reshape(b, rows, n_cb, n_loc, h, dh)
    vg = jnp.take(v, flat, axis=1).reshape(b, rows, n_cb, n_loc, h, dh)
    s_loc = jnp.einsum('brjqhd,brjkhd->bhrjqk', qb, kg).astype(jnp.float32)
    bias = rpb[:, row_off[:, None, None, :, None], col_off[None, :, :, None, :]].astype(jnp.float32)
    bias = jnp.where(col_ok[None, :, :, None, :], bias, NEG_INF)
    s_loc = s_loc + bias.reshape(h, rows, n_cb, Q_BLOCK_COLS, n_loc)[None]
    s_ctx = jnp.einsum('brjqhd,bkhd->bhrjqk', qb, k_ctx).astype(jnp.float32)
    p = jax.nn.softmax(jnp.concatenate([s_loc, s_ctx], axis=-1), axis=-1).astype(v.dtype)
    o = (jnp.einsum('bhrjqk,brjkhd->brjqhd', p[..., :n_loc], vg)
         + jnp.einsum('bhrjqk,bkhd->brjqhd', p[..., n_loc:], v_ctx))
    return o.reshape(b, L, h * dh)


def context_attention(q, k, v):
    b, L, h, dh = q.shape
    s = jnp.einsum('bqhd,bkhd->bhqk', q * (dh ** -0.5), k).astype(jnp.float32)
    p = jax.nn.softmax(s, axis=-1).astype(v.dtype)
    return jnp.einsum('bhqk,bkhd->bqhd', p, v).reshape(b, L, h * dh)


def mix_branches(p, attn, lp):
    conv = conv_module(p[..., O_CONV:O_GMLP], lp['conv_dw'], lp['conv_db'], lp['conv_ln_g'], lp['conv_ln_b'])
    gm = gmlp_module(p[..., O_GMLP:O_Q], lp['gmlp_ln_g'], lp['gmlp_ln_b'], lp['gmlp_ws'], lp['gmlp_bs'])
    g = jax.nn.sigmoid(p[..., O_GATE:].astype(jnp.float32)).astype(p.dtype)
    y = (g[..., :D_MODEL] * (conv @ lp['w_conv_out'])
         + g[..., D_MODEL:2 * D_MODEL] * (gm @ lp['w_gmlp_out'])
         + g[..., 2 * D_MODEL:] * (attn @ lp['w_attn_out']))
    return y @ lp['w_out']


def hybrid_layer(x, cx, mod_x, mod_c, lp, update_ctx):
    x = ffn_sublayer(x, mod_x, 0, lp['norm_ffn1'], lp['ffn1_w_gu'], lp['ffn1_w_down'])
    cx = ffn_sublayer(cx, mod_c, 0, lp['norm_ffn1'], lp['ffn1_w_gu'], lp['ffn1_w_down'])
    hx = modulate(rms_norm(x, lp['norm_mix']), mod_x[3], mod_x[4])
    hc = modulate(rms_norm(cx, lp['norm_mix']), mod_c[3], mod_c[4])
    px = hx @ lp['w_in']
    if update_ctx:
        pc = hc @ lp['w_in']
        kv_c = pc[..., O_K:O_GATE]
    else:
        kv_c = hc @ lp['w_in'][:, O_K:O_GATE]
    k_c = heads(kv_c[..., :ATTN_DIM])
    v_c = heads(kv_c[..., ATTN_DIM:])
    attn_x = neighbourhood_attention(heads(px[..., O_Q:O_K]), heads(px[..., O_K:O_V]),
                                     heads(px[..., O_V:O_GATE]), k_c, v_c, lp['attn_rpb'])
    x = x + mod_x[5] * mix_branches(px, attn_x, lp)
    x = ffn_sublayer(x, mod_x, 2, lp['norm_ffn2'], lp['ffn2_w_gu'], lp['ffn2_w_down'])
    if update_ctx:
        attn_c = context_attention(heads(pc[..., O_Q:O_K]), k_c, v_c)
        cx = cx + mod_c[5] * mix_branches(pc, attn_c, lp)
        cx = ffn_sublayer(cx, mod_c, 2, lp['norm_ffn2'], lp['ffn2_w_gu'], lp['ffn2_w_down'])
    return x, cx


def _normal(key, shape, scale):
    return jax.random.normal(key, shape, jnp.float32) * scale


def setup_inputs(seed: int = 0) -> dict:
    key = jax.random.key(seed)
    ks = jax.random.split(key, 28)
    D = D_MODEL
    return {
        'x': _normal(ks[0], (BATCH, SEQ, D), 1.0),
        'c': _normal(ks[1], (BATCH, D), 1.0),
        'ctx': _normal(ks[2], (BATCH, CTX_LEN, D), 1.0),
        'c_ctx': _normal(ks[3], (D,), 1.0),
        'w_mod': _normal(ks[4], (DEPTH, D, N_MOD * D), 0.5 * D ** -0.5),
        'b_mod': _normal(ks[5], (DEPTH, N_MOD * D), 0.01),
        'norm_ffn1': 1.0 + _normal(ks[6], (DEPTH, D), 0.01),
        'ffn1_w_gu': _normal(ks[7], (DEPTH, D, 2 * D_FF), D ** -0.5),
        'ffn1_w_down': _normal(ks[8], (DEPTH, D_FF, D), D_FF ** -0.5),
        'norm_mix': 1.0 + _normal(ks[9], (DEPTH, D), 0.01),
        'w_in': _normal(ks[10], (DEPTH, D, IN_COLS), D ** -0.5),
        'conv_dw': _normal(ks[11], (DEPTH, CONV_WIDTH, CONV_DIM), CONV_WIDTH ** -0.5),
        'conv_db': _normal(ks[12], (DEPTH, CONV_DIM), 0.01),
        'conv_ln_g': 1.0 + _normal(ks[13], (DEPTH, CONV_DIM), 0.01),
        'conv_ln_b': _normal(ks[14], (DEPTH, CONV_DIM), 0.01),
        'gmlp_ln_g': 1.0 + _normal(ks[15], (DEPTH, GMLP_DIM), 0.01),
        'gmlp_ln_b': _normal(ks[16], (DEPTH, GMLP_DIM), 0.01),
        'gmlp_ws': _normal(ks[17], (DEPTH, GMLP_GROUPS, GMLP_CHUNK, GMLP_CHUNK), GMLP_CHUNK ** -0.5),
        'gmlp_bs': 1.0 + _normal(ks[18], (DEPTH, GMLP_GROUPS, GMLP_CHUNK), 0.01),
        'attn_rpb': _normal(ks[19], (DEPTH, N_HEADS, 2 * WIN_ROWS - 1, 2 * WIN_COLS - 1), 0.1),
        'w_conv_out': _normal(ks[20], (DEPTH, CONV_DIM, D), CONV_DIM ** -0.5),
        'w_gmlp_out': _normal(ks[21], (DEPTH, GMLP_DIM, D), GMLP_DIM ** -0.5),
        'w_attn_out': _normal(ks[22], (DEPTH, ATTN_DIM, D), ATTN_DIM ** -0.5),
        'w_out': _normal(ks[23], (DEPTH, D, D), D ** -0.5),
        'norm_ffn2': 1.0 + _normal(ks[24], (DEPTH, D), 0.01),
        'ffn2_w_gu': _normal(ks[25], (DEPTH, D, 2 * D_FF), D ** -0.5),
        'ffn2_w_down': _normal(ks[26], (DEPTH, D_FF, D), D_FF ** -0.5),
        'final_norm': 1.0 + _normal(ks[27], (D,), 0.01),
    }


def reference(x, c, ctx, c_ctx, w_mod, b_mod, norm_ffn1, ffn1_w_gu, ffn1_w_down, norm_mix, w_in,
              conv_dw, conv_db, conv_ln_g, conv_ln_b, gmlp_ln_g, gmlp_ln_b, gmlp_ws, gmlp_bs,
              attn_rpb, w_conv_out, w_gmlp_out, w_attn_out, w_out, norm_ffn2, ffn2_w_gu,
              ffn2_w_down, final_norm):
    sc = jax.nn.silu(c)
    scc = jax.nn.silu(c_ctx)
    cx = ctx
    for i in range(DEPTH):
        mod_x = (sc @ w_mod[i] + b_mod[i]).reshape(-1, N_MOD, D_MODEL).transpose(1, 0, 2)[:, :, None, :]
        mod_c = (scc @ w_mod[i] + b_mod[i]).reshape(N_MOD, 1, 1, D_MODEL)
        lp = {
            'norm_ffn1': norm_ffn1[i], 'ffn1_w_gu': ffn1_w_gu[i], 'ffn1_w_down': ffn1_w_down[i],
            'norm_mix': norm_mix[i], 'w_in': w_in[i],
            'conv_dw': conv_dw[i], 'conv_db': conv_db[i], 'conv_ln_g': conv_ln_g[i], 'conv_ln_b': conv_ln_b[i],
            'gmlp_ln_g': gmlp_ln_g[i], 'gmlp_ln_b': gmlp_ln_b[i], 'gmlp_ws': gmlp_ws[i], 'gmlp_bs': gmlp_bs[i],
            'attn_rpb': attn_rpb[i], 'w_conv_out': w_conv_out[i], 'w_gmlp_out': w_gmlp_out[i],
            'w_attn_out': w_attn_out[i], 'w_out': w_out[i],
            'norm_ffn2': norm_ffn2[i], 'ffn2_w_gu': ffn2_w_gu[i], 'ffn2_w_down': ffn2_w_down[i],
        }
        x, cx = hybrid_layer(x, cx, mod_x, mod_c, lp, i < DEPTH - 1)
    return rms_norm(x, final_norm)
```

```python
import numpy as np
from contextlib import ExitStack
import concourse.bass as bass
import concourse.mybir as mybir
from concourse.bass_utils import run_bass_kernel_spmd

F32 = mybir.dt.float32
BF16 = mybir.dt.bfloat16
BIGW = {"wgu1": (44 * 128, 4096), "wd1": (32 * 128, 2816), "wconv": (8 * 128, 4096), "wfm1": (24 * 128, 2048),
        "wintm": (4 * 128, 8192), "wmix": (16 * 128, 9216), "wout": (16 * 128, 2048), "wgu2": (44 * 128, 4096),
        "wd2": (32 * 128, 2816)}
WOFF = {}
_o = 0
for _k, (_r, _c) in BIGW.items():
    WOFF[_k] = _o
    _o += _r * _c
BLK = 2 * (1 << 20)
NBW = ((_o + BLK - 1) // BLK + 1) // 2 * 2
Q_GROUPS = [[0, 1, 2, 3], [4, 5, 6, 7]]
P4_GROUPS = [[0, 4], [1, 5], [2, 6], [3, 7]]
MODW = 96
PUMP = 7
AF = mybir.ActivationFunctionType
ALU = mybir.AluOpType
AX = mybir.AxisListType

D = 2048
KC = 16
DFF = 5632
HFC = 22
T = 1280
TL = 1024
EPS = 1e-6
TBS = [(0, 512, 0), (512, 512, 0), (1024, 256, 1)]
WS_ROWS = [0, 2, 4, 6, 8, 10, 12, 12]
KLOC = 768
AW = 1344
N_CORES = 8
DEPTH = 4


def _key(sem):
    return id(sem)


class Eng:
    def __init__(self, k, raw, name):
        self.k, self.raw, self.name = k, raw, name
        self.sem = k.es.enter_context(k.nc.semaphore("tick_" + name))
        self.n = 0
        self.seen = {}

    def wait(self, *deps):
        for d in deps:
            if d is None:
                continue
            if isinstance(d, list):
                self.wait(*d)
                continue
            sem, v = d
            kk = _key(sem)
            if self.seen.get(kk, 0) < v:
                self.raw.wait_ge(sem, v)
                self.seen[kk] = v

    def mark(self, ins):
        self.n += 1
        ins.then_inc(self.sem, 1)
        return (self.sem, self.n)


class Slot:
    def __init__(self, k, name):
        self.sem = k.es.enter_context(k.nc.semaphore("dma_" + name))
        self.n = 0

    def dma(self, eng, out, in_):
        ins = eng.raw.dma_start(out=out, in_=in_)
        self.n += 16
        ins.then_inc(self.sem, 16)
        return (self.sem, self.n)

    def dep(self):
        return (self.sem, self.n)


class K:
    def __init__(self, layers, final, dbg=None, ncores=8):
        self.dbg = dbg
        self.ncores = ncores
        self.layers = layers
        self.NL = len(layers)
        self.final = final
        self.nc = bass.Bass("TRN2", target_bir_lowering=False)
        self.es = ExitStack()
        self.uid = 0
        self.slots = {}
        self.bar_n = 0
        self.ccn = 0

    def uniq(self, s):
        self.uid += 1
        return "%s_%d" % (s, self.uid)

    def slot(self, name):
        if name not in self.slots:
            self.slots[name] = Slot(self, name)
        return self.slots[name]

    def sb(self, st, name, shape, dt):
        return st.enter_context(self.nc.sbuf_tensor(self.uniq(name), shape, dt))

    def act(self, out, in_, func, bias=None, scale=None, deps=(), accum_out=None):
        self.actE.wait(*deps)
        kw = {}
        if bias is not None:
            kw["bias"] = bias
        if scale is not None:
            kw["scale"] = scale
        if accum_out is not None:
            kw["accum_out"] = accum_out
        return self.actE.mark(self.nc.scalar.activation(out=out, in_=in_, func=func, **kw))

    def tt(self, out, in0, in1, op, deps=(), eng=None):
        e = eng or self.dve
        e.wait(*deps)
        return e.mark(e.raw.tensor_tensor(out=out, in0=in0, in1=in1, op=op))

    def stt(self, out, in0, scalar, in1, op0, op1, deps=()):
        self.dve.wait(*deps)
        return self.dve.mark(self.nc.vector.scalar_tensor_tensor(out=out, in0=in0, scalar=scalar, in1=in1, op0=op0, op1=op1))

    def ts(self, out, in0, s1, s2, op0, op1=None, deps=(), eng=None):
        e = eng or self.dve
        e.wait(*deps)
        if op1 is None:
            return e.mark(e.raw.tensor_scalar(out=out, in0=in0, scalar1=s1, scalar2=None, op0=op0))
        return e.mark(e.raw.tensor_scalar(out=out, in0=in0, scalar1=s1, scalar2=s2, op0=op0, op1=op1))

    def cp(self, out, in_, deps=(), eng=None):
        e = eng or self.dve
        e.wait(*deps)
        if e is self.actE:
            return e.mark(self.nc.scalar.copy(out=out, in_=in_))
        return e.mark(e.raw.tensor_copy(out=out, in_=in_))

    def recip(self, out, in_, deps=()):
        self.dve.wait(*deps)
        return self.dve.mark(self.nc.vector.reciprocal(out=out, in_=in_))

    def mm(self, out, lhsT, rhs, start, stop):
        return self.nc.tensor.matmul(out, lhsT=lhsT, rhs=rhs, start=start, stop=stop)

    def slab(self, name, li, s):
        cols = BIGW[name][1]
        flat = self.Wf[li].ap().rearrange("r c -> (r c)")
        o = WOFF[name] + s * 128 * cols
        return flat[o:o + 128 * cols].rearrange("(p c) -> p c", c=cols)

    def wdep(self, name, li):
        rows, cols = BIGW[name]
        m_last = min((WOFF[name] + rows * cols - 1) // BLK + 2, NBW - 1)
        return (self.wg_sem, self.s2_seq[(li, m_last)] + 1)

    def make_jobs(self):
        self.s2_seq = {}
        seq = 0
        for li in range(self.NL):
            for i in range(NBW // 2):
                self.jobs.append(("cp", li, i))
            for i in range(NBW // 2):
                self.jobs.append(("s1", li, i))
                seq += 1
            for m in range(NBW):
                self.jobs.append(("s2", li, m))
                self.s2_seq[(li, m)] = seq
                seq += 1
        self.s1_end = {}
        c = 0
        for li in range(self.NL):
            c += NBW // 2
            self.s1_end[li] = c
            c += NBW

    def pump(self, quota, upto_layer=None):
        nc = self.nc
        n = 0
        while self.job_i < len(self.jobs) and n < quota:
            kind, li, i = self.jobs[self.job_i]
            if upto_layer is not None and li > upto_layer:
                break
            if kind == "cp":
                self.slot("wgcp").dma(self.pool, self.Wex[li].ap()[i * 256:(i + 1) * 256, :], self.wsh[li, i * 256:(i + 1) * 256, :])
            elif kind == "s1":
                if i == 0:
                    self.pool.wait(self.slot("wgcp").dep())
                nc.gpsimd.collective_compute("AllGather", ALU.bypass, replica_groups=Q_GROUPS,
                                             ins=[self.Wex[li].ap()[i * 256:(i + 1) * 256, :]],
                                             outs=[self.Wh[li].ap()[i * 1024:(i + 1) * 1024, :]]).then_inc(self.wg_sem, 1)
                n += 1
            else:
                if i == 0:
                    self.pool.wait((self.wg_sem, self.s1_end[li]))
                nc.gpsimd.collective_compute("AllGather", ALU.bypass, replica_groups=P4_GROUPS,
                                             ins=[self.Wh[li].ap()[i * 512:(i + 1) * 512, :]],
                                             outs=[self.Wf[li].ap()[i * 1024:(i + 1) * 1024, :]]).then_inc(self.wg_sem, 1)
                n += 1
            self.job_i += 1

    def prologue_mods(self):
        nc = self.nc
        with ExitStack() as st:
            wb = [self.sb(st, "wmodb", [128, 6 * 2048], BF16) for _ in range(3)]
            slots = [self.slot("w%d" % i) for i in range(3)]
            bm = self.sb(st, "bmodl", [128, self.NL, 18], F32)
            dbm = self.slot("misc").dma(self.sp, bm[:, :, :], self.bmodsh.rearrange("l p c -> p l c"))
            ml = [self.sb(st, "modl", [128, MODW], F32) for _ in range(2)]
            pe_last = {}
            n = 0
            ml_free = [None, None]
            for li in range(self.NL):
                bk = self.next_bank()
                self.pe.wait(self.bank_free[bk])
                ps = self.ps[bk]
                for piece in range(3):
                    b = n % 3
                    n += 1
                    self.pool.wait(pe_last.get(b))
                    d = slots[b].dma(self.pool, wb[b][:, :].rearrange("p (s c) -> p s c", c=2048),
                                     self.wmodsh[li, piece * 6:(piece + 1) * 6].rearrange("s p c -> p s c"))
                    self.pe.wait(d)
                    for o in range(6):
                        j = piece * 6 + o
                        for k in range(KC):
                            ins = self.mm(ps[:, 5 * j:5 * j + 5], wb[b][:, o * 2048 + k * 128:o * 2048 + (k + 1) * 128],
                                          self.scT[:, 5 * k:5 * k + 5], k == 0, k == KC - 1)
                    pe_last[b] = self.pe.mark(ins)
                a = li % 2
                dz = self.pool.mark(nc.gpsimd.memset(ml[a][:, 90:MODW], 0.0)) if li < 2 else None
                self.dve.wait(pe_last[(n - 1) % 3], dbm, ml_free[a], dz)
                for col in range(5):
                    dd = self.dve.mark(nc.vector.tensor_tensor(out=ml[a][:, col:90:5], in0=ps[:, col:90:5], in1=bm[:, li, :], op=ALU.add))
                self.bank_free[bk] = dd
                self.pool.wait(dd)
                dst = self.slot("misc2").dma(self.pool, self.Emod[li].ap()[:, :], ml[a][:, :])
                ml_free[a] = dst
                self.pool.wait(dst)
                nc.gpsimd.collective_compute("AllGather", ALU.bypass, replica_groups=Q_GROUPS,
                                             ins=[self.Emod[li].ap()[:, :]], outs=[self.Hmod[li].ap()[:, :]]).then_inc(self.cc_sem, 1)
                self.ccn += 1
                self.pool.wait((self.cc_sem, self.ccn))
                nc.gpsimd.collective_compute("AllGather", ALU.bypass, replica_groups=P4_GROUPS,
                                             ins=[self.Hmod[li].ap()[:, :]], outs=[self.Fmod[li].ap()[:, :]]).then_inc(self.cc_sem, 1)
                self.ccn += 1
            self.pool.wait((self.cc_sem, self.ccn))
        self.barrier()

    def next_bank(self):
        b = self.bank_i % len(self.ps)
        self.bank_i += 1
        return b

    def barrier(self):
        self.bar_n += 1
        if self.jobs and getattr(self, "cur_layer", None) is not None:
            self.pump(PUMP, upto_layer=self.cur_layer + 1)
        for s in self.slots.values():
            if s.n:
                self.sp.wait(s.dep())
        if self.ccn:
            self.sp.wait((self.cc_sem, self.ccn))
        engs = [self.pe, self.actE, self.dve, self.pool, self.sp]
        for e in engs:
            e.raw.drain().then_inc(self.bar_sem, 1)
        for e in engs:
            e.raw.wait_ge(self.bar_sem, len(engs) * self.bar_n)

    def build(self):
        nc, es = self.nc, self.es
        NL = self.NL
        self.pe = Eng(self, nc.tensor, "pe")
        self.actE = Eng(self, nc.scalar, "act")
        self.dve = Eng(self, nc.vector, "dve")
        self.pool = Eng(self, nc.gpsimd, "pool")
        self.sp = Eng(self, nc.sync, "sp")
        self.wq = self.sp
        self.aq = self.pool
        self.cur_layer = None
        self.bar_sem = es.enter_context(nc.semaphore("bar"))
        self.cc_sem = es.enter_context(nc.semaphore("cc"))

        def din(name, shape):
            return nc.dram_tensor(name, shape, F32, kind="ExternalInput").ap()

        self.xin = din("xin", [128, KC, T])
        self.scin = din("scin", [128, 80])
        self.cmask = din("cmask", [128, 2])
        self.mrow = din("mrow", [128, 5 * KLOC])
        self.normg = din("normg", [NL, 128, 48])
        self.wsh = din("wsh", [NL, NBW // 2 * 256, 2048])
        self.wmodsh = din("wmodsh", [NL, 18, 128, 2048])
        self.bmodsh = din("bmodsh", [NL, 128, 18])
        self.sel = din("sel", [128, 4])
        self.btd = din("bt", [NL, 32 * 128, 768])
        self.Wex = [nc.dram_tensor("w_e%d" % l, [NBW // 2 * 256, 2048], BF16) for l in range(NL)]
        self.Wh = [nc.dram_tensor("w_h%d" % l, [NBW * 512, 2048], BF16) for l in range(NL)]
        self.Wf = [nc.dram_tensor("w_f%d" % l, [NBW * 1024, 2048], BF16) for l in range(NL)]
        self.Emod = [nc.dram_tensor("mod_e%d" % l, [128, MODW], F32) for l in range(NL)]
        self.Hmod = [nc.dram_tensor("mod_h%d" % l, [512, MODW], F32) for l in range(NL)]
        self.Fmod = [nc.dram_tensor("mod_f%d" % l, [1024, MODW], F32) for l in range(NL)]
        self.wg_sem = es.enter_context(nc.semaphore("wg"))
        self.wgn = 0
        self.jobs = []
        self.job_i = 0
        self.convdw = din("convdw", [NL, 128, 248])
        self.convp = din("convp", [NL, 128, 24])
        self.glng = din("glng", [NL, 128, 1024])
        self.glnb = din("glnb", [NL, 128, 1024])
        self.wsT = din("wsT", [NL, 128, 512])
        self.bsb = din("bsb", [NL, 128, 2048])
        self.fnorm = din("fnorm", [128, KC])
        self.ident_in = din("ident", [128, 128])

        self.xT = nc.dram_tensor("xT", [128, KC, T], F32, kind="ExternalOutput").ap()
        if self.final:
            self.outd = nc.dram_tensor("out", [128, KC, TL], F32, kind="ExternalOutput").ap()
        self.a_d = nc.dram_tensor("a_d", [128, 8, T], F32).ap()
        self.q_d = nc.dram_tensor("q_d", [128, 8, T], BF16).ap()
        self.k_d = nc.dram_tensor("k_d", [128, 8, T], BF16).ap()
        self.v_d = nc.dram_tensor("v_d", [10, 128, 1024], BF16).ap()
        self.hT_d = nc.dram_tensor("hT_d", [128, KC, T], BF16).ap()
        dk = {"kind": "ExternalOutput"} if self.dbg else {}
        self.conv_d = nc.dram_tensor("conv_d", [128, 8, T], BF16, **dk).ap()
        self.gm_d = nc.dram_tensor("gm_d", [128, 8, T], BF16, **dk).ap()
        self.attn_d = nc.dram_tensor("attn_d", [128, 8, T], BF16, **dk).ap()
        self.y_d = nc.dram_tensor("y_d", [128, KC, T], BF16, **dk).ap()
        self.EA = nc.dram_tensor("EA", [256, 128], F32)
        self.IA = nc.dram_tensor("IA", [512, 128], F32)
        self.EK = nc.dram_tensor("EK", [256, 2048], BF16)
        self.IK = nc.dram_tensor("IK", [512, 2048], BF16)
        self.EV = nc.dram_tensor("EV", [512, 1024], BF16)
        self.IV = nc.dram_tensor("IV", [1024, 1024], BF16)

        P = lambda name, shape, dt: es.enter_context(nc.sbuf_tensor(name, shape, dt))
        self.ps = [es.enter_context(nc.psum_tensor("ps%d" % i, [128, 512], F32)) for i in range(6)]
        self.pt = [es.enter_context(nc.psum_tensor("pt%d" % i, [128, 1024], BF16)) for i in range(2)]
        self.bank_i = 0
        self.bank_free = [None] * 6
        self.pt_free = [None] * 2
        self.onesf = P("onesf", [128, 128], F32)
        self.identf = P("identf", [128, 128], F32)
        self.identb = P("identb", [128, 128], BF16)
        self.scT = P("scT", [128, 80], BF16)
        self.scf = P("scf", [128, 80], F32)
        self.sel_t = P("sel_t", [128, 4], F32)
        self.modT = P("modT", [128, 2, 144], F32)
        self.normg_t = P("normg_t", [128, 48], F32)
        self.Asc = P("Asc", [128, 3, 2, KC], F32)
        self.CG = P("CG", [128, 3, 2, KC], F32)
        self.fnorm_t = P("fnorm_t", [128, KC], F32)
        self.cmask_t = P("cmask_t", [128, 2], F32)
        self.epsc = P("epsc", [128, 1], F32)
        self.mrow_t = P("mrow_t", [128, 5 * KLOC], F32)

        s0 = self.slot("misc")
        d = [s0.dma(self.sp, self.scf[:, :], self.scin[:, :]),
             s0.dma(self.sp, self.fnorm_t[:, :], self.fnorm[:, :]),
             s0.dma(self.sp, self.cmask_t[:, :], self.cmask[:, :]),
             s0.dma(self.sp, self.mrow_t[:, :], self.mrow[:, :]),
             s0.dma(self.sp, self.sel_t[:, :], self.sel[:, :]),
             s0.dma(self.sp, self.identf[:, :], self.ident_in[:, :])]
        self.pool.mark(nc.gpsimd.memset(self.onesf[:, :], 1.0))
        self.pool.mark(nc.gpsimd.memset(self.epsc[:, :], EPS))
        self.act(self.scT[:, :], self.scf[:, :], AF.Silu, deps=[d[-1]])
        self.cp(self.identb[:, :], self.identf[:, :], deps=[d[-1]])
        self.barrier()

        xsrc = self.xin
        self.prologue_mods()
        self.make_jobs()
        self.pump(10 ** 9, upto_layer=0)
        for li in range(NL):
            self.cur_layer = li
            self.phase_mod(li)
            self.barrier()
            self.phase_ffn(li, 0, xsrc, self.xT)
            xsrc = self.xT
            if self.dbg == "ffn1":
                break
            self.phase_mix(li)
            if self.dbg == "mix":
                break
            self.phase_ffn(li, 2, self.xT, self.xT)
            self.pump(10 ** 9, upto_layer=li + 1)
        if self.final:
            self.phase_final()
            self.barrier()
        return nc

    def phase_mod(self, li):
        nc = self.nc
        with ExitStack() as st:
            ma = self.sb(st, "modall", [128, 8, MODW], F32)
            sm = self.slot("misc")
            sm.dma(self.pool, self.normg_t[:, :], self.normg[li])
            dl = sm.dma(self.pool, ma[:, :, :], self.Fmod[li].ap().rearrange("(c p) f -> p c f", p=128))
            mv = ma[:, :, 0:90].rearrange("p c (j f) -> p c j f", f=5)
            m0 = self.modT[:, 0, :].rearrange("p (c j) -> p c j", j=18)
            m1 = self.modT[:, 1, :].rearrange("p (c j) -> p c j", j=18)
            dd = self.ts(m0, mv[:, :, :, 0], self.sel_t[:, 0:1], None, ALU.mult, deps=[dl])
            for col in range(1, 4):
                dd = self.stt(m0, mv[:, :, :, col], self.sel_t[:, col:col + 1], m0, ALU.mult, ALU.add, deps=[dd])
            dd = self.cp(m1, mv[:, :, :, 4], deps=[dd])
            for j in range(3):
                for tt_ in range(2):
                    sc = self.modT[:, tt_, (3 * j + 1) * 16:(3 * j + 2) * 16]
                    gt = self.modT[:, tt_, (3 * j + 2) * 16:(3 * j + 3) * 16]
                    self.stt(self.Asc[:, j, tt_, :], sc, 1.0, self.normg_t[:, j * 16:(j + 1) * 16], ALU.add, ALU.mult, deps=[dd])
                    self.ts(self.CG[:, j, tt_, :], gt, 0.5 if j != 1 else 1.0, None, ALU.mult, deps=[dd])

    def Bsc(self, j, tt_, k):
        return self.modT[:, tt_, 3 * j * 16 + k:3 * j * 16 + k + 1]

    def phase_norm(self, j, xsrc, hT, final=False):
        nc = self.nc
        NS = 256
        with ExitStack() as st:
            xs = [self.sb(st, "nxs", [128, KC, NS], F32) for _ in range(2)]
            sq = [self.sb(st, "nsq", [128, KC, NS], F32) for _ in range(2)]
            tmp = [self.sb(st, "ntmp", [128, KC, NS], F32) for _ in range(2)]
            std = [self.sb(st, "nstd", [128, NS], F32) for _ in range(2)]
            lsl = [self.slot("nx%d" % i) for i in range(2)]
            ssl = [self.slot("nst%d" % i) for i in range(2)]
            xs_free = [None, None]
            sq_free = [None, None]
            tmp_free = [None, None]
            std_free = [None, None]
            nblk = (TL if final else T) // NS
            for i in range(nblk):
                t0 = i * NS
                tt_ = 0 if t0 < TL else 1
                b = i % 2
                self.aq.wait(xs_free[b])
                ld = lsl[b].dma(self.aq, xs[b][:, :, :], xsrc[:, :, t0:t0 + NS])
                d_sq = self.act(sq[b][:, :, :], xs[b][:, :, :], AF.Square, deps=[ld, sq_free[b]])
                bk = self.next_bank()
                self.pe.wait(self.bank_free[bk], d_sq)
                for k in range(KC):
                    ins = self.mm(self.ps[bk][:, :NS], self.onesf[:, :], sq[b][:, k, :], k == 0, k == KC - 1)
                d_pe = self.pe.mark(ins)
                sq_free[b] = d_pe
                d_std = self.act(std[b][:, :], self.ps[bk][:, :NS], AF.Sqrt, bias=self.epsc[:, 0:1], scale=1.0 / D,
                                 deps=[d_pe, std_free[b]])
                self.bank_free[bk] = d_std
                d_r = self.recip(std[b][:, :], std[b][:, :], deps=[d_std])
                d1 = d2 = None
                for k in range(KC):
                    if final:
                        d1 = self.stt(tmp[b][:, k, :], xs[b][:, k, :], self.fnorm_t[:, k:k + 1], std[b][:, :], ALU.mult, ALU.mult,
                                      deps=[d_r, tmp_free[b]])
                    else:
                        d1 = self.stt(tmp[b][:, k, :], xs[b][:, k, :], self.Asc[:, j, tt_, k:k + 1], std[b][:, :], ALU.mult, ALU.mult,
                                      deps=[d_r, tmp_free[b]])
                        d2 = self.act(hT[:, k, t0:t0 + NS], tmp[b][:, k, :], AF.Identity, bias=self.Bsc(j, tt_, k), deps=[d1])
                xs_free[b] = d1
                std_free[b] = d1
                if final:
                    self.aq.wait(d1)
                    tmp_free[b] = ssl[b].dma(self.aq, self.outd[:, :, t0:t0 + NS], tmp[b][:, :, :])
                else:
                    tmp_free[b] = d2

    def phase_final(self):
        self.phase_norm(0, self.xT, None, final=True)

    def linear_fm(self, wsrc, nslab, slab_cols, units, tbs, epi, nbuf=3, wdep=None):
        if isinstance(units[0], tuple):
            units = [units]
        with ExitStack() as st:
            wb = [self.sb(st, "wslab", [128, slab_cols], BF16) for _ in range(nbuf)]
            slots = [self.slot("w%d" % i) for i in range(nbuf)]
            pe_last = {}
            deps = {}

            def issue(s):
                b = s % nbuf
                self.wq.wait(pe_last.get(b), wdep)
                deps[s] = slots[b].dma(self.wq, wb[b][:, :], wsrc(s))

            for s in range(min(nbuf - 1, nslab)):
                issue(s)
            for s in range(nslab):
                if s + nbuf - 1 < nslab:
                    issue(s + nbuf - 1)
                b = s % nbuf
                self.pe.wait(deps[s])
                pe_dep = None
                for ti, (t0, nt, tt_) in enumerate(tbs):
                    for ui, groups in enumerate(units):
                        banks = []
                        for (src, kcg, off) in groups:
                            bk = self.next_bank()
                            self.pe.wait(self.bank_free[bk])
                            for k in range(kcg):
                                ins = self.mm(self.ps[bk][:, :nt], wb[b][:, off + k * 128:off + (k + 1) * 128],
                                              src[:, k, t0:t0 + nt], k == 0, k == kcg - 1)
                            banks.append(bk)
                        pe_dep = self.pe.mark(ins)
                        rel = epi(s, ti, t0, nt, tt_, ui, [self.ps[bk][:, :nt] for bk in banks], pe_dep)
                        for bk in banks:
                            self.bank_free[bk] = rel
                pe_last[b] = pe_dep

    def linear_tm(self, st, wsrcs, hT, epi, wdep=None):
        wv = [self.sb(st, "wtm", [128, 8192], BF16) for _ in range(2)]
        self.wq.wait(wdep)
        dl = [self.slot("wt%d" % i).dma(self.wq, wv[i][:, :], wsrcs[i]) for i in range(2)]
        self.pe.wait(*dl)
        for tile in range(10):
            banks = []
            for blk in range(2):
                bk = self.next_bank()
                self.pe.wait(self.bank_free[bk])
                for k in range(KC):
                    ins = self.mm(self.ps[bk][:, :], hT[:, k, tile * 128:(tile + 1) * 128], wv[blk][:, k * 512:(k + 1) * 512],
                                  k == 0, k == KC - 1)
                banks.append(bk)
            pe_dep = self.pe.mark(ins)
            rel = epi(tile, [self.ps[bk] for bk in banks], pe_dep)
            for bk in banks:
                self.bank_free[bk] = rel

    def make_residual_epi(self, st, xs_ap, xd_ap, j):
        NR = 4
        xin_t = [self.sb(st, "rxi", [128, 512], F32) for _ in range(NR)]
        xout_t = [self.sb(st, "rxo", [128, 512], F32) for _ in range(NR)]
        lsl = [self.slot("rl%d" % i) for i in range(NR)]
        ssl = [self.slot("rs%d" % i) for i in range(NR)]
        state = {"n": 0, "xin_free": [None] * NR, "st_done": [None] * NR}

        def epi(s, ti, t0, nt, tt_, ui, banks, pe_dep):
            r = state["n"] % NR
            state["n"] += 1
            self.aq.wait(state["xin_free"][r])
            ld = lsl[r].dma(self.aq, xin_t[r][:, :nt], xs_ap[:, s, t0:t0 + nt])
            dd = self.stt(xout_t[r][:, :nt], banks[0], self.CG[:, j, tt_, s:s + 1], xin_t[r][:, :nt], ALU.mult, ALU.add,
                          deps=[pe_dep, ld, state["st_done"][r]])
            state["xin_free"][r] = dd
            self.actE.wait(dd)
            state["st_done"][r] = ssl[r].dma(self.actE, xd_ap[:, s, t0:t0 + nt], xout_t[r][:, :nt])
            return dd
        return epi

    def phase_ffn(self, li, j, xsrc, xdst):
        w = 0 if j == 0 else 1
        with ExitStack() as st:
            hT = self.sb(st, "hT", [128, KC, T], BF16)
            self.phase_norm(j, xsrc, hT)
            self.barrier()
            actT = self.sb(st, "actT", [128, HFC, T], BF16)
            sg = [self.sb(st, "sg", [128, 512], F32) for _ in range(2)]
            for hf in range(2):
                state = {"n": 0, "free": [None, None]}

                def epi_b(s, ti, t0, nt, tt_, ui, banks, pe_dep):
                    b = state["n"] % 2
                    state["n"] += 1
                    d1 = self.act(sg[b][:, :nt], banks[0], AF.Silu, deps=[pe_dep, state["free"][b]])
                    d2 = self.tt(actT[:, s, t0:t0 + nt], sg[b][:, :nt], banks[1], ALU.mult, deps=[d1])
                    state["free"][b] = d2
                    return d2
                self.linear_fm(lambda s: self.slab('wgu%d' % (w + 1), li, hf * HFC + s), HFC, 4096, [(hT, KC, 0), (hT, KC, 2048)], TBS, epi_b, wdep=self.wdep('wgu%d' % (w + 1), li))
                self.barrier()
                with ExitStack() as st2:
                    epi_r = self.make_residual_epi(st2, xsrc if hf == 0 else xdst, xdst, j)
                    self.linear_fm(lambda s: self.slab('wd%d' % (w + 1), li, hf * 16 + s), 16, HFC * 128, [(actT, HFC, 0)], TBS, epi_r, wdep=self.wdep('wd%d' % (w + 1), li))
                    self.barrier()

    def phase_mix(self, li):
        with ExitStack() as stA:
            hT = self.sb(stA, "hT", [128, KC, T], BF16)
            self.phase_norm(1, self.xT, hT)
            self.barrier()
            self.slot("misc").dma(self.aq, self.hT_d[:, :, :], hT[:, :, :])
            self.stage_convin(li, hT)
            self.barrier()
            self.stage_qkv(li, hT)
            self.barrier()
            self.exchange()
            self.stage_gmlp(li, hT)
            self.barrier()
        self.stage_conv(li)
        self.barrier()
        self.stage_attn(li)
        self.barrier()
        self.stage_merge(li)
        self.barrier()
        self.stage_wout(li)
        self.barrier()

    def stage_convin(self, li, hT):
        with ExitStack() as st:
            sgm = [self.sb(st, "cisg", [128, 512], F32) for _ in range(2)]
            ao = [self.sb(st, "ciao", [128, 512], F32) for _ in range(2)]
            ssl = [self.slot("st%d" % i) for i in range(2)]
            state = {"n": 0, "sg_free": [None, None], "st_done": [None, None]}

            def epi(s, ti, t0, nt, tt_, ui, banks, pe_dep):
                b = state["n"] % 2
                state["n"] += 1
                d1 = self.act(sgm[b][:, :nt], banks[1], AF.Sigmoid, deps=[pe_dep, state["sg_free"][b]])
                d2 = self.tt(ao[b][:, :nt], sgm[b][:, :nt], banks[0], ALU.mult, deps=[d1, state["st_done"][b]])
                state["sg_free"][b] = d2
                self.aq.wait(d2)
                state["st_done"][b] = ssl[b].dma(self.aq, self.a_d[:, s, t0:t0 + nt], ao[b][:, :nt])
                return d2
            self.linear_fm(lambda s: self.slab('wconv', li, s), 8, 4096, [(hT, KC, 0), (hT, KC, 2048)], TBS, epi, wdep=self.wdep('wconv', li))

    def stage_qkv(self, li, hT):
        with ExitStack() as st:
            qs = [self.sb(st, "qs", [128, 512], BF16) for _ in range(2)]
            ssl = [self.slot("st%d" % i) for i in range(2)]
            state = {"n": 0, "st_done": [None, None]}

            def epi(s, ti, t0, nt, tt_, ui, banks, pe_dep):
                b = state["n"] % 2
                state["n"] += 1
                isk = s >= 8
                c = s % 8
                if isk:
                    d = self.cp(qs[b][:, :nt], banks[0], deps=[pe_dep, state["st_done"][b]], eng=self.actE)
                else:
                    d = self.ts(qs[b][:, :nt], banks[0], 0.125, None, ALU.mult, deps=[pe_dep, state["st_done"][b]])
                self.aq.wait(d)
                dst = self.k_d if isk else self.q_d
                state["st_done"][b] = ssl[b].dma(self.aq, dst[:, c, t0:t0 + nt], qs[b][:, :nt])
                return d
            self.linear_fm(lambda s: self.slab('wfm1', li, 8 + s), 16, 2048, [(hT, KC, 0)], TBS, epi, wdep=self.wdep('wfm1', li))
            self.barrier()
            vs = [self.sb(st, "vs", [128, 1024], BF16) for _ in range(2)]
            state2 = {"n": 0, "st_done": [None, None]}

            def epi_v(tile, banks, pe_dep):
                b = state2["n"] % 2
                state2["n"] += 1
                d1 = self.cp(vs[b][:, 0:512], banks[0][:, :], deps=[pe_dep, state2["st_done"][b]], eng=self.actE)
                d2 = self.cp(vs[b][:, 512:1024], banks[1][:, :], deps=[pe_dep, state2["st_done"][b]])
                self.aq.wait(d1, d2)
                state2["st_done"][b] = ssl[b].dma(self.aq, self.v_d[tile], vs[b][:, :])
                return [d1, d2]
            self.linear_tm(st, [self.slab('wintm', li, 2), self.slab('wintm', li, 3)], hT, epi_v, wdep=self.wdep('wintm', li))

    def exchange(self):
        nc = self.nc
        s = self.slot("exp")
        EA, EK, EV = self.EA.ap(), self.EK.ap(), self.EV.ap()
        for side, (a0, k0, vt) in enumerate([(0, 0, 0), (TL - 16, TL - 256, 6)]):
            s.dma(self.aq, EA[side * 128:(side + 1) * 128, :].rearrange("p (c j) -> p c j", j=16), self.a_d[:, :, a0:a0 + 16])
            s.dma(self.aq, EK[side * 128:(side + 1) * 128, :].rearrange("p (c j) -> p c j", j=256), self.k_d[:, :, k0:k0 + 256])
            s.dma(self.aq, EV[side * 256:(side + 1) * 256, :].rearrange("(t p) f -> t p f", p=128), self.v_d[vt:vt + 2])
        self.pool.wait(s.dep())
        groups = [[2 * i, 2 * i + 1] for i in range(self.ncores // 2)]
        for (E, I) in ((self.EA, self.IA), (self.EK, self.IK), (self.EV, self.IV)):
            nc.gpsimd.collective_compute("AllGather", ALU.bypass, replica_groups=groups,
                                         ins=[E.ap()[:, :]], outs=[I.ap()[:, :]]).then_inc(self.cc_sem, 1)
            self.ccn += 1
            self.pool.wait((self.cc_sem, self.ccn))

    def stage_gmlp(self, li, hT):
        nc = self.nc
        with ExitStack() as st:
            vln = self.sb(st, "vln", [128, 10, 1024], BF16)
            glng_t = self.sb(st, "glng", [128, 1024], F32)
            glnb_t = self.sb(st, "glnb", [128, 1024], F32)
            wsT_t = self.sb(st, "wsT", [128, 512], BF16)
            bsb_t = self.sb(st, "bsb", [128, 2048], F32)
            sm = self.slot("misc")
            dl = [sm.dma(self.aq, glng_t[:, :], self.glng[li]), sm.dma(self.aq, glnb_t[:, :], self.glnb[li]),
                  sm.dma(self.aq, bsb_t[:, :], self.bsb[li])]
            dws = self.slot("misc2").dma(self.pool, wsT_t[:, :], self.wsT[li])
            with ExitStack() as st1:
                vf = [self.sb(st1, "vf", [128, 1024], F32) for _ in range(2)]
                stats = [self.sb(st1, "vstat", [128, 12], F32) for _ in range(2)]
                mv = [self.sb(st1, "vmv", [128, 2], F32) for _ in range(2)]
                sd = [self.sb(st1, "vsd", [128, 1], F32) for _ in range(2)]
                state = {"n": 0, "free": [None, None]}

                def epi_v(tile, banks, pe_dep):
                    b = state["n"] % 2
                    state["n"] += 1
                    g1 = self.act(vf[b][:, 0:512], banks[0][:, :], AF.Gelu_apprx_tanh, deps=[pe_dep, state["free"][b]])
                    g2 = self.act(vf[b][:, 512:1024], banks[1][:, :], AF.Gelu_apprx_tanh, deps=[pe_dep])
                    self.dve.wait(g1, g2)
                    self.dve.mark(nc.vector.bn_stats(out=stats[b][:, 0:6], in_=vf[b][:, 0:512]))
                    d = self.dve.mark(nc.vector.bn_stats(out=stats[b][:, 6:12], in_=vf[b][:, 512:1024]))
                    self.dve.wait(d)
                    d = self.dve.mark(nc.vector.bn_aggr(out=mv[b][:, :], in_=stats[b][:, :]))
                    d = self.act(sd[b][:, :], mv[b][:, 1:2], AF.Sqrt, bias=self.epsc[:, 0:1], deps=[d])
                    d = self.recip(sd[b][:, :], sd[b][:, :], deps=[d])
                    d = self.ts(vf[b][:, :], vf[b][:, :], mv[b][:, 0:1], sd[b][:, 0:1], ALU.subtract, ALU.mult, deps=[d])
                    d = self.tt(vf[b][:, :], vf[b][:, :], glng_t[:, :], ALU.mult, deps=[d, dl[2]])
                    d = self.tt(vln[:, tile, :], vf[b][:, :], glnb_t[:, :], ALU.add, deps=[d, dl[2]])
                    state["free"][b] = d
                    return [g1, g2]
                self.linear_tm(st1, [self.slab('wintm', li, 0), self.slab('wintm', li, 1)], hT, epi_v, wdep=self.wdep('wintm', li))
                self.barrier()
            uf = [self.sb(st, "guf", [128, 512], F32) for _ in range(2)]
            tmp = [self.sb(st, "gtmp", [128, 512], F32) for _ in range(2)]
            go = [self.sb(st, "ggo", [128, 512], BF16) for _ in range(2)]
            ssl = [self.slot("st%d" % i) for i in range(2)]
            state2 = {"n": 0, "uf_free": [None, None], "st_done": [None, None]}

            def epi_u(s, ti, t0, nt, tt_, ui, banks, pe_dep):
                b = state2["n"] % 2
                state2["n"] += 1
                g = s // 2
                d1 = self.act(uf[b][:, :nt], banks[0], AF.Gelu_apprx_tanh, deps=[pe_dep, state2["uf_free"][b]])
                bk2 = self.next_bank()
                self.pe.wait(self.bank_free[bk2], dws)
                for n_ in range(nt // 128):
                    tile = t0 // 128 + n_
                    ins = self.mm(self.ps[bk2][:, n_ * 128:(n_ + 1) * 128], vln[:, tile, s * 128:(s + 1) * 128],
                                  wsT_t[:, g * 128:(g + 1) * 128], True, True)
                dpe2 = self.pe.mark(ins)
                d2 = self.tt(tmp[b][:, :nt], self.ps[bk2][:, :nt], bsb_t[:, g * 512:g * 512 + nt], ALU.add,
                             deps=[dpe2, dl[2], state2["uf_free"][b]])
                self.bank_free[bk2] = d2
                d3 = self.tt(go[b][:, :nt], tmp[b][:, :nt], uf[b][:, :nt], ALU.mult, deps=[d2, d1, state2["st_done"][b]])
                state2["uf_free"][b] = d3
                self.aq.wait(d3)
                state2["st_done"][b] = ssl[b].dma(self.aq, self.gm_d[:, s, t0:t0 + nt], go[b][:, :nt])
                return d1
            self.linear_fm(lambda s: self.slab('wfm1', li, s), 8, 2048, [(hT, KC, 0)], TBS, epi_u, wdep=self.wdep('wfm1', li))

    def stage_conv(self, li):
        nc = self.nc
        with ExitStack() as st:
            aT = self.sb(st, "caT", [128, 8, AW], F32)
            y = self.sb(st, "cy", [128, 8, T], F32)
            cT = self.sb(st, "ccT", [128, 8, T], BF16)
            dw = self.sb(st, "cdw", [128, 248], F32)
            cpp = self.sb(st, "ccp", [128, 24], F32)
            sq = self.sb(st, "csq", [128, 8, 512], F32)
            mean = self.sb(st, "cmean", [128, 512], F32)
            m2 = self.sb(st, "cm2", [128, 512], F32)
            rstd = self.sb(st, "crstd", [128, 512], F32)
            zt = [self.sb(st, "czt", [128, 512], F32) for _ in range(2)]
            sl = self.slot("misc")
            IA = self.IA.ap()
            dl = [sl.dma(self.aq, dw[:, :], self.convdw[li]), sl.dma(self.aq, cpp[:, :], self.convp[li]),
                  sl.dma(self.aq, aT[:, :, 16:16 + TL], self.a_d[:, :, 0:TL]),
                  sl.dma(self.aq, aT[:, :, 1072:1328], self.a_d[:, :, TL:T]),
                  sl.dma(self.aq, aT[:, :, 0:16], IA[128:256, :].rearrange("p (c j) -> p c j", j=16)),
                  sl.dma(self.aq, aT[:, :, 1040:1056], IA[256:384, :].rearrange("p (c j) -> p c j", j=16))]
            dall = dl[-1]
            dz = self.pool.mark(nc.gpsimd.memset(aT[:, :, 1056:1072], 0.0))
            dz = self.pool.mark(nc.gpsimd.memset(aT[:, :, 1328:1344], 0.0))
            dm = self.ts(aT[:, :, 0:16], aT[:, :, 0:16], self.cmask_t[:, 0:1], None, ALU.mult, deps=[dall])
            dm = self.ts(aT[:, :, 1040:1056], aT[:, :, 1040:1056], self.cmask_t[:, 1:2], None, ALU.mult, deps=[dall])
            self.dve.wait(dz)
            for cc in range(8):
                for (o0, n, yo) in ((0, TL, 0), (1056, 256, TL)):
                    d = self.ts(y[:, cc, yo:yo + n], aT[:, cc, o0 + 1:o0 + 1 + n], dw[:, cc * 31:cc * 31 + 1], cpp[:, cc:cc + 1],
                                ALU.mult, ALU.add, deps=[dm])
                    for k in range(1, 31):
                        d = self.stt(y[:, cc, yo:yo + n], aT[:, cc, o0 + 1 + k:o0 + 1 + k + n], dw[:, cc * 31 + k:cc * 31 + k + 1],
                                     y[:, cc, yo:yo + n], ALU.mult, ALU.add, deps=[d])
            dconv = d
            zfree = [None, None]
            zi = 0
            dlast = None
            for (t0, nt, tt_) in TBS:
                dsq = self.act(sq[:, :, :nt], y[:, :, t0:t0 + nt], AF.Square, deps=[dconv, dlast])
                b1 = self.next_bank()
                self.pe.wait(self.bank_free[b1], dconv)
                for cc in range(8):
                    ins = self.mm(self.ps[b1][:, :nt], self.onesf[:, :], y[:, cc, t0:t0 + nt], cc == 0, cc == 7)
                dp1 = self.pe.mark(ins)
                b2 = self.next_bank()
                self.pe.wait(self.bank_free[b2], dsq)
                for cc in range(8):
                    ins = self.mm(self.ps[b2][:, :nt], self.onesf[:, :], sq[:, cc, :nt], cc == 0, cc == 7)
                dp2 = self.pe.mark(ins)
                dmean = self.act(mean[:, :nt], self.ps[b1][:, :nt], AF.Identity, scale=1.0 / 1024, deps=[dp1, dlast])
                self.bank_free[b1] = dmean
                d = self.tt(m2[:, :nt], mean[:, :nt], mean[:, :nt], ALU.mult, deps=[dmean])
                d = self.stt(m2[:, :nt], self.ps[b2][:, :nt], 1.0 / 1024, m2[:, :nt], ALU.mult, ALU.subtract, deps=[d, dp2])
                self.bank_free[b2] = d
                d = self.act(rstd[:, :nt], m2[:, :nt], AF.Sqrt, bias=self.epsc[:, 0:1], deps=[d])
                drs = self.recip(rstd[:, :nt], rstd[:, :nt], deps=[d])
                for cc in range(8):
                    zb = zi % 2
                    zi += 1
                    d = self.tt(zt[zb][:, :nt], y[:, cc, t0:t0 + nt], mean[:, :nt], ALU.subtract, deps=[drs, zfree[zb]])
                    d = self.tt(zt[zb][:, :nt], zt[zb][:, :nt], rstd[:, :nt], ALU.mult, deps=[d])
                    d = self.act(cT[:, cc, t0:t0 + nt], zt[zb][:, :nt], AF.Silu, bias=cpp[:, 16 + cc:17 + cc], scale=cpp[:, 8 + cc:9 + cc],
                                 deps=[d])
                    zfree[zb] = d
                dlast = d
            self.aq.wait(dlast)
            self.slot("misc").dma(self.aq, self.conv_d[:, :, :], cT[:, :, :])

    def stage_attn(self, li):
        nc = self.nc
        with ExitStack() as st:
            Kb = [self.sb(st, "aK", [128, 1792], BF16) for _ in range(2)]
            Qb = [self.sb(st, "aQ", [128, T], BF16) for _ in range(2)]
            Vb = [self.sb(st, "aV", [128, 14, 128], BF16) for _ in range(2)]
            NBI = 10
            bias_t = [self.sb(st, "abias", [128, KLOC], F32) for _ in range(NBI)]
            S = [self.sb(st, "aS", [128, 1024], F32) for _ in range(2)]
            Pm = [self.sb(st, "aP", [128, 1024], BF16) for _ in range(2)]
            PT = [self.sb(st, "aPT", [128, 1024], BF16) for _ in range(2)]
            Osb = [self.sb(st, "aO", [128, 10, 128], BF16) for _ in range(2)]
            ao = [self.sb(st, "aao", [128, T], BF16) for _ in range(2)]
            nmx = [self.sb(st, "anmx", [128, 1], F32) for _ in range(2)]
            rsum = [self.sb(st, "arsum", [128, 1], F32) for _ in range(2)]
            rinv = [self.sb(st, "arinv", [128, 1], F32) for _ in range(2)]
            kslot = [self.slot("akqv%d" % i) for i in range(2)]
            bt_t = [self.sb(st, "abt", [128, KLOC], F32) for _ in range(4)]
            bslot = [self.slot("ab%d" % i) for i in range(4)]
            bt_free = [None] * 4
            oslot = [self.slot("st%d" % i) for i in range(2)]
            IK, IV = self.IK.ap(), self.IV.ap()
            kqv_free = [None, None]
            bias_free = [None] * NBI
            s_free = [None, None]
            ao_done = [None, None]
            osb_free = [None, None]
            un = 0
            for c in range(8):
                b = c % 2
                self.aq.wait(kqv_free[b])
                sl = kslot[b]
                sl.dma(self.aq, Kb[b][:, 256:1280], self.k_d[:, c, 0:TL])
                sl.dma(self.aq, Kb[b][:, 1536:1792], self.k_d[:, c, TL:T])
                sl.dma(self.aq, Kb[b][:, 0:256], IK[128:256, c * 256:(c + 1) * 256])
                sl.dma(self.aq, Kb[b][:, 1280:1536], IK[256:384, c * 256:(c + 1) * 256])
                sl.dma(self.aq, Qb[b][:, :], self.q_d[:, c, :])
                sl.dma(self.aq, Vb[b][:, 2:10, :], self.v_d[0:8, :, c * 128:(c + 1) * 128].rearrange("t p f -> p t f"))
                sl.dma(self.aq, Vb[b][:, 12:14, :], self.v_d[8:10, :, c * 128:(c + 1) * 128].rearrange("t p f -> p t f"))
                sl.dma(self.aq, Vb[b][:, 0:2, :], IV[256:512, c * 128:(c + 1) * 128].rearrange("(t p) f -> p t f", p=128))
                dkqv = sl.dma(self.aq, Vb[b][:, 10:12, :], IV[512:768, c * 128:(c + 1) * 128].rearrange("(t p) f -> p t f", p=128))
                last_pe = None
                for hh in range(2):
                    h = 2 * c + hh
                    p0 = hh * 64
                    bdeps = []
                    btd = []
                    for v in range(2):
                        ti_ = (h % 2) * 2 + v
                        self.aq.wait(bt_free[ti_])
                        btd.append(bslot[ti_].dma(self.aq, bt_t[ti_][:, :], self.btd[li, (h * 2 + v) * 128:(h * 2 + v + 1) * 128, :]))
                    for slot_i in range(5):
                        bi = (h % 2) * 5 + slot_i
                        v = 1 if slot_i == 4 else 0
                        ti_ = (h % 2) * 2 + v
                        dd_ = self.tt(bias_t[bi][:, :], bt_t[ti_][:, :], self.mrow_t[:, slot_i * KLOC:(slot_i + 1) * KLOC], ALU.add,
                                      deps=[btd[v], bias_free[bi]])
                        bdeps.append(dd_)
                        bt_free[ti_] = dd_
                    for u in range(10):
                        r = un % 2
                        un += 1
                        lat = u < 8
                        qc = u * 128
                        nk = 1024 if lat else 256
                        sA = self.next_bank()
                        sB = self.next_bank() if lat else None
                        oB = self.next_bank()
                        self.pe.wait(dkqv, self.bank_free[sA])
                        qT = Qb[b][p0:p0 + 64, qc:qc + 128]
                        if lat:
                            koff = WS_ROWS[u] * 64
                            self.mm(self.ps[sA][:, :], qT, Kb[b][p0:p0 + 64, koff:koff + 512], True, True)
                            self.pe.wait(self.bank_free[sB])
                            self.mm(self.ps[sB][:, 0:256], qT, Kb[b][p0:p0 + 64, koff + 512:koff + 768], True, True)
                            ins = self.mm(self.ps[sB][:, 256:512], qT, Kb[b][p0:p0 + 64, 1536:1792], True, True)
                        else:
                            ins = self.mm(self.ps[sA][:, 0:256], qT, Kb[b][p0:p0 + 64, 1536:1792], True, True)
                        dS = self.pe.mark(ins)
                        if lat:
                            bi = (h % 2) * 5 + SLOT_OF_RP[u]
                            d1 = self.tt(S[r][:, 0:512], self.ps[sA][:, :], bias_t[bi][:, 0:512], ALU.add,
                                         deps=[dS, bdeps[SLOT_OF_RP[u]], s_free[r]])
                            d2 = self.tt(S[r][:, 512:768], self.ps[sB][:, 0:256], bias_t[bi][:, 512:768], ALU.add, deps=[dS])
                            bias_free[bi] = d2
                            d3 = self.cp(S[r][:, 768:1024], self.ps[sB][:, 256:512], deps=[dS, s_free[r]], eng=self.actE)
                            self.bank_free[sA] = d1
                            self.bank_free[sB] = [d2, d3]
                            dsc = [d1, d2, d3]
                        else:
                            d3 = self.cp(S[r][:, 0:256], self.ps[sA][:, 0:256], deps=[dS, s_free[r]], eng=self.actE)
                            self.bank_free[sA] = d3
                            dsc = [d3]
                        self.dve.wait(*dsc)
                        dmx = self.dve.mark(nc.vector.tensor_reduce(out=nmx[r][:, :], in_=S[r][:, :nk], axis=AX.X, op=ALU.max, negate=True))
                        dex = self.act(Pm[r][:, :nk], S[r][:, :nk], AF.Exp, bias=nmx[r][:, 0:1], deps=[dmx], accum_out=rsum[r][:, 0:1])
                        pb = r
                        self.pe.wait(dex, self.pt_free[pb])
                        for jj in range(nk // 128):
                            ins = nc.tensor.transpose(self.pt[pb][:, jj * 128:(jj + 1) * 128], Pm[r][:, jj * 128:(jj + 1) * 128], self.identb[:, :])
                        dT = self.pe.mark(ins)
                        ceng = self.actE if (un % 2 == 0) else self.dve
                        dcp = self.cp(PT[r][:, :nk], self.pt[pb][:, :nk], deps=[dT], eng=ceng)
                        self.pt_free[pb] = dcp
                        self.pe.wait(dcp, self.bank_free[oB])
                        nj = nk // 128
                        for jj in range(nj):
                            if lat:
                                vt = (WS_ROWS[u] // 2 + jj) if jj < 6 else (12 + jj - 6)
                            else:
                                vt = 12 + jj
                            ins = self.mm(self.ps[oB][:, 0:64], PT[r][:, jj * 128:(jj + 1) * 128], Vb[b][:, vt, p0:p0 + 64], jj == 0, jj == nj - 1)
                        dO = self.pe.mark(ins)
                        last_pe = dO
                        dri = self.recip(rinv[r][:, :], rsum[r][:, :], deps=[dex])
                        dos = self.ts(Osb[b][:, u, p0:p0 + 64], self.ps[oB][:, 0:64], rinv[r][:, 0:1], None, ALU.mult,
                                      deps=[dO, dri, osb_free[b]])
                        self.bank_free[oB] = dos
                        s_free[r] = [dcp, dos, dO]
                kqv_free[b] = last_pe
                for (u0, nu) in ((0, 8), (8, 2)):
                    pb = un % 2
                    un += 1
                    self.pe.wait(dos, self.pt_free[pb])
                    for uu in range(nu):
                        ins = nc.tensor.transpose(self.pt[pb][:, uu * 128:(uu + 1) * 128], Osb[b][:, u0 + uu, :], self.identb[:, :])
                    dT = self.pe.mark(ins)
                    dcp = self.cp(ao[b][:, u0 * 128:(u0 + nu) * 128], self.pt[pb][:, :nu * 128], deps=[dT, ao_done[b]])
                    self.pt_free[pb] = dcp
                osb_free[b] = dT
                self.aq.wait(dcp)
                ao_done[b] = oslot[b].dma(self.aq, self.attn_d[:, c, :], ao[b][:, :])

    def stage_merge(self, li):
        with ExitStack() as st:
            hT = self.sb(st, "mhT", [128, KC, T], BF16)
            bT = [self.sb(st, "mbT", [128, 8, T], BF16) for _ in range(3)]
            sl = self.slot("misc")
            dl = [sl.dma(self.aq, hT[:, :, :], self.hT_d[:, :, :])]
            for i, src in enumerate((self.conv_d, self.gm_d, self.attn_d)):
                dl.append(sl.dma(self.aq, bT[i][:, :, :], src[:, :, :]))
            self.pe.wait(dl[-1])
            sgt = [self.sb(st, "msg", [128, 512], F32) for _ in range(2)]
            tmp = [self.sb(st, "mtmp", [128, 512], F32) for _ in range(2)]
            acc = [self.sb(st, "macc", [128, 512], F32) for _ in range(2)]
            ys = [self.sb(st, "mys", [128, 512], BF16) for _ in range(2)]
            ssl = [self.slot("st%d" % i) for i in range(2)]
            state = {"n": 0, "sg_free": [None, None], "acc_dep": [None, None], "acc_free": [None, None], "st_done": [None, None],
                     "m": 0}

            def epi(s, ti, t0, nt, tt_, ui, banks, pe_dep):
                b = state["n"] % 2
                state["n"] += 1
                a = state["m"] % 2
                d1 = self.act(sgt[b][:, :nt], banks[0], AF.Sigmoid, deps=[pe_dep, state["sg_free"][b]])
                if ui == 0:
                    d2 = self.tt(acc[a][:, :nt], sgt[b][:, :nt], banks[1], ALU.mult, deps=[d1, state["acc_free"][a]])
                    state["acc_dep"][a] = d2
                    state["sg_free"][b] = d2
                    return d2
                d2 = self.tt(tmp[b][:, :nt], sgt[b][:, :nt], banks[1], ALU.mult, deps=[d1])
                if ui == 1:
                    d3 = self.tt(acc[a][:, :nt], acc[a][:, :nt], tmp[b][:, :nt], ALU.add, deps=[d2, state["acc_dep"][a]])
                    state["acc_dep"][a] = d3
                    state["sg_free"][b] = d3
                    return d2
                d3 = self.tt(ys[a][:, :nt], acc[a][:, :nt], tmp[b][:, :nt], ALU.add, deps=[d2, state["acc_dep"][a], state["st_done"][a]])
                state["sg_free"][b] = d3
                state["acc_free"][a] = d3
                state["m"] += 1
                self.aq.wait(d3)
                state["st_done"][a] = ssl[a].dma(self.aq, self.y_d[:, s, t0:t0 + nt], ys[a][:, :nt])
                return d2
            units = [[(hT, KC, br * 2048), (bT[br], 8, 6144 + br * 1024)] for br in range(3)]
            self.linear_fm(lambda s: self.slab('wmix', li, s), 16, 9216, units, TBS, epi, nbuf=2, wdep=self.wdep('wmix', li))

    def stage_wout(self, li):
        with ExitStack() as st:
            yT = self.sb(st, "yT", [128, KC, T], BF16)
            d = self.slot("misc").dma(self.aq, yT[:, :, :], self.y_d[:, :, :])
            self.pe.wait(d)
            epi_r = self.make_residual_epi(st, self.xT, self.xT, 1)
            self.linear_fm(lambda s: self.slab('wout', li, s), 16, 2048, [(yT, KC, 0)], TBS, epi_r, wdep=self.wdep('wout', li))


def fm_slabs(w, kc):
    K_, N_ = w.shape
    return np.ascontiguousarray(w.reshape(kc, 128, N_ // 128, 128).transpose(2, 1, 0, 3)).reshape(N_ // 128, 128, kc * 128)


def fvec(v):
    return np.ascontiguousarray(v.reshape(-1, 128).T)


SLOT_OF_RP = [0, 1, 2, 2, 2, 2, 3, 4]
REP_RP = [0, 1, 2, 6, 7]


def build_bt(rpb):
    ri = np.arange(2)[:, None, None, None]
    c = np.arange(64)[None, :, None, None]
    j = np.arange(12)[None, None, :, None]
    kc = np.arange(64)[None, None, None, :]
    cs = np.clip(c - 8, 0, 48)
    vcol = np.broadcast_to((kc >= cs) & (kc < cs + 16), (2, 64, 12, 64))
    dc = np.broadcast_to(np.clip(kc - c, -15, 15) + 15, (2, 64, 12, 64))
    out = np.empty((16, 2, 128, KLOC), np.float32)
    for v, off in enumerate((3, 1)):
        dr = np.broadcast_to(np.clip(j - ri + off, 0, 14), (2, 64, 12, 64))
        vals = rpb[:, dr, dc]
        vals = np.where(vcol[None], vals, np.float32(-1e9))
        out[:, v] = vals.reshape(16, 128, KLOC)
    return out


def build_mrow(half):
    R0 = 16 * half
    out = np.empty((128, 5, KLOC), np.float32)
    ri = np.arange(2)[:, None, None, None]
    j = np.arange(12)[None, None, :, None]
    for slot, rp in enumerate(REP_RP):
        r = R0 + 2 * rp + ri
        g = WS_ROWS[rp] + j + R0 - 4
        rs = np.clip(r - 4, 0, 24)
        vrow = (g >= 0) & (g < 32) & (g >= rs) & (g < rs + 8)
        m = np.where(np.broadcast_to(vrow, (2, 64, 12, 64)), np.float32(0.0), np.float32(-1e9))
        out[:, slot] = m.reshape(128, KLOC)
    return out.reshape(128, 5 * KLOC)


def prep_shared(inp, layers):
    sh = {}
    L = layers
    st = lambda f: np.stack([f(i) for i in L]).astype(np.float32, copy=False)
    sh["wmod"] = st(lambda i: fm_slabs(inp["w_mod"][i], KC))
    sh["bmod"] = st(lambda i: fvec(inp["b_mod"][i]))
    sh["normg"] = st(lambda i: np.concatenate([fvec(inp["norm_ffn1"][i]), fvec(inp["norm_mix"][i]), fvec(inp["norm_ffn2"][i])], axis=1))
    for nm, src in (("wgu1", "ffn1_w_gu"), ("wgu2", "ffn2_w_gu")):
        sh[nm] = st(lambda i: np.concatenate([fm_slabs(inp[src][i][:, :DFF], KC), fm_slabs(inp[src][i][:, DFF:], KC)], axis=2))
    for nm, src in (("wd1", "ffn1_w_down"), ("wd2", "ffn2_w_down")):
        sh[nm] = st(lambda i: np.concatenate([fm_slabs(inp[src][i][hf * 2816:(hf + 1) * 2816], HFC) for hf in range(2)], axis=0))
    win = inp["w_in"]
    sh["wconv"] = st(lambda i: np.concatenate([fm_slabs(win[i][:, 0:1024], KC), fm_slabs(win[i][:, 1024:2048], KC)], axis=2))
    sh["wfm1"] = st(lambda i: np.concatenate([fm_slabs(win[i][:, 2048:3072], KC), fm_slabs(win[i][:, 4096:5120], KC),
                                              fm_slabs(win[i][:, 5120:6144], KC)], axis=0))

    def tm(w):
        return np.ascontiguousarray(w.reshape(KC, 128, 512).transpose(1, 0, 2)).reshape(128, KC * 512)
    sh["wintm"] = st(lambda i: np.stack([tm(win[i][:, 3072:3584]), tm(win[i][:, 3584:4096]),
                                         tm(win[i][:, 6144:6656]), tm(win[i][:, 6656:7168])]))

    def mix(i):
        g = [fm_slabs(win[i][:, 7168 + b * D:7168 + (b + 1) * D], KC) for b in range(3)]
        o = [fm_slabs(inp[n][i], 8) for n in ("w_conv_out", "w_gmlp_out", "w_attn_out")]
        return np.concatenate(g + o, axis=2)
    sh["wmix"] = st(mix)
    sh["wout"] = st(lambda i: fm_slabs(inp["w_out"][i], KC))
    sh["convdw"] = st(lambda i: np.ascontiguousarray(inp["conv_dw"][i].reshape(31, 8, 128).transpose(2, 1, 0)).reshape(128, 248))
    sh["convp"] = st(lambda i: np.concatenate([fvec(inp["conv_db"][i]), fvec(inp["conv_ln_g"][i]), fvec(inp["conv_ln_b"][i])], axis=1))
    sh["glng"] = st(lambda i: np.broadcast_to(inp["gmlp_ln_g"][i][None, :], (128, 1024)))
    sh["glnb"] = st(lambda i: np.broadcast_to(inp["gmlp_ln_b"][i][None, :], (128, 1024)))
    sh["wsT"] = st(lambda i: np.ascontiguousarray(inp["gmlp_ws"][i].transpose(2, 0, 1)).reshape(128, 512))
    sh["bsb"] = st(lambda i: np.broadcast_to(np.broadcast_to(inp["gmlp_bs"][i][:, None, :], (4, 4, 128)).reshape(1, 2048), (128, 2048)))
    sh["bt"] = st(lambda i: build_bt(inp["attn_rpb"][i])).reshape(len(L), 32 * 128, 768)
    sh["fnorm"] = fvec(inp["final_norm"]).astype(np.float32)
    sh["ident"] = np.eye(128, dtype=np.float32)
    return {k: np.ascontiguousarray(v) for k, v in sh.items()}


def prep_core(inp, layers, core, x_T=None):
    b, half = core // 2, core % 2
    pc = {}
    if x_T is None:
        xl = inp["x"][b, half * TL:(half + 1) * TL, :]
        xc = inp["ctx"][b]
        xa = np.concatenate([xl, xc], axis=0)
        x_T = np.ascontiguousarray(xa.T.reshape(KC, 128, T).transpose(1, 0, 2))
    pc["xin"] = x_T
    sc = np.concatenate([inp["c"].T, inp["c_ctx"][:, None]], axis=1)
    pc["scin"] = np.ascontiguousarray(sc.reshape(KC, 128, 5).transpose(1, 0, 2)).reshape(128, 80)
    sel = np.zeros((128, 4), np.float32)
    sel[:, b] = 1.0
    pc["sel"] = sel
    cm = np.zeros((128, 2), np.float32)
    cm[:, 0] = 1.0 if half == 1 else 0.0
    cm[:, 1] = 1.0 if half == 0 else 0.0
    pc["cmask"] = cm
    pc["mrow"] = build_mrow(half)
    return pc


def shard_flat(sh, nl, core):
    flat = np.concatenate([sh[k].reshape(nl, -1) for k in BIGW], axis=1)
    pad = NBW * BLK - flat.shape[1]
    if pad:
        flat = np.concatenate([flat, np.zeros((nl, pad), np.float32)], axis=1)
    s_, q_ = core // 4, core % 4
    h = flat.reshape(nl, NBW, 2, 512 * 2048)[:, :, s_]
    e = h.reshape(nl, NBW // 2, 4, 256 * 2048)[:, :, q_]
    return np.ascontiguousarray(e).reshape(nl, NBW // 2 * 256, 2048)


def core_maps(inp, layers, cores, xs=None):
    sh = prep_shared(inp, layers)
    nl = len(layers)
    maps = []
    for ci, c in enumerate(cores):
        m = {k: v for k, v in sh.items() if k not in BIGW and k not in ("wmod", "bmod")}
        m["wsh"] = shard_flat(sh, nl, c)
        m["wmodsh"] = np.ascontiguousarray(sh["wmod"][:, 18 * c:18 * (c + 1)])
        m["bmodsh"] = np.ascontiguousarray(sh["bmod"][:, :, 18 * c:18 * (c + 1)])
        m.update(prep_core(inp, layers, c, x_T=None if xs is None else xs[ci]))
        maps.append(m)
    return maps


FUSED = True
_PROGS = {}


def _prog(nl, final):
    key = (nl, final)
    if key not in _PROGS:
        _PROGS[key] = K(list(range(nl)), final, ncores=N_CORES).build()
    return _PROGS[key]


def _assemble(outs):
    B = N_CORES // 2
    y = np.empty((B, 2 * TL, D), np.float32)
    for core in range(N_CORES):
        b, half = core // 2, core % 2
        o = np.asarray(outs[core], dtype=np.float32)
        y[b, half * TL:(half + 1) * TL, :] = o.transpose(2, 1, 0).reshape(TL, D)
    return y


def kernel(**inputs):
    inp = {k: np.asarray(v) for k, v in inputs.items()}
    cores = list(range(N_CORES))
    if FUSED:
        maps = core_maps(inp, list(range(DEPTH)), cores)
        res = run_bass_kernel_spmd(_prog(DEPTH, True), maps, core_ids=cores)
        return _assemble([r["out"] for r in res.results])
    xs = None
    res = None
    for l in range(DEPTH):
        maps = core_maps(inp, [l], cores, xs=xs)
        res = run_bass_kernel_spmd(_prog(1, l == DEPTH - 1), maps, core_ids=cores)
        xs = [np.asarray(r["xT"], dtype=np.float32) for r in res.results]
    return _assemble([r["out"] for r in res.results])
```

```python
import numpy as np
from contextlib import ExitStack
import concourse.bass as bass
import concourse.mybir as mybir
from concourse.bass_utils import run_bass_kernel_spmd

F32 = mybir.dt.float32
BF16 = mybir.dt.bfloat16
BIGW = {"wgu1": (44 * 128, 4096), "wd1": (32 * 128, 2816), "wconv": (8 * 128, 4096), "wfm1": (24 * 128, 2048),
        "wintm": (4 * 128, 8192), "wmix": (16 * 128, 9216), "wout": (16 * 128, 2048), "wgu2": (44 * 128, 4096),
        "wd2": (32 * 128, 2816)}
WOFF = {}
_o = 0
for _k, (_r, _c) in BIGW.items():
    WOFF[_k] = _o
    _o += _r * _c
BLK = 2 * (1 << 20)
NBW = ((_o + BLK - 1) // BLK + 1) // 2 * 2
Q_GROUPS = [[0, 1, 2, 3], [4, 5, 6, 7]]
P4_GROUPS = [[0, 4], [1, 5], [2, 6], [3, 7]]
MODW = 96
PUMP = 7
AF = mybir.ActivationFunctionType
ALU = mybir.AluOpType
AX = mybir.AxisListType

D = 2048
KC = 16
DFF = 5632
HFC = 22
T = 1280
TL = 1024
EPS = 1e-6
TBS = [(0, 512, 0), (512, 512, 0), (1024, 256, 1)]
WS_ROWS = [0, 2, 4, 6, 8, 10, 12, 12]
KLOC = 768
AW = 1344
N_CORES = 8
DEPTH = 4


def _key(sem):
    return id(sem)


class Eng:
    def __init__(self, k, raw, name):
        self.k, self.raw, self.name = k, raw, name
        self.sem = k.es.enter_context(k.nc.semaphore("tick_" + name))
        self.n = 0
        self.seen = {}

    def wait(self, *deps):
        for d in deps:
            if d is None:
                continue
            if isinstance(d, list):
                self.wait(*d)
                continue
            sem, v = d
            kk = _key(sem)
            if self.seen.get(kk, 0) < v:
                self.raw.wait_ge(sem, v)
                self.seen[kk] = v

    def mark(self, ins):
        self.n += 1
        ins.then_inc(self.sem, 1)
        return (self.sem, self.n)


class Slot:
    def __init__(self, k, name):
        self.sem = k.es.enter_context(k.nc.semaphore("dma_" + name))
        self.n = 0

    def dma(self, eng, out, in_):
        ins = eng.raw.dma_start(out=out, in_=in_)
        self.n += 16
        ins.then_inc(self.sem, 16)
        return (self.sem, self.n)

    def dep(self):
        return (self.sem, self.n)


class K:
    def __init__(self, layers, final, dbg=None, ncores=8):
        self.dbg = dbg
        self.ncores = ncores
        self.layers = layers
        self.NL = len(layers)
        self.final = final
        self.nc = bass.Bass("TRN2", target_bir_lowering=False)
        self.es = ExitStack()
        self.uid = 0
        self.slots = {}
        self.bar_n = 0
        self.ccn = 0

    def uniq(self, s):
        self.uid += 1
        return "%s_%d" % (s, self.uid)

    def slot(self, name):
        if name not in self.slots:
            self.slots[name] = Slot(self, name)
        return self.slots[name]

    def sb(self, st, name, shape, dt):
        return st.enter_context(self.nc.sbuf_tensor(self.uniq(name), shape, dt))

    def act(self, out, in_, func, bias=None, scale=None, deps=(), accum_out=None):
        self.actE.wait(*deps)
        kw = {}
        if bias is not None:
            kw["bias"] = bias
        if scale is not None:
            kw["scale"] = scale
        if accum_out is not None:
            kw["accum_out"] = accum_out
        return self.actE.mark(self.nc.scalar.activation(out=out, in_=in_, func=func, **kw))

    def tt(self, out, in0, in1, op, deps=(), eng=None):
        e = eng or self.dve
        e.wait(*deps)
        return e.mark(e.raw.tensor_tensor(out=out, in0=in0, in1=in1, op=op))

    def stt(self, out, in0, scalar, in1, op0, op1, deps=()):
        self.dve.wait(*deps)
        return self.dve.mark(self.nc.vector.scalar_tensor_tensor(out=out, in0=in0, scalar=scalar, in1=in1, op0=op0, op1=op1))

    def ts(self, out, in0, s1, s2, op0, op1=None, deps=(), eng=None):
        e = eng or self.dve
        e.wait(*deps)
        if op1 is None:
            return e.mark(e.raw.tensor_scalar(out=out, in0=in0, scalar1=s1, scalar2=None, op0=op0))
        return e.mark(e.raw.tensor_scalar(out=out, in0=in0, scalar1=s1, scalar2=s2, op0=op0, op1=op1))

    def cp(self, out, in_, deps=(), eng=None):
        e = eng or self.dve
        e.wait(*deps)
        if e is self.actE:
            return e.mark(self.nc.scalar.copy(out=out, in_=in_))
        return e.mark(e.raw.tensor_copy(out=out, in_=in_))

    def recip(self, out, in_, deps=()):
        self.dve.wait(*deps)
        return self.dve.mark(self.nc.vector.reciprocal(out=out, in_=in_))

    def mm(self, out, lhsT, rhs, start, stop):
        return self.nc.tensor.matmul(out, lhsT=lhsT, rhs=rhs, start=start, stop=stop)

    def slab(self, name, li, s):
        cols = BIGW[name][1]
        flat = self.Wf[li].ap().rearrange("r c -> (r c)")
        o = WOFF[name] + s * 128 * cols
        return flat[o:o + 128 * cols].rearrange("(p c) -> p c", c=cols)

    def wdep(self, name, li):
        rows, cols = BIGW[name]
        m_last = min((WOFF[name] + rows * cols - 1) // BLK + 2, NBW - 1)
        return (self.wg_sem, self.s2_seq[(li, m_last)] + 1)

    def make_jobs(self):
        self.s2_seq = {}
        seq = 0
        for li in range(self.NL):
            for i in range(NBW // 2):
                self.jobs.append(("cp", li, i))
            for i in range(NBW // 2):
                self.jobs.append(("s1", li, i))
                seq += 1
            for m in range(NBW):
                self.jobs.append(("s2", li, m))
                self.s2_seq[(li, m)] = seq
                seq += 1
        self.s1_end = {}
        c = 0
        for li in range(self.NL):
            c += NBW // 2
            self.s1_end[li] = c
            c += NBW

    def pump(self, quota, upto_layer=None):
        nc = self.nc
        n = 0
        while self.job_i < len(self.jobs) and n < quota:
            kind, li, i = self.jobs[self.job_i]
            if upto_layer is not None and li > upto_layer:
                break
            if kind == "cp":
                self.slot("wgcp").dma(self.pool, self.Wex[li].ap()[i * 256:(i + 1) * 256, :], self.wsh[li, i * 256:(i + 1) * 256, :])
            elif kind == "s1":
                if i == 0:
                    self.pool.wait(self.slot("wgcp").dep())
                nc.gpsimd.collective_compute("AllGather", ALU.bypass, replica_groups=Q_GROUPS,
                                             ins=[self.Wex[li].ap()[i * 256:(i + 1) * 256, :]],
                                             outs=[self.Wh[li].ap()[i * 1024:(i + 1) * 1024, :]]).then_inc(self.wg_sem, 1)
                n += 1
            else:
                if i == 0:
                    self.pool.wait((self.wg_sem, self.s1_end[li]))
                nc.gpsimd.collective_compute("AllGather", ALU.bypass, replica_groups=P4_GROUPS,
                                             ins=[self.Wh[li].ap()[i * 512:(i + 1) * 512, :]],
                                             outs=[self.Wf[li].ap()[i * 1024:(i + 1) * 1024, :]]).then_inc(self.wg_sem, 1)
                n += 1
            self.job_i += 1

    def prologue_mods(self):
        nc = self.nc
        with ExitStack() as st:
            wb = [self.sb(st, "wmodb", [128, 6 * 2048], BF16) for _ in range(3)]
            slots = [self.slot("w%d" % i) for i in range(3)]
            bm = self.sb(st, "bmodl", [128, self.NL, 18], F32)
            dbm = self.slot("misc").dma(self.sp, bm[:, :, :], self.bmodsh.rearrange("l p c -> p l c"))
            ml = [self.sb(st, "modl", [128, MODW], F32) for _ in range(2)]
            pe_last = {}
            n = 0
            ml_free = [None, None]
            for li in range(self.NL):
                bk = self.next_bank()
                self.pe.wait(self.bank_free[bk])
                ps = self.ps[bk]
                for piece in range(3):
                    b = n % 3
                    n += 1
                    self.pool.wait(pe_last.get(b))
                    d = slots[b].dma(self.pool, wb[b][:, :].rearrange("p (s c) -> p s c", c=2048),
                                     self.wmodsh[li, piece * 6:(piece + 1) * 6].rearrange("s p c -> p s c"))
                    self.pe.wait(d)
                    for o in range(6):
                        j = piece * 6 + o
                        for k in range(KC):
                            ins = self.mm(ps[:, 5 * j:5 * j + 5], wb[b][:, o * 2048 + k * 128:o * 2048 + (k + 1) * 128],
                                          self.scT[:, 5 * k:5 * k + 5], k == 0, k == KC - 1)
                    pe_last[b] = self.pe.mark(ins)
                a = li % 2
                dz = self.pool.mark(nc.gpsimd.memset(ml[a][:, 90:MODW], 0.0)) if li < 2 else None
                self.dve.wait(pe_last[(n - 1) % 3], dbm, ml_free[a], dz)
                for col in range(5):
                    dd = self.dve.mark(nc.vector.tensor_tensor(out=ml[a][:, col:90:5], in0=ps[:, col:90:5], in1=bm[:, li, :], op=ALU.add))
                self.bank_free[bk] = dd
                self.pool.wait(dd)
                dst = self.slot("misc2").dma(self.pool, self.Emod[li].ap()[:, :], ml[a][:, :])
                ml_free[a] = dst
                self.pool.wait(dst)
                nc.gpsimd.collective_compute("AllGather", ALU.bypass, replica_groups=Q_GROUPS,
                                             ins=[self.Emod[li].ap()[:, :]], outs=[self.Hmod[li].ap()[:, :]]).then_inc(self.cc_sem, 1)
                self.ccn += 1
                self.pool.wait((self.cc_sem, self.ccn))
                nc.gpsimd.collective_compute("AllGather", ALU.bypass, replica_groups=P4_GROUPS,
                                             ins=[self.Hmod[li].ap()[:, :]], outs=[self.Fmod[li].ap()[:, :]]).then_inc(self.cc_sem, 1)
                self.ccn += 1
            self.pool.wait((self.cc_sem, self.ccn))
        self.barrier()

    def next_bank(self):
        b = self.bank_i % len(self.ps)
        self.bank_i += 1
        return b

    def barrier(self):
        self.bar_n += 1
        if self.jobs and getattr(self, "cur_layer", None) is not None:
            self.pump(PUMP, upto_layer=self.cur_layer + 1)
        for s in self.slots.values():
            if s.n:
                self.sp.wait(s.dep())
        if self.ccn:
            self.sp.wait((self.cc_sem, self.ccn))
        engs = [self.pe, self.actE, self.dve, self.pool, self.sp]
        for e in engs:
            e.raw.drain().then_inc(self.bar_sem, 1)
        for e in engs:
            e.raw.wait_ge(self.bar_sem, len(engs) * self.bar_n)

    def build(self):
        nc, es = self.nc, self.es
        NL = self.NL
        self.pe = Eng(self, nc.tensor, "pe")
        self.actE = Eng(self, nc.scalar, "act")
        self.dve = Eng(self, nc.vector, "dve")
        self.pool = Eng(self, nc.gpsimd, "pool")
        self.sp = Eng(self, nc.sync, "sp")
        self.wq = self.sp
        self.aq = self.pool
        self.cur_layer = None
        self.bar_sem = es.enter_context(nc.semaphore("bar"))
        self.cc_sem = es.enter_context(nc.semaphore("cc"))

        def din(name, shape):
            return nc.dram_tensor(name, shape, F32, kind="ExternalInput").ap()

        self.xin = din("xin", [128, KC, T])
        self.scin = din("scin", [128, 80])
        self.cmask = din("cmask", [128, 2])
        self.mrow = din("mrow", [128, 5 * KLOC])
        self.normg = din("normg", [NL, 128, 48])
        self.wsh = din("wsh", [NL, NBW // 2 * 256, 2048])
        self.wmodsh = din("wmodsh", [NL, 18, 128, 2048])
        self.bmodsh = din("bmodsh", [NL, 128, 18])
        self.sel = din("sel", [128, 4])
        self.btd = din("bt", [NL, 32 * 128, 768])
        self.Wex = [nc.dram_tensor("w_e%d" % l, [NBW // 2 * 256, 2048], BF16) for l in range(NL)]
        self.Wh = [nc.dram_tensor("w_h%d" % l, [NBW * 512, 2048], BF16) for l in range(NL)]
        self.Wf = [nc.dram_tensor("w_f%d" % l, [NBW * 1024, 2048], BF16) for l in range(NL)]
        self.Emod = [nc.dram_tensor("mod_e%d" % l, [128, MODW], F32) for l in range(NL)]
        self.Hmod = [nc.dram_tensor("mod_h%d" % l, [512, MODW], F32) for l in range(NL)]
        self.Fmod = [nc.dram_tensor("mod_f%d" % l, [1024, MODW], F32) for l in range(NL)]
        self.wg_sem = es.enter_context(nc.semaphore("wg"))
        self.wgn = 0
        self.jobs = []
        self.job_i = 0
        self.convdw = din("convdw", [NL, 128, 248])
        self.convp = din("convp", [NL, 128, 24])
        self.glng = din("glng", [NL, 128, 1024])
        self.glnb = din("glnb", [NL, 128, 1024])
        self.wsT = din("wsT", [NL, 128, 512])
        self.bsb = din("bsb", [NL, 128, 2048])
        self.fnorm = din("fnorm", [128, KC])
        self.ident_in = din("ident", [128, 128])

        self.xT = nc.dram_tensor("xT", [128, KC, T], F32, kind="ExternalOutput").ap()
        if self.final:
            self.outd = nc.dram_tensor("out", [128, KC, TL], F32, kind="ExternalOutput").ap()
        self.a_d = nc.dram_tensor("a_d", [128, 8, T], F32).ap()
        self.q_d = nc.dram_tensor("q_d", [128, 8, T], BF16).ap()
        self.k_d = nc.dram_tensor("k_d", [128, 8, T], BF16).ap()
        self.v_d = nc.dram_tensor("v_d", [10, 128, 1024], BF16).ap()
        self.hT_d = nc.dram_tensor("hT_d", [128, KC, T], BF16).ap()
        dk = {"kind": "ExternalOutput"} if self.dbg else {}
        self.conv_d = nc.dram_tensor("conv_d", [128, 8, T], BF16, **dk).ap()
        self.gm_d = nc.dram_tensor("gm_d", [128, 8, T], BF16, **dk).ap()
        self.attn_d = nc.dram_tensor("attn_d", [128, 8, T], BF16, **dk).ap()
        self.y_d = nc.dram_tensor("y_d", [128, KC, T], BF16, **dk).ap()
        self.EA = nc.dram_tensor("EA", [256, 128], F32)
        self.IA = nc.dram_tensor("IA", [512, 128], F32)
        self.EK = nc.dram_tensor("EK", [256, 2048], BF16)
        self.IK = nc.dram_tensor("IK", [512, 2048], BF16)
        self.EV = nc.dram_tensor("EV", [512, 1024], BF16)
        self.IV = nc.dram_tensor("IV", [1024, 1024], BF16)

        P = lambda name, shape, dt: es.enter_context(nc.sbuf_tensor(name, shape, dt))
        self.ps = [es.enter_context(nc.psum_tensor("ps%d" % i, [128, 512], F32)) for i in range(6)]
        self.pt = [es.enter_context(nc.psum_tensor("pt0", [128, 1024], BF16))]
        self.po = es.enter_context(nc.psum_tensor("po", [128, 512], F32))
        self.po_free = [None] * 8
        self.bank_i = 0
        self.bank_free = [None] * 6
        self.pt_free = [None]
        self.onesf = P("onesf", [128, 128], F32)
        self.identf = P("identf", [128, 128], F32)
        self.identb = P("identb", [128, 128], BF16)
        self.scT = P("scT", [128, 80], BF16)
        self.scf = P("scf", [128, 80], F32)
        self.sel_t = P("sel_t", [128, 4], F32)
        self.modT = P("modT", [128, 2, 144], F32)
        self.normg_t = P("normg_t", [128, 48], F32)
        self.Asc = P("Asc", [128, 3, 2, KC], F32)
        self.CG = P("CG", [128, 3, 2, KC], F32)
        self.fnorm_t = P("fnorm_t", [128, KC], F32)
        self.cmask_t = P("cmask_t", [128, 2], F32)
        self.epsc = P("epsc", [128, 1], F32)
        self.mrow_t = P("mrow_t", [128, 5 * KLOC], F32)

        s0 = self.slot("misc")
        d = [s0.dma(self.sp, self.scf[:, :], self.scin[:, :]),
             s0.dma(self.sp, self.fnorm_t[:, :], self.fnorm[:, :]),
             s0.dma(self.sp, self.cmask_t[:, :], self.cmask[:, :]),
             s0.dma(self.sp, self.mrow_t[:, :], self.mrow[:, :]),
             s0.dma(self.sp, self.sel_t[:, :], self.sel[:, :]),
             s0.dma(self.sp, self.identf[:, :], self.ident_in[:, :])]
        self.pool.mark(nc.gpsimd.memset(self.onesf[:, :], 1.0))
        self.pool.mark(nc.gpsimd.memset(self.epsc[:, :], EPS))
        self.act(self.scT[:, :], self.scf[:, :], AF.Silu, deps=[d[-1]])
        self.cp(self.identb[:, :], self.identf[:, :], deps=[d[-1]])
        self.barrier()

        xsrc = self.xin
        self.prologue_mods()
        self.make_jobs()
        self.pump(10 ** 9, upto_layer=0)
        for li in range(NL):
            self.cur_layer = li
            self.phase_mod(li)
            self.barrier()
            self.phase_ffn(li, 0, xsrc, self.xT)
            xsrc = self.xT
            if self.dbg == "ffn1":
                break
            self.phase_mix(li)
            if self.dbg == "mix":
                break
            self.phase_ffn(li, 2, self.xT, self.xT)
            self.pump(10 ** 9, upto_layer=li + 1)
        if self.final:
            self.phase_final()
            self.barrier()
        return nc

    def phase_mod(self, li):
        nc = self.nc
        with ExitStack() as st:
            ma = self.sb(st, "modall", [128, 8, MODW], F32)
            sm = self.slot("misc")
            sm.dma(self.pool, self.normg_t[:, :], self.normg[li])
            dl = sm.dma(self.pool, ma[:, :, :], self.Fmod[li].ap().rearrange("(c p) f -> p c f", p=128))
            mv = ma[:, :, 0:90].rearrange("p c (j f) -> p c j f", f=5)
            m0 = self.modT[:, 0, :].rearrange("p (c j) -> p c j", j=18)
            m1 = self.modT[:, 1, :].rearrange("p (c j) -> p c j", j=18)
            dd = self.ts(m0, mv[:, :, :, 0], self.sel_t[:, 0:1], None, ALU.mult, deps=[dl])
            for col in range(1, 4):
                dd = self.stt(m0, mv[:, :, :, col], self.sel_t[:, col:col + 1], m0, ALU.mult, ALU.add, deps=[dd])
            dd = self.cp(m1, mv[:, :, :, 4], deps=[dd])
            for j in range(3):
                for tt_ in range(2):
                    sc = self.modT[:, tt_, (3 * j + 1) * 16:(3 * j + 2) * 16]
                    gt = self.modT[:, tt_, (3 * j + 2) * 16:(3 * j + 3) * 16]
                    self.stt(self.Asc[:, j, tt_, :], sc, 1.0, self.normg_t[:, j * 16:(j + 1) * 16], ALU.add, ALU.mult, deps=[dd])
                    self.ts(self.CG[:, j, tt_, :], gt, 0.5 if j != 1 else 1.0, None, ALU.mult, deps=[dd])

    def Bsc(self, j, tt_, k):
        return self.modT[:, tt_, 3 * j * 16 + k:3 * j * 16 + k + 1]

    def phase_norm(self, j, xsrc, hT, final=False):
        nc = self.nc
        NS = 256
        with ExitStack() as st:
            xs = [self.sb(st, "nxs", [128, KC, NS], F32) for _ in range(2)]
            sq = [self.sb(st, "nsq", [128, KC, NS], F32) for _ in range(2)]
            tmp = [self.sb(st, "ntmp", [128, KC, NS], F32) for _ in range(2)]
            std = [self.sb(st, "nstd", [128, NS], F32) for _ in range(2)]
            lsl = [self.slot("nx%d" % i) for i in range(2)]
            ssl = [self.slot("nst%d" % i) for i in range(2)]
            xs_free = [None, None]
            sq_free = [None, None]
            tmp_free = [None, None]
            std_free = [None, None]
            nblk = (TL if final else T) // NS
            for i in range(nblk):
                t0 = i * NS
                tt_ = 0 if t0 < TL else 1
                b = i % 2
                self.aq.wait(xs_free[b])
                ld = lsl[b].dma(self.aq, xs[b][:, :, :], xsrc[:, :, t0:t0 + NS])
                d_sq = self.act(sq[b][:, :, :], xs[b][:, :, :], AF.Square, deps=[ld, sq_free[b]])
                bk = self.next_bank()
                self.pe.wait(self.bank_free[bk], d_sq)
                for k in range(KC):
                    ins = self.mm(self.ps[bk][:, :NS], self.onesf[:, :], sq[b][:, k, :], k == 0, k == KC - 1)
                d_pe = self.pe.mark(ins)
                sq_free[b] = d_pe
                d_std = self.act(std[b][:, :], self.ps[bk][:, :NS], AF.Sqrt, bias=self.epsc[:, 0:1], scale=1.0 / D,
                                 deps=[d_pe, std_free[b]])
                self.bank_free[bk] = d_std
                d_r = self.recip(std[b][:, :], std[b][:, :], deps=[d_std])
                d1 = d2 = None
                for k in range(KC):
                    if final:
                        d1 = self.stt(tmp[b][:, k, :], xs[b][:, k, :], self.fnorm_t[:, k:k + 1], std[b][:, :], ALU.mult, ALU.mult,
                                      deps=[d_r, tmp_free[b]])
                    else:
                        d1 = self.stt(tmp[b][:, k, :], xs[b][:, k, :], self.Asc[:, j, tt_, k:k + 1], std[b][:, :], ALU.mult, ALU.mult,
                                      deps=[d_r, tmp_free[b]])
                        d2 = self.act(hT[:, k, t0:t0 + NS], tmp[b][:, k, :], AF.Identity, bias=self.Bsc(j, tt_, k), deps=[d1])
                xs_free[b] = d1
                std_free[b] = d1
                if final:
                    self.aq.wait(d1)
                    tmp_free[b] = ssl[b].dma(self.aq, self.outd[:, :, t0:t0 + NS], tmp[b][:, :, :])
                else:
                    tmp_free[b] = d2

    def phase_final(self):
        self.phase_norm(0, self.xT, None, final=True)

    def linear_fm(self, wsrc, nslab, slab_cols, units, tbs, epi, nbuf=3, wdep=None):
        if isinstance(units[0], tuple):
            units = [units]
        with ExitStack() as st:
            wb = [self.sb(st, "wslab", [128, slab_cols], BF16) for _ in range(nbuf)]
            slots = [self.slot("w%d" % i) for i in range(nbuf)]
            pe_last = {}
            deps = {}

            def issue(s):
                b = s % nbuf
                self.wq.wait(pe_last.get(b), wdep)
                deps[s] = slots[b].dma(self.wq, wb[b][:, :], wsrc(s))

            for s in range(min(nbuf - 1, nslab)):
                issue(s)
            for s in range(nslab):
                if s + nbuf - 1 < nslab:
                    issue(s + nbuf - 1)
                b = s % nbuf
                self.pe.wait(deps[s])
                pe_dep = None
                for ti, (t0, nt, tt_) in enumerate(tbs):
                    for ui, groups in enumerate(units):
                        banks = []
                        for (src, kcg, off) in groups:
                            bk = self.next_bank()
                            self.pe.wait(self.bank_free[bk])
                            for k in range(kcg):
                                ins = self.mm(self.ps[bk][:, :nt], wb[b][:, off + k * 128:off + (k + 1) * 128],
                                              src[:, k, t0:t0 + nt], k == 0, k == kcg - 1)
                            banks.append(bk)
                        pe_dep = self.pe.mark(ins)
                        rel = epi(s, ti, t0, nt, tt_, ui, [self.ps[bk][:, :nt] for bk in banks], pe_dep)
                        for bk in banks:
                            self.bank_free[bk] = rel
                pe_last[b] = pe_dep

    def linear_tm(self, st, wsrcs, hT, epi, wdep=None):
        wv = [self.sb(st, "wtm", [128, 8192], BF16) for _ in range(2)]
        self.wq.wait(wdep)
        dl = [self.slot("wt%d" % i).dma(self.wq, wv[i][:, :], wsrcs[i]) for i in range(2)]
        self.pe.wait(*dl)
        for tile in range(10):
            banks = []
            for blk in range(2):
                bk = self.next_bank()
                self.pe.wait(self.bank_free[bk])
                for k in range(KC):
                    ins = self.mm(self.ps[bk][:, :], hT[:, k, tile * 128:(tile + 1) * 128], wv[blk][:, k * 512:(k + 1) * 512],
                                  k == 0, k == KC - 1)
                banks.append(bk)
            pe_dep = self.pe.mark(ins)
            rel = epi(tile, [self.ps[bk] for bk in banks], pe_dep)
            for bk in banks:
                self.bank_free[bk] = rel

    def make_residual_epi(self, st, xs_ap, xd_ap, j):
        NR = 4
        xin_t = [self.sb(st, "rxi", [128, 512], F32) for _ in range(NR)]
        xout_t = [self.sb(st, "rxo", [128, 512], F32) for _ in range(NR)]
        lsl = [self.slot("rl%d" % i) for i in range(NR)]
        ssl = [self.slot("rs%d" % i) for i in range(NR)]
        state = {"n": 0, "xin_free": [None] * NR, "st_done": [None] * NR}

        def epi(s, ti, t0, nt, tt_, ui, banks, pe_dep):
            r = state["n"] % NR
            state["n"] += 1
            self.aq.wait(state["xin_free"][r])
            ld = lsl[r].dma(self.aq, xin_t[r][:, :nt], xs_ap[:, s, t0:t0 + nt])
            dd = self.stt(xout_t[r][:, :nt], banks[0], self.CG[:, j, tt_, s:s + 1], xin_t[r][:, :nt], ALU.mult, ALU.add,
                          deps=[pe_dep, ld, state["st_done"][r]])
            state["xin_free"][r] = dd
            self.actE.wait(dd)
            state["st_done"][r] = ssl[r].dma(self.actE, xd_ap[:, s, t0:t0 + nt], xout_t[r][:, :nt])
            return dd
        return epi

    def phase_ffn(self, li, j, xsrc, xdst):
        w = 0 if j == 0 else 1
        with ExitStack() as st:
            hT = self.sb(st, "hT", [128, KC, T], BF16)
            self.phase_norm(j, xsrc, hT)
            self.barrier()
            actT = self.sb(st, "actT", [128, HFC, T], BF16)
            sg = [self.sb(st, "sg", [128, 512], F32) for _ in range(2)]
            for hf in range(2):
                state = {"n": 0, "free": [None, None]}

                def epi_b(s, ti, t0, nt, tt_, ui, banks, pe_dep):
                    b = state["n"] % 2
                    state["n"] += 1
                    d1 = self.act(sg[b][:, :nt], banks[0], AF.Silu, deps=[pe_dep, state["free"][b]])
                    d2 = self.tt(actT[:, s, t0:t0 + nt], sg[b][:, :nt], banks[1], ALU.mult, deps=[d1])
                    state["free"][b] = d2
                    return d2
                self.linear_fm(lambda s: self.slab('wgu%d' % (w + 1), li, hf * HFC + s), HFC, 4096, [(hT, KC, 0), (hT, KC, 2048)], TBS, epi_b, wdep=self.wdep('wgu%d' % (w + 1), li))
                self.barrier()
                with ExitStack() as st2:
                    epi_r = self.make_residual_epi(st2, xsrc if hf == 0 else xdst, xdst, j)
                    self.linear_fm(lambda s: self.slab('wd%d' % (w + 1), li, hf * 16 + s), 16, HFC * 128, [(actT, HFC, 0)], TBS, epi_r, wdep=self.wdep('wd%d' % (w + 1), li))
                    self.barrier()

    def phase_mix(self, li):
        with ExitStack() as stA:
            hT = self.sb(stA, "hT", [128, KC, T], BF16)
            self.phase_norm(1, self.xT, hT)
            self.barrier()
            self.slot("misc").dma(self.aq, self.hT_d[:, :, :], hT[:, :, :])
            self.stage_convin(li, hT)
            self.barrier()
            self.stage_qkv(li, hT)
            self.barrier()
            self.exchange()
            self.stage_gmlp(li, hT)
            self.barrier()
        self.stage_conv(li)
        self.barrier()
        self.stage_attn(li)
        self.barrier()
        self.stage_merge(li)
        self.barrier()
        self.stage_wout(li)
        self.barrier()

    def stage_convin(self, li, hT):
        with ExitStack() as st:
            sgm = [self.sb(st, "cisg", [128, 512], F32) for _ in range(2)]
            ao = [self.sb(st, "ciao", [128, 512], F32) for _ in range(2)]
            ssl = [self.slot("st%d" % i) for i in range(2)]
            state = {"n": 0, "sg_free": [None, None], "st_done": [None, None]}

            def epi(s, ti, t0, nt, tt_, ui, banks, pe_dep):
                b = state["n"] % 2
                state["n"] += 1
                d1 = self.act(sgm[b][:, :nt], banks[1], AF.Sigmoid, deps=[pe_dep, state["sg_free"][b]])
                d2 = self.tt(ao[b][:, :nt], sgm[b][:, :nt], banks[0], ALU.mult, deps=[d1, state["st_done"][b]])
                state["sg_free"][b] = d2
                self.aq.wait(d2)
                state["st_done"][b] = ssl[b].dma(self.aq, self.a_d[:, s, t0:t0 + nt], ao[b][:, :nt])
                return d2
            self.linear_fm(lambda s: self.slab('wconv', li, s), 8, 4096, [(hT, KC, 0), (hT, KC, 2048)], TBS, epi, wdep=self.wdep('wconv', li))

    def stage_qkv(self, li, hT):
        with ExitStack() as st:
            qs = [self.sb(st, "qs", [128, 512], BF16) for _ in range(2)]
            ssl = [self.slot("st%d" % i) for i in range(2)]
            state = {"n": 0, "st_done": [None, None]}

            def epi(s, ti, t0, nt, tt_, ui, banks, pe_dep):
                b = state["n"] % 2
                state["n"] += 1
                isk = s >= 8
                c = s % 8
                if isk:
                    d = self.cp(qs[b][:, :nt], banks[0], deps=[pe_dep, state["st_done"][b]], eng=self.actE)
                else:
                    d = self.ts(qs[b][:, :nt], banks[0], 0.125, None, ALU.mult, deps=[pe_dep, state["st_done"][b]])
                self.aq.wait(d)
                dst = self.k_d if isk else self.q_d
                state["st_done"][b] = ssl[b].dma(self.aq, dst[:, c, t0:t0 + nt], qs[b][:, :nt])
                return d
            self.linear_fm(lambda s: self.slab('wfm1', li, 8 + s), 16, 2048, [(hT, KC, 0)], TBS, epi, wdep=self.wdep('wfm1', li))
            self.barrier()
            vs = [self.sb(st, "vs", [128, 1024], BF16) for _ in range(2)]
            state2 = {"n": 0, "st_done": [None, None]}

            def epi_v(tile, banks, pe_dep):
                b = state2["n"] % 2
                state2["n"] += 1
                d1 = self.cp(vs[b][:, 0:512], banks[0][:, :], deps=[pe_dep, state2["st_done"][b]], eng=self.actE)
                d2 = self.cp(vs[b][:, 512:1024], banks[1][:, :], deps=[pe_dep, state2["st_done"][b]])
                self.aq.wait(d1, d2)
                state2["st_done"][b] = ssl[b].dma(self.aq, self.v_d[tile], vs[b][:, :])
                return [d1, d2]
            self.linear_tm(st, [self.slab('wintm', li, 2), self.slab('wintm', li, 3)], hT, epi_v, wdep=self.wdep('wintm', li))

    def exchange(self):
        nc = self.nc
        s = self.slot("exp")
        EA, EK, EV = self.EA.ap(), self.EK.ap(), self.EV.ap()
        for side, (a0, k0, vt) in enumerate([(0, 0, 0), (TL - 16, TL - 256, 6)]):
            s.dma(self.aq, EA[side * 128:(side + 1) * 128, :].rearrange("p (c j) -> p c j", j=16), self.a_d[:, :, a0:a0 + 16])
            s.dma(self.aq, EK[side * 128:(side + 1) * 128, :].rearrange("p (c j) -> p c j", j=256), self.k_d[:, :, k0:k0 + 256])
            s.dma(self.aq, EV[side * 256:(side + 1) * 256, :].rearrange("(t p) f -> t p f", p=128), self.v_d[vt:vt + 2])
        self.pool.wait(s.dep())
        groups = [[2 * i, 2 * i + 1] for i in range(self.ncores // 2)]
        for (E, I) in ((self.EA, self.IA), (self.EK, self.IK), (self.EV, self.IV)):
            nc.gpsimd.collective_compute("AllGather", ALU.bypass, replica_groups=groups,
                                         ins=[E.ap()[:, :]], outs=[I.ap()[:, :]]).then_inc(self.cc_sem, 1)
            self.ccn += 1
            self.pool.wait((self.cc_sem, self.ccn))

    def stage_gmlp(self, li, hT):
        nc = self.nc
        with ExitStack() as st:
            vln = self.sb(st, "vln", [128, 10, 1024], BF16)
            glng_t = self.sb(st, "glng", [128, 1024], F32)
            glnb_t = self.sb(st, "glnb", [128, 1024], F32)
            wsT_t = self.sb(st, "wsT", [128, 512], BF16)
            bsb_t = self.sb(st, "bsb", [128, 2048], F32)
            sm = self.slot("misc")
            dl = [sm.dma(self.aq, glng_t[:, :], self.glng[li]), sm.dma(self.aq, glnb_t[:, :], self.glnb[li]),
                  sm.dma(self.aq, bsb_t[:, :], self.bsb[li])]
            dws = self.slot("misc2").dma(self.pool, wsT_t[:, :], self.wsT[li])
            with ExitStack() as st1:
                vf = [self.sb(st1, "vf", [128, 1024], F32) for _ in range(2)]
                stats = [self.sb(st1, "vstat", [128, 12], F32) for _ in range(2)]
                mv = [self.sb(st1, "vmv", [128, 2], F32) for _ in range(2)]
                sd = [self.sb(st1, "vsd", [128, 1], F32) for _ in range(2)]
                state = {"n": 0, "free": [None, None]}

                def epi_v(tile, banks, pe_dep):
                    b = state["n"] % 2
                    state["n"] += 1
                    g1 = self.act(vf[b][:, 0:512], banks[0][:, :], AF.Gelu_apprx_tanh, deps=[pe_dep, state["free"][b]])
                    g2 = self.act(vf[b][:, 512:1024], banks[1][:, :], AF.Gelu_apprx_tanh, deps=[pe_dep])
                    self.dve.wait(g1, g2)
                    self.dve.mark(nc.vector.bn_stats(out=stats[b][:, 0:6], in_=vf[b][:, 0:512]))
                    d = self.dve.mark(nc.vector.bn_stats(out=stats[b][:, 6:12], in_=vf[b][:, 512:1024]))
                    self.dve.wait(d)
                    d = self.dve.mark(nc.vector.bn_aggr(out=mv[b][:, :], in_=stats[b][:, :]))
                    d = self.act(sd[b][:, :], mv[b][:, 1:2], AF.Sqrt, bias=self.epsc[:, 0:1], deps=[d])
                    d = self.recip(sd[b][:, :], sd[b][:, :], deps=[d])
                    d = self.ts(vf[b][:, :], vf[b][:, :], mv[b][:, 0:1], sd[b][:, 0:1], ALU.subtract, ALU.mult, deps=[d])
                    d = self.tt(vf[b][:, :], vf[b][:, :], glng_t[:, :], ALU.mult, deps=[d, dl[2]])
                    d = self.tt(vln[:, tile, :], vf[b][:, :], glnb_t[:, :], ALU.add, deps=[d, dl[2]])
                    state["free"][b] = d
                    return [g1, g2]
                self.linear_tm(st1, [self.slab('wintm', li, 0), self.slab('wintm', li, 1)], hT, epi_v, wdep=self.wdep('wintm', li))
                self.barrier()
            uf = [self.sb(st, "guf", [128, 512], F32) for _ in range(2)]
            tmp = [self.sb(st, "gtmp", [128, 512], F32) for _ in range(2)]
            go = [self.sb(st, "ggo", [128, 512], BF16) for _ in range(2)]
            ssl = [self.slot("st%d" % i) for i in range(2)]
            state2 = {"n": 0, "uf_free": [None, None], "st_done": [None, None]}

            def epi_u(s, ti, t0, nt, tt_, ui, banks, pe_dep):
                b = state2["n"] % 2
                state2["n"] += 1
                g = s // 2
                d1 = self.act(uf[b][:, :nt], banks[0], AF.Gelu_apprx_tanh, deps=[pe_dep, state2["uf_free"][b]])
                bk2 = self.next_bank()
                self.pe.wait(self.bank_free[bk2], dws)
                for n_ in range(nt // 128):
                    tile = t0 // 128 + n_
                    ins = self.mm(self.ps[bk2][:, n_ * 128:(n_ + 1) * 128], vln[:, tile, s * 128:(s + 1) * 128],
                                  wsT_t[:, g * 128:(g + 1) * 128], True, True)
                dpe2 = self.pe.mark(ins)
                d2 = self.tt(tmp[b][:, :nt], self.ps[bk2][:, :nt], bsb_t[:, g * 512:g * 512 + nt], ALU.add,
                             deps=[dpe2, dl[2], state2["uf_free"][b]])
                self.bank_free[bk2] = d2
                d3 = self.tt(go[b][:, :nt], tmp[b][:, :nt], uf[b][:, :nt], ALU.mult, deps=[d2, d1, state2["st_done"][b]])
                state2["uf_free"][b] = d3
                self.aq.wait(d3)
                state2["st_done"][b] = ssl[b].dma(self.aq, self.gm_d[:, s, t0:t0 + nt], go[b][:, :nt])
                return d1
            self.linear_fm(lambda s: self.slab('wfm1', li, s), 8, 2048, [(hT, KC, 0)], TBS, epi_u, wdep=self.wdep('wfm1', li))

    def stage_conv(self, li):
        nc = self.nc
        with ExitStack() as st:
            aT = self.sb(st, "caT", [128, 8, AW], BF16)
            y = self.sb(st, "cy", [128, 8, T], F32)
            cT = self.sb(st, "ccT", [128, 8, T], BF16)
            dw = self.sb(st, "cdw", [128, 248], F32)
            cpp = self.sb(st, "ccp", [128, 24], F32)
            sq = self.sb(st, "csq", [128, 8, 512], F32)
            mean = self.sb(st, "cmean", [128, 512], F32)
            m2 = self.sb(st, "cm2", [128, 512], F32)
            rstd = self.sb(st, "crstd", [128, 512], F32)
            zt = [self.sb(st, "czt", [128, 512], F32) for _ in range(2)]
            dg = [self.sb(st, "cdg", [128, 31, 128], BF16) for _ in range(2)]
            sl = self.slot("misc")
            IA = self.IA.ap()
            dl = [sl.dma(self.aq, dw[:, :], self.convdw[li]), sl.dma(self.aq, cpp[:, :], self.convp[li]),
                  sl.dma(self.aq, aT[:, :, 16:16 + TL], self.a_d[:, :, 0:TL]),
                  sl.dma(self.aq, aT[:, :, 1072:1328], self.a_d[:, :, TL:T]),
                  sl.dma(self.aq, aT[:, :, 0:16], IA[128:256, :].rearrange("p (c j) -> p c j", j=16)),
                  sl.dma(self.aq, aT[:, :, 1040:1056], IA[256:384, :].rearrange("p (c j) -> p c j", j=16))]
            dall = dl[-1]
            dz = self.pool.mark(nc.gpsimd.memset(aT[:, :, 1056:1072], 0.0))
            dz = self.pool.mark(nc.gpsimd.memset(aT[:, :, 1328:1344], 0.0))
            dm = self.ts(aT[:, :, 0:16], aT[:, :, 0:16], self.cmask_t[:, 0:1], None, ALU.mult, deps=[dall])
            dm = self.ts(aT[:, :, 1040:1056], aT[:, :, 1040:1056], self.cmask_t[:, 1:2], None, ALU.mult, deps=[dall])
            dg_free = [None, None]
            d = None
            for cc in range(8):
                gb = cc % 2
                dgd = None
                for k in range(31):
                    dgd = self.ts(dg[gb][:, k, :], self.identb[:, :], dw[:, cc * 31 + k:cc * 31 + k + 1], None, ALU.mult,
                                  deps=[dall, dg_free[gb]])
                for (o0, yo, nt) in ((0, 0, 512), (512, 512, 512), (1056, TL, 256)):
                    bk = self.next_bank()
                    self.pe.wait(self.bank_free[bk], dgd, dm, dz)
                    for k in range(31):
                        ins = self.mm(self.ps[bk][:, :nt], dg[gb][:, k, :], aT[:, cc, o0 + 1 + k:o0 + 1 + k + nt], k == 0, k == 30)
                    dpe = self.pe.mark(ins)
                    d = self.act(y[:, cc, yo:yo + nt], self.ps[bk][:, :nt], AF.Identity, bias=cpp[:, cc:cc + 1], deps=[dpe, dall])
                    self.bank_free[bk] = d
                dg_free[gb] = dpe
            dconv = d
            zfree = [None, None]
            zi = 0
            dlast = None
            for (t0, nt, tt_) in TBS:
                dsq = self.act(sq[:, :, :nt], y[:, :, t0:t0 + nt], AF.Square, deps=[dconv, dlast])
                b1 = self.next_bank()
                self.pe.wait(self.bank_free[b1], dconv)
                for cc in range(8):
                    ins = self.mm(self.ps[b1][:, :nt], self.onesf[:, :], y[:, cc, t0:t0 + nt], cc == 0, cc == 7)
                dp1 = self.pe.mark(ins)
                b2 = self.next_bank()
                self.pe.wait(self.bank_free[b2], dsq)
                for cc in range(8):
                    ins = self.mm(self.ps[b2][:, :nt], self.onesf[:, :], sq[:, cc, :nt], cc == 0, cc == 7)
                dp2 = self.pe.mark(ins)
                dmean = self.act(mean[:, :nt], self.ps[b1][:, :nt], AF.Identity, scale=1.0 / 1024, deps=[dp1, dlast])
                self.bank_free[b1] = dmean
                d = self.tt(m2[:, :nt], mean[:, :nt], mean[:, :nt], ALU.mult, deps=[dmean])
                d = self.stt(m2[:, :nt], self.ps[b2][:, :nt], 1.0 / 1024, m2[:, :nt], ALU.mult, ALU.subtract, deps=[d, dp2])
                self.bank_free[b2] = d
                d = self.act(rstd[:, :nt], m2[:, :nt], AF.Sqrt, bias=self.epsc[:, 0:1], deps=[d])
                drs = self.recip(rstd[:, :nt], rstd[:, :nt], deps=[d])
                for cc in range(8):
                    zb = zi % 2
                    zi += 1
                    d = self.tt(zt[zb][:, :nt], y[:, cc, t0:t0 + nt], mean[:, :nt], ALU.subtract, deps=[drs, zfree[zb]])
                    d = self.tt(zt[zb][:, :nt], zt[zb][:, :nt], rstd[:, :nt], ALU.mult, deps=[d])
                    d = self.act(cT[:, cc, t0:t0 + nt], zt[zb][:, :nt], AF.Silu, bias=cpp[:, 16 + cc:17 + cc], scale=cpp[:, 8 + cc:9 + cc],
                                 deps=[d])
                    zfree[zb] = d
                dlast = d
            self.aq.wait(dlast)
            self.slot("misc").dma(self.aq, self.conv_d[:, :, :], cT[:, :, :])

    def stage_attn(self, li):
        nc = self.nc
        with ExitStack() as st:
            Kb = [self.sb(st, "aK", [128, 1792], BF16) for _ in range(2)]
            Qb = [self.sb(st, "aQ", [128, T], BF16) for _ in range(2)]
            Vb = [self.sb(st, "aV", [128, 14, 128], BF16) for _ in range(2)]
            NBI = 10
            bias_t = [self.sb(st, "abias", [128, KLOC], F32) for _ in range(NBI)]
            NRG = 3
            S = [self.sb(st, "aS", [128, 1024], F32) for _ in range(NRG)]
            Pm = [self.sb(st, "aP", [128, 1024], BF16) for _ in range(NRG)]
            PT = [self.sb(st, "aPT", [128, 1024], BF16) for _ in range(NRG)]
            Osb = [self.sb(st, "aO", [128, 10, 128], BF16) for _ in range(2)]
            ao = [self.sb(st, "aao", [128, T], BF16) for _ in range(2)]
            nmx = [self.sb(st, "anmx", [128, 1], F32) for _ in range(NRG)]
            rsum = [self.sb(st, "arsum", [128, 1], F32) for _ in range(NRG)]
            rinv = [self.sb(st, "arinv", [128, 1], F32) for _ in range(NRG)]
            kslot = [self.slot("akqv%d" % i) for i in range(2)]
            bt_t = [self.sb(st, "abt", [128, KLOC], F32) for _ in range(4)]
            bslot = [self.slot("ab%d" % i) for i in range(4)]
            bt_free = [None] * 4
            oslot = [self.slot("st%d" % i) for i in range(2)]
            IK, IV = self.IK.ap(), self.IV.ap()
            kqv_free = [None, None]
            bias_free = [None] * NBI
            s_free = [None] * NRG
            ocnt = 0
            ao_done = [None, None]
            osb_free = [None, None]
            un = 0
            for c in range(8):
                b = c % 2
                self.aq.wait(kqv_free[b])
                sl = kslot[b]
                sl.dma(self.aq, Kb[b][:, 256:1280], self.k_d[:, c, 0:TL])
                sl.dma(self.aq, Kb[b][:, 1536:1792], self.k_d[:, c, TL:T])
                sl.dma(self.aq, Kb[b][:, 0:256], IK[128:256, c * 256:(c + 1) * 256])
                sl.dma(self.aq, Kb[b][:, 1280:1536], IK[256:384, c * 256:(c + 1) * 256])
                sl.dma(self.aq, Qb[b][:, :], self.q_d[:, c, :])
                sl.dma(self.aq, Vb[b][:, 2:10, :], self.v_d[0:8, :, c * 128:(c + 1) * 128].rearrange("t p f -> p t f"))
                sl.dma(self.aq, Vb[b][:, 12:14, :], self.v_d[8:10, :, c * 128:(c + 1) * 128].rearrange("t p f -> p t f"))
                sl.dma(self.aq, Vb[b][:, 0:2, :], IV[256:512, c * 128:(c + 1) * 128].rearrange("(t p) f -> p t f", p=128))
                dkqv = sl.dma(self.aq, Vb[b][:, 10:12, :], IV[512:768, c * 128:(c + 1) * 128].rearrange("(t p) f -> p t f", p=128))
                last_pe = None
                for hh in range(2):
                    h = 2 * c + hh
                    p0 = hh * 64
                    bdeps = []
                    btd = []
                    for v in range(2):
                        ti_ = (h % 2) * 2 + v
                        self.aq.wait(bt_free[ti_])
                        btd.append(bslot[ti_].dma(self.aq, bt_t[ti_][:, :], self.btd[li, (h * 2 + v) * 128:(h * 2 + v + 1) * 128, :]))
                    for slot_i in range(5):
                        bi = (h % 2) * 5 + slot_i
                        v = 1 if slot_i == 4 else 0
                        ti_ = (h % 2) * 2 + v
                        dd_ = self.tt(bias_t[bi][:, :], bt_t[ti_][:, :], self.mrow_t[:, slot_i * KLOC:(slot_i + 1) * KLOC], ALU.add,
                                      deps=[btd[v], bias_free[bi]])
                        bdeps.append(dd_)
                        bt_free[ti_] = dd_
                    for u in range(10):
                        r = un % NRG
                        un += 1
                        lat = u < 8
                        qc = u * 128
                        nk = 1024 if lat else 256
                        sA = self.next_bank()
                        sB = self.next_bank() if lat else None
                        osl = ocnt % 8
                        ocnt += 1
                        oP = self.po[:, osl * 64:(osl + 1) * 64]
                        self.pe.wait(dkqv, self.bank_free[sA])
                        qT = Qb[b][p0:p0 + 64, qc:qc + 128]
                        if lat:
                            koff = WS_ROWS[u] * 64
                            self.mm(self.ps[sA][:, :], qT, Kb[b][p0:p0 + 64, koff:koff + 512], True, True)
                            self.pe.wait(self.bank_free[sB])
                            self.mm(self.ps[sB][:, 0:256], qT, Kb[b][p0:p0 + 64, koff + 512:koff + 768], True, True)
                            ins = self.mm(self.ps[sB][:, 256:512], qT, Kb[b][p0:p0 + 64, 1536:1792], True, True)
                        else:
                            ins = self.mm(self.ps[sA][:, 0:256], qT, Kb[b][p0:p0 + 64, 1536:1792], True, True)
                        dS = self.pe.mark(ins)
                        if lat:
                            bi = (h % 2) * 5 + SLOT_OF_RP[u]
                            d1 = self.tt(S[r][:, 0:512], self.ps[sA][:, :], bias_t[bi][:, 0:512], ALU.add,
                                         deps=[dS, bdeps[SLOT_OF_RP[u]], s_free[r]])
                            d2 = self.tt(S[r][:, 512:768], self.ps[sB][:, 0:256], bias_t[bi][:, 512:768], ALU.add, deps=[dS])
                            bias_free[bi] = d2
                            d3 = self.cp(S[r][:, 768:1024], self.ps[sB][:, 256:512], deps=[dS, s_free[r]], eng=self.actE)
                            self.bank_free[sA] = d1
                            self.bank_free[sB] = [d2, d3]
                            dsc = [d1, d2, d3]
                        else:
                            d3 = self.cp(S[r][:, 0:256], self.ps[sA][:, 0:256], deps=[dS, s_free[r]], eng=self.actE)
                            self.bank_free[sA] = d3
                            dsc = [d3]
                        self.dve.wait(*dsc)
                        dmx = self.dve.mark(nc.vector.tensor_reduce(out=nmx[r][:, :], in_=S[r][:, :nk], axis=AX.X, op=ALU.max, negate=True))
                        dex = self.act(Pm[r][:, :nk], S[r][:, :nk], AF.Exp, bias=nmx[r][:, 0:1], deps=[dmx], accum_out=rsum[r][:, 0:1])
                        pb = 0
                        self.pe.wait(dex, self.pt_free[pb])
                        for jj in range(nk // 128):
                            ins = nc.tensor.transpose(self.pt[pb][:, jj * 128:(jj + 1) * 128], Pm[r][:, jj * 128:(jj + 1) * 128], self.identb[:, :])
                        dT = self.pe.mark(ins)
                        ceng = self.actE if (un % 2 == 0) else self.dve
                        dcp = self.cp(PT[r][:, :nk], self.pt[pb][:, :nk], deps=[dT], eng=ceng)
                        self.pt_free[pb] = dcp
                        self.pe.wait(dcp, self.po_free[osl])
                        nj = nk // 128
                        for jj in range(nj):
                            if lat:
                                vt = (WS_ROWS[u] // 2 + jj) if jj < 6 else (12 + jj - 6)
                            else:
                                vt = 12 + jj
                            ins = self.mm(oP, PT[r][:, jj * 128:(jj + 1) * 128], Vb[b][:, vt, p0:p0 + 64], jj == 0, jj == nj - 1)
                        dO = self.pe.mark(ins)
                        last_pe = dO
                        dri = self.recip(rinv[r][:, :], rsum[r][:, :], deps=[dex])
                        dos = self.ts(Osb[b][:, u, p0:p0 + 64], oP, rinv[r][:, 0:1], None, ALU.mult,
                                      deps=[dO, dri, osb_free[b]])
                        self.po_free[osl] = dos
                        s_free[r] = [dcp, dos, dO]
                kqv_free[b] = last_pe
                for (u0, nu) in ((0, 8), (8, 2)):
                    pb = 0
                    self.pe.wait(dos, self.pt_free[pb])
                    for uu in range(nu):
                        ins = nc.tensor.transpose(self.pt[pb][:, uu * 128:(uu + 1) * 128], Osb[b][:, u0 + uu, :], self.identb[:, :])
                    dT = self.pe.mark(ins)
                    dcp = self.cp(ao[b][:, u0 * 128:(u0 + nu) * 128], self.pt[pb][:, :nu * 128], deps=[dT, ao_done[b]])
                    self.pt_free[pb] = dcp
                osb_free[b] = dT
                self.aq.wait(dcp)
                ao_done[b] = oslot[b].dma(self.aq, self.attn_d[:, c, :], ao[b][:, :])

    def stage_merge(self, li):
        with ExitStack() as st:
            hT = self.sb(st, "mhT", [128, KC, T], BF16)
            bT = [self.sb(st, "mbT", [128, 8, T], BF16) for _ in range(3)]
            sl = self.slot("misc")
            dl = [sl.dma(self.aq, hT[:, :, :], self.hT_d[:, :, :])]
            for i, src in enumerate((self.conv_d, self.gm_d, self.attn_d)):
                dl.append(sl.dma(self.aq, bT[i][:, :, :], src[:, :, :]))
            self.pe.wait(dl[-1])
            sgt = [self.sb(st, "msg", [128, 512], F32) for _ in range(2)]
            tmp = [self.sb(st, "mtmp", [128, 512], F32) for _ in range(2)]
            acc = [self.sb(st, "macc", [128, 512], F32) for _ in range(2)]
            ys = [self.sb(st, "mys", [128, 512], BF16) for _ in range(2)]
            ssl = [self.slot("st%d" % i) for i in range(2)]
            state = {"n": 0, "sg_free": [None, None], "acc_dep": [None, None], "acc_free": [None, None], "st_done": [None, None],
                     "m": 0}

            def epi(s, ti, t0, nt, tt_, ui, banks, pe_dep):
                b = state["n"] % 2
                state["n"] += 1
                a = state["m"] % 2
                d1 = self.act(sgt[b][:, :nt], banks[0], AF.Sigmoid, deps=[pe_dep, state["sg_free"][b]])
                if ui == 0:
                    d2 = self.tt(acc[a][:, :nt], sgt[b][:, :nt], banks[1], ALU.mult, deps=[d1, state["acc_free"][a]])
                    state["acc_dep"][a] = d2
                    state["sg_free"][b] = d2
                    return d2
                d2 = self.tt(tmp[b][:, :nt], sgt[b][:, :nt], banks[1], ALU.mult, deps=[d1])
                if ui == 1:
                    d3 = self.tt(acc[a][:, :nt], acc[a][:, :nt], tmp[b][:, :nt], ALU.add, deps=[d2, state["acc_dep"][a]])
                    state["acc_dep"][a] = d3
                    state["sg_free"][b] = d3
                    return d2
                d3 = self.tt(ys[a][:, :nt], acc[a][:, :nt], tmp[b][:, :nt], ALU.add, deps=[d2, state["acc_dep"][a], state["st_done"][a]])
                state["sg_free"][b] = d3
                state["acc_free"][a] = d3
                state["m"] += 1
                self.aq.wait(d3)
                state["st_done"][a] = ssl[a].dma(self.aq, self.y_d[:, s, t0:t0 + nt], ys[a][:, :nt])
                return d2
            units = [[(hT, KC, br * 2048), (bT[br], 8, 6144 + br * 1024)] for br in range(3)]
            self.linear_fm(lambda s: self.slab('wmix', li, s), 16, 9216, units, TBS, epi, nbuf=2, wdep=self.wdep('wmix', li))

    def stage_wout(self, li):
        with ExitStack() as st:
            yT = self.sb(st, "yT", [128, KC, T], BF16)
            d = self.slot("misc").dma(self.aq, yT[:, :, :], self.y_d[:, :, :])
            self.pe.wait(d)
            epi_r = self.make_residual_epi(st, self.xT, self.xT, 1)
            self.linear_fm(lambda s: self.slab('wout', li, s), 16, 2048, [(yT, KC, 0)], TBS, epi_r, wdep=self.wdep('wout', li))


def fm_slabs(w, kc):
    K_, N_ = w.shape
    return np.ascontiguousarray(w.reshape(kc, 128, N_ // 128, 128).transpose(2, 1, 0, 3)).reshape(N_ // 128, 128, kc * 128)


def fvec(v):
    return np.ascontiguousarray(v.reshape(-1, 128).T)


SLOT_OF_RP = [0, 1, 2, 2, 2, 2, 3, 4]
REP_RP = [0, 1, 2, 6, 7]


def build_bt(rpb):
    ri = np.arange(2)[:, None, None, None]
    c = np.arange(64)[None, :, None, None]
    j = np.arange(12)[None, None, :, None]
    kc = np.arange(64)[None, None, None, :]
    cs = np.clip(c - 8, 0, 48)
    vcol = np.broadcast_to((kc >= cs) & (kc < cs + 16), (2, 64, 12, 64))
    dc = np.broadcast_to(np.clip(kc - c, -15, 15) + 15, (2, 64, 12, 64))
    out = np.empty((16, 2, 128, KLOC), np.float32)
    for v, off in enumerate((3, 1)):
        dr = np.broadcast_to(np.clip(j - ri + off, 0, 14), (2, 64, 12, 64))
        vals = rpb[:, dr, dc]
        vals = np.where(vcol[None], vals, np.float32(-1e9))
        out[:, v] = vals.reshape(16, 128, KLOC)
    return out


def build_mrow(half):
    R0 = 16 * half
    out = np.empty((128, 5, KLOC), np.float32)
    ri = np.arange(2)[:, None, None, None]
    j = np.arange(12)[None, None, :, None]
    for slot, rp in enumerate(REP_RP):
        r = R0 + 2 * rp + ri
        g = WS_ROWS[rp] + j + R0 - 4
        rs = np.clip(r - 4, 0, 24)
        vrow = (g >= 0) & (g < 32) & (g >= rs) & (g < rs + 8)
        m = np.where(np.broadcast_to(vrow, (2, 64, 12, 64)), np.float32(0.0), np.float32(-1e9))
        out[:, slot] = m.reshape(128, KLOC)
    return out.reshape(128, 5 * KLOC)


def prep_shared(inp, layers):
    sh = {}
    L = layers
    st = lambda f: np.stack([f(i) for i in L]).astype(np.float32, copy=False)
    sh["wmod"] = st(lambda i: fm_slabs(inp["w_mod"][i], KC))
    sh["bmod"] = st(lambda i: fvec(inp["b_mod"][i]))
    sh["normg"] = st(lambda i: np.concatenate([fvec(inp["norm_ffn1"][i]), fvec(inp["norm_mix"][i]), fvec(inp["norm_ffn2"][i])], axis=1))
    for nm, src in (("wgu1", "ffn1_w_gu"), ("wgu2", "ffn2_w_gu")):
        sh[nm] = st(lambda i: np.concatenate([fm_slabs(inp[src][i][:, :DFF], KC), fm_slabs(inp[src][i][:, DFF:], KC)], axis=2))
    for nm, src in (("wd1", "ffn1_w_down"), ("wd2", "ffn2_w_down")):
        sh[nm] = st(lambda i: np.concatenate([fm_slabs(inp[src][i][hf * 2816:(hf + 1) * 2816], HFC) for hf in range(2)], axis=0))
    win = inp["w_in"]
    sh["wconv"] = st(lambda i: np.concatenate([fm_slabs(win[i][:, 0:1024], KC), fm_slabs(win[i][:, 1024:2048], KC)], axis=2))
    sh["wfm1"] = st(lambda i: np.concatenate([fm_slabs(win[i][:, 2048:3072], KC), fm_slabs(win[i][:, 4096:5120], KC),
                                              fm_slabs(win[i][:, 5120:6144], KC)], axis=0))

    def tm(w):
        return np.ascontiguousarray(w.reshape(KC, 128, 512).transpose(1, 0, 2)).reshape(128, KC * 512)
    sh["wintm"] = st(lambda i: np.stack([tm(win[i][:, 3072:3584]), tm(win[i][:, 3584:4096]),
                                         tm(win[i][:, 6144:6656]), tm(win[i][:, 6656:7168])]))

    def mix(i):
        g = [fm_slabs(win[i][:, 7168 + b * D:7168 + (b + 1) * D], KC) for b in range(3)]
        o = [fm_slabs(inp[n][i], 8) for n in ("w_conv_out", "w_gmlp_out", "w_attn_out")]
        return np.concatenate(g + o, axis=2)
    sh["wmix"] = st(mix)
    sh["wout"] = st(lambda i: fm_slabs(inp["w_out"][i], KC))
    sh["convdw"] = st(lambda i: np.ascontiguousarray(inp["conv_dw"][i].reshape(31, 8, 128).transpose(2, 1, 0)).reshape(128, 248))
    sh["convp"] = st(lambda i: np.concatenate([fvec(inp["conv_db"][i]), fvec(inp["conv_ln_g"][i]), fvec(inp["conv_ln_b"][i])], axis=1))
    sh["glng"] = st(lambda i: np.broadcast_to(inp["gmlp_ln_g"][i][None, :], (128, 1024)))
    sh["glnb"] = st(lambda i: np.broadcast_to(inp["gmlp_ln_b"][i][None, :], (128, 1024)))
    sh["wsT"] = st(lambda i: np.ascontiguousarray(inp["gmlp_ws"][i].transpose(2, 0, 1)).reshape(128, 512))
    sh["bsb"] = st(lambda i: np.broadcast_to(np.broadcast_to(inp["gmlp_bs"][i][:, None, :], (4, 4, 128)).reshape(1, 2048), (128, 2048)))
    sh["bt"] = st(lambda i: build_bt(inp["attn_rpb"][i])).reshape(len(L), 32 * 128, 768)
    sh["fnorm"] = fvec(inp["final_norm"]).astype(np.float32)
    sh["ident"] = np.eye(128, dtype=np.float32)
    return {k: np.ascontiguousarray(v) for k, v in sh.items()}


def prep_core(inp, layers, core, x_T=None):
    b, half = core // 2, core % 2
    pc = {}
    if x_T is None:
        xl = inp["x"][b, half * TL:(half + 1) * TL, :]
        xc = inp["ctx"][b]
        xa = np.concatenate([xl, xc], axis=0)
        x_T = np.ascontiguousarray(xa.T.reshape(KC, 128, T).transpose(1, 0, 2))
    pc["xin"] = x_T
    sc = np.concatenate([inp["c"].T, inp["c_ctx"][:, None]], axis=1)
    pc["scin"] = np.ascontiguousarray(sc.reshape(KC, 128, 5).transpose(1, 0, 2)).reshape(128, 80)
    sel = np.zeros((128, 4), np.float32)
    sel[:, b] = 1.0
    pc["sel"] = sel
    cm = np.zeros((128, 2), np.float32)
    cm[:, 0] = 1.0 if half == 1 else 0.0
    cm[:, 1] = 1.0 if half == 0 else 0.0
    pc["cmask"] = cm
    pc["mrow"] = build_mrow(half)
    return pc


def shard_flat(sh, nl, core):
    flat = np.concatenate([sh[k].reshape(nl, -1) for k in BIGW], axis=1)
    pad = NBW * BLK - flat.shape[1]
    if pad:
        flat = np.concatenate([flat, np.zeros((nl, pad), np.float32)], axis=1)
    s_, q_ = core // 4, core % 4
    h = flat.reshape(nl, NBW, 2, 512 * 2048)[:, :, s_]
    e = h.reshape(nl, NBW // 2, 4, 256 * 2048)[:, :, q_]
    return np.ascontiguousarray(e).reshape(nl, NBW // 2 * 256, 2048)


def core_maps(inp, layers, cores, xs=None):
    sh = prep_shared(inp, layers)
    nl = len(layers)
    maps = []
    for ci, c in enumerate(cores):
        m = {k: v for k, v in sh.items() if k not in BIGW and k not in ("wmod", "bmod")}
        m["wsh"] = shard_flat(sh, nl, c)
        m["wmodsh"] = np.ascontiguousarray(sh["wmod"][:, 18 * c:18 * (c + 1)])
        m["bmodsh"] = np.ascontiguousarray(sh["bmod"][:, :, 18 * c:18 * (c + 1)])
        m.update(prep_core(inp, layers, c, x_T=None if xs is None else xs[ci]))
        maps.append(m)
    return maps


FUSED = True
_PROGS = {}


def _prog(nl, final):
    key = (nl, final)
    if key not in _PROGS:
        _PROGS[key] = K(list(range(nl)), final, ncores=N_CORES).build()
    return _PROGS[key]


def _assemble(outs):
    B = N_CORES // 2
    y = np.empty((B, 2 * TL, D), np.float32)
    for core in range(N_CORES):
        b, half = core // 2, core % 2
        o = np.asarray(outs[core], dtype=np.float32)
        y[b, half * TL:(half + 1) * TL, :] = o.transpose(2, 1, 0).reshape(TL, D)
    return y


def kernel(**inputs):
    inp = {k: np.asarray(v) for k, v in inputs.items()}
    cores = list(range(N_CORES))
    if FUSED:
        maps = core_maps(inp, list(range(DEPTH)), cores)
        res = run_bass_kernel_spmd(_prog(DEPTH, True), maps, core_ids=cores)
        return _assemble([r["out"] for r in res.results])
    xs = None
    res = None
    for l in range(DEPTH):
        maps = core_maps(inp, [l], cores, xs=xs)
        res = run_bass_kernel_spmd(_prog(1, l == DEPTH - 1), maps, core_ids=cores)
        xs = [np.asarray(r["xT"], dtype=np.float32) for r in res.results]
    return _assemble([r["out"] for r in res.results])
```

```python
import numpy as np
from contextlib import ExitStack
import concourse.bass as bass
import concourse.mybir as mybir
from concourse.bass_utils import run_bass_kernel_spmd

F32 = mybir.dt.float32
BF16 = mybir.dt.bfloat16
BIGW = {"wgu1": (44 * 128, 4096), "wd1": (32 * 128, 2816), "wconv": (8 * 128, 4096), "wfm1": (24 * 128, 2048),
        "wintm": (4 * 128, 8192), "wmix": (16 * 128, 9216), "wout": (16 * 128, 2048), "wgu2": (44 * 128, 4096),
        "wd2": (32 * 128, 2816)}
WOFF = {}
_o = 0
for _k, (_r, _c) in BIGW.items():
    WOFF[_k] = _o
    _o += _r * _c
BLK = 2 * (1 << 20)
NBW = ((_o + BLK - 1) // BLK + 1) // 2 * 2
Q_GROUPS = [[0, 1, 2, 3], [4, 5, 6, 7]]
P4_GROUPS = [[0, 4], [1, 5], [2, 6], [3, 7]]
MODW = 96
PUMP = 7
AF = mybir.ActivationFunctionType
ALU = mybir.AluOpType
AX = mybir.AxisListType

D = 2048
KC = 16
DFF = 5632
HFC = 22
T = 1280
TL = 1024
EPS = 1e-6
TBS = [(0, 512, 0), (512, 512, 0), (1024, 256, 1)]
WS_ROWS = [0, 2, 4, 6, 8, 10, 12, 12]
KLOC = 768
AW = 1344
N_CORES = 8
DEPTH = 4


def _key(sem):
    return id(sem)


class Eng:
    def __init__(self, k, raw, name):
        self.k, self.raw, self.name = k, raw, name
        self.sem = k.es.enter_context(k.nc.semaphore("tick_" + name))
        self.n = 0
        self.seen = {}

    def wait(self, *deps):
        for d in deps:
            if d is None:
                continue
            if isinstance(d, list):
                self.wait(*d)
                continue
            sem, v = d
            kk = _key(sem)
            if self.seen.get(kk, 0) < v:
                self.raw.wait_ge(sem, v)
                self.seen[kk] = v

    def mark(self, ins):
        self.n += 1
        ins.then_inc(self.sem, 1)
        return (self.sem, self.n)


class Slot:
    def __init__(self, k, name):
        self.sem = k.es.enter_context(k.nc.semaphore("dma_" + name))
        self.n = 0

    def dma(self, eng, out, in_):
        ins = eng.raw.dma_start(out=out, in_=in_)
        self.n += 16
        ins.then_inc(self.sem, 16)
        return (self.sem, self.n)

    def dep(self):
        return (self.sem, self.n)


class K:
    def __init__(self, layers, final, dbg=None, ncores=8):
        self.dbg = dbg
        self.ncores = ncores
        self.layers = layers
        self.NL = len(layers)
        self.final = final
        self.nc = bass.Bass("TRN2", target_bir_lowering=False)
        self.es = ExitStack()
        self.uid = 0
        self.slots = {}
        self.bar_n = 0
        self.ccn = 0

    def uniq(self, s):
        self.uid += 1
        return "%s_%d" % (s, self.uid)

    def slot(self, name):
        if name not in self.slots:
            self.slots[name] = Slot(self, name)
        return self.slots[name]

    def sb(self, st, name, shape, dt):
        return st.enter_context(self.nc.sbuf_tensor(self.uniq(name), shape, dt))

    def act(self, out, in_, func, bias=None, scale=None, deps=(), accum_out=None):
        self.actE.wait(*deps)
        kw = {}
        if bias is not None:
            kw["bias"] = bias
        if scale is not None:
            kw["scale"] = scale
        if accum_out is not None:
            kw["accum_out"] = accum_out
        return self.actE.mark(self.nc.scalar.activation(out=out, in_=in_, func=func, **kw))

    def tt(self, out, in0, in1, op, deps=(), eng=None):
        e = eng or self.dve
        e.wait(*deps)
        return e.mark(e.raw.tensor_tensor(out=out, in0=in0, in1=in1, op=op))

    def stt(self, out, in0, scalar, in1, op0, op1, deps=()):
        self.dve.wait(*deps)
        return self.dve.mark(self.nc.vector.scalar_tensor_tensor(out=out, in0=in0, scalar=scalar, in1=in1, op0=op0, op1=op1))

    def ts(self, out, in0, s1, s2, op0, op1=None, deps=(), eng=None):
        e = eng or self.dve
        e.wait(*deps)
        if op1 is None:
            return e.mark(e.raw.tensor_scalar(out=out, in0=in0, scalar1=s1, scalar2=None, op0=op0))
        return e.mark(e.raw.tensor_scalar(out=out, in0=in0, scalar1=s1, scalar2=s2, op0=op0, op1=op1))

    def cp(self, out, in_, deps=(), eng=None):
        e = eng or self.dve
        e.wait(*deps)
        if e is self.actE:
            return e.mark(self.nc.scalar.copy(out=out, in_=in_))
        return e.mark(e.raw.tensor_copy(out=out, in_=in_))

    def recip(self, out, in_, deps=()):
        self.dve.wait(*deps)
        return self.dve.mark(self.nc.vector.reciprocal(out=out, in_=in_))

    def mm(self, out, lhsT, rhs, start, stop):
        return self.nc.tensor.matmul(out, lhsT=lhsT, rhs=rhs, start=start, stop=stop)

    def slab(self, name, li, s):
        cols = BIGW[name][1]
        flat = self.Wf[li].ap().rearrange("r c -> (r c)")
        o = WOFF[name] + s * 128 * cols
        return flat[o:o + 128 * cols].rearrange("(p c) -> p c", c=cols)

    def wdep(self, name, li):
        rows, cols = BIGW[name]
        m_last = min((WOFF[name] + rows * cols - 1) // BLK + 2, NBW - 1)
        return (self.wg_sem, self.s2_seq[(li, m_last)] + 1)

    def make_jobs(self):
        self.s2_seq = {}
        seq = 0
        for li in range(self.NL):
            for i in range(NBW // 2):
                self.jobs.append(("cp", li, i))
            for i in range(NBW // 2):
                self.jobs.append(("s1", li, i))
                seq += 1
            for m in range(NBW):
                self.jobs.append(("s2", li, m))
                self.s2_seq[(li, m)] = seq
                seq += 1
        self.s1_end = {}
        c = 0
        for li in range(self.NL):
            c += NBW // 2
            self.s1_end[li] = c
            c += NBW

    def pump(self, quota, upto_layer=None):
        nc = self.nc
        n = 0
        while self.job_i < len(self.jobs) and n < quota:
            kind, li, i = self.jobs[self.job_i]
            if upto_layer is not None and li > upto_layer:
                break
            if kind == "cp":
                self.slot("wgcp").dma(self.pool, self.Wex[li].ap()[i * 256:(i + 1) * 256, :], self.wsh[li, i * 256:(i + 1) * 256, :])
            elif kind == "s1":
                if i == 0:
                    self.pool.wait(self.slot("wgcp").dep())
                nc.gpsimd.collective_compute("AllGather", ALU.bypass, replica_groups=Q_GROUPS,
                                             ins=[self.Wex[li].ap()[i * 256:(i + 1) * 256, :]],
                                             outs=[self.Wh[li].ap()[i * 1024:(i + 1) * 1024, :]]).then_inc(self.wg_sem, 1)
                n += 1
            else:
                if i == 0:
                    self.pool.wait((self.wg_sem, self.s1_end[li]))
                nc.gpsimd.collective_compute("AllGather", ALU.bypass, replica_groups=P4_GROUPS,
                                             ins=[self.Wh[li].ap()[i * 512:(i + 1) * 512, :]],
                                             outs=[self.Wf[li].ap()[i * 1024:(i + 1) * 1024, :]]).then_inc(self.wg_sem, 1)
                n += 1
            self.job_i += 1

    def prologue_mods(self):
        nc = self.nc
        with ExitStack() as st:
            wb = [self.sb(st, "wmodb", [128, 6 * 2048], BF16) for _ in range(3)]
            slots = [self.slot("w%d" % i) for i in range(3)]
            bm = self.sb(st, "bmodl", [128, self.NL, 18], F32)
            dbm = self.slot("misc").dma(self.sp, bm[:, :, :], self.bmodsh.rearrange("l p c -> p l c"))
            ml = [self.sb(st, "modl", [128, MODW], F32) for _ in range(2)]
            pe_last = {}
            n = 0
            ml_free = [None, None]
            for li in range(self.NL):
                bk = self.next_bank()
                self.pe.wait(self.bank_free[bk])
                ps = self.ps[bk]
                for piece in range(3):
                    b = n % 3
                    n += 1
                    self.pool.wait(pe_last.get(b))
                    d = slots[b].dma(self.pool, wb[b][:, :].rearrange("p (s c) -> p s c", c=2048),
                                     self.wmodsh[li, piece * 6:(piece + 1) * 6].rearrange("s p c -> p s c"))
                    self.pe.wait(d)
                    for o in range(6):
                        j = piece * 6 + o
                        for k in range(KC):
                            ins = self.mm(ps[:, 5 * j:5 * j + 5], wb[b][:, o * 2048 + k * 128:o * 2048 + (k + 1) * 128],
                                          self.scT[:, 5 * k:5 * k + 5], k == 0, k == KC - 1)
                    pe_last[b] = self.pe.mark(ins)
                a = li % 2
                dz = self.pool.mark(nc.gpsimd.memset(ml[a][:, 90:MODW], 0.0)) if li < 2 else None
                self.dve.wait(pe_last[(n - 1) % 3], dbm, ml_free[a], dz)
                for col in range(5):
                    dd = self.dve.mark(nc.vector.tensor_tensor(out=ml[a][:, col:90:5], in0=ps[:, col:90:5], in1=bm[:, li, :], op=ALU.add))
                self.bank_free[bk] = dd
                self.pool.wait(dd)
                dst = self.slot("misc2").dma(self.pool, self.Emod[li].ap()[:, :], ml[a][:, :])
                ml_free[a] = dst
                self.pool.wait(dst)
                nc.gpsimd.collective_compute("AllGather", ALU.bypass, replica_groups=Q_GROUPS,
                                             ins=[self.Emod[li].ap()[:, :]], outs=[self.Hmod[li].ap()[:, :]]).then_inc(self.cc_sem, 1)
                self.ccn += 1
                self.pool.wait((self.cc_sem, self.ccn))
                nc.gpsimd.collective_compute("AllGather", ALU.bypass, replica_groups=P4_GROUPS,
                                             ins=[self.Hmod[li].ap()[:, :]], outs=[self.Fmod[li].ap()[:, :]]).then_inc(self.cc_sem, 1)
                self.ccn += 1
            self.pool.wait((self.cc_sem, self.ccn))
        self.barrier()

    def next_bank(self):
        b = self.bank_i % len(self.ps)
        self.bank_i += 1
        return b

    def barrier(self):
        self.bar_n += 1
        if self.jobs and getattr(self, "cur_layer", None) is not None:
            self.pump(PUMP, upto_layer=self.cur_layer + 1)
        for s in self.slots.values():
            if s.n:
                self.sp.wait(s.dep())
        if self.ccn:
            self.sp.wait((self.cc_sem, self.ccn))
        engs = [self.pe, self.actE, self.dve, self.pool, self.sp]
        for e in engs:
            e.raw.drain().then_inc(self.bar_sem, 1)
        for e in engs:
            e.raw.wait_ge(self.bar_sem, len(engs) * self.bar_n)

    def build(self):
        nc, es = self.nc, self.es
        NL = self.NL
        self.pe = Eng(self, nc.tensor, "pe")
        self.actE = Eng(self, nc.scalar, "act")
        self.dve = Eng(self, nc.vector, "dve")
        self.pool = Eng(self, nc.gpsimd, "pool")
        self.sp = Eng(self, nc.sync, "sp")
        self.wq = self.sp
        self.aq = self.pool
        self.cur_layer = None
        self.bar_sem = es.enter_context(nc.semaphore("bar"))
        self.cc_sem = es.enter_context(nc.semaphore("cc"))

        def din(name, shape):
            return nc.dram_tensor(name, shape, F32, kind="ExternalInput").ap()

        self.xin = din("xin", [128, KC, T])
        self.scin = din("scin", [128, 80])
        self.cmask = din("cmask", [128, 2])
        self.mrow = din("mrow", [128, 5 * KLOC])
        self.normg = din("normg", [NL, 128, 48])
        self.wsh = din("wsh", [NL, NBW // 2 * 256, 2048])
        self.wmodsh = din("wmodsh", [NL, 18, 128, 2048])
        self.bmodsh = din("bmodsh", [NL, 128, 18])
        self.sel = din("sel", [128, 4])
        self.btd = din("bt", [NL, 32 * 128, 768])
        self.Wex = [nc.dram_tensor("w_e%d" % l, [NBW // 2 * 256, 2048], BF16) for l in range(NL)]
        self.Wh = [nc.dram_tensor("w_h%d" % l, [NBW * 512, 2048], BF16) for l in range(NL)]
        self.Wf = [nc.dram_tensor("w_f%d" % l, [NBW * 1024, 2048], BF16) for l in range(NL)]
        self.Emod = [nc.dram_tensor("mod_e%d" % l, [128, MODW], F32) for l in range(NL)]
        self.Hmod = [nc.dram_tensor("mod_h%d" % l, [512, MODW], F32) for l in range(NL)]
        self.Fmod = [nc.dram_tensor("mod_f%d" % l, [1024, MODW], F32) for l in range(NL)]
        self.wg_sem = es.enter_context(nc.semaphore("wg"))
        self.wgn = 0
        self.jobs = []
        self.job_i = 0
        self.convdw = din("convdw", [NL, 128, 248])
        self.convp = din("convp", [NL, 128, 24])
        self.glng = din("glng", [NL, 128, 1024])
        self.glnb = din("glnb", [NL, 128, 1024])
        self.wsT = din("wsT", [NL, 128, 512])
        self.bsb = din("bsb", [NL, 128, 2048])
        self.fnorm = din("fnorm", [128, KC])
        self.ident_in = din("ident", [128, 128])

        self.xT = nc.dram_tensor("xT", [128, KC, T], F32, kind="ExternalOutput").ap()
        if self.final:
            self.outd = nc.dram_tensor("out", [128, KC, TL], F32, kind="ExternalOutput").ap()
        self.a_d = nc.dram_tensor("a_d", [128, 8, T], F32).ap()
        self.q_d = nc.dram_tensor("q_d", [128, 8, T], BF16).ap()
        self.k_d = nc.dram_tensor("k_d", [128, 8, T], BF16).ap()
        self.v_d = nc.dram_tensor("v_d", [10, 128, 1024], BF16).ap()
        self.hT_d = nc.dram_tensor("hT_d", [128, KC, T], BF16).ap()
        dk = {"kind": "ExternalOutput"} if self.dbg else {}
        self.conv_d = nc.dram_tensor("conv_d", [128, 8, T], BF16, **dk).ap()
        self.gm_d = nc.dram_tensor("gm_d", [128, 8, T], BF16, **dk).ap()
        self.attn_d = nc.dram_tensor("attn_d", [128, 8, T], BF16, **dk).ap()
        self.y_d = nc.dram_tensor("y_d", [128, KC, T], BF16, **dk).ap()
        self.EA = nc.dram_tensor("EA", [256, 128], F32)
        self.IA = nc.dram_tensor("IA", [512, 128], F32)
        self.EK = nc.dram_tensor("EK", [256, 2048], BF16)
        self.IK = nc.dram_tensor("IK", [512, 2048], BF16)
        self.EV = nc.dram_tensor("EV", [512, 1024], BF16)
        self.IV = nc.dram_tensor("IV", [1024, 1024], BF16)

        P = lambda name, shape, dt: es.enter_context(nc.sbuf_tensor(name, shape, dt))
        self.ps = [es.enter_context(nc.psum_tensor("ps%d" % i, [128, 512], F32)) for i in range(6)]
        self.pt = [es.enter_context(nc.psum_tensor("pt0", [128, 1024], BF16))]
        self.po = es.enter_context(nc.psum_tensor("po", [128, 512], F32))
        self.po_free = [None] * 8
        self.bank_i = 0
        self.bank_free = [None] * 6
        self.pt_free = [None]
        self.onesf = P("onesf", [128, 128], F32)
        self.identf = P("identf", [128, 128], F32)
        self.identb = P("identb", [128, 128], BF16)
        self.scT = P("scT", [128, 80], BF16)
        self.scf = P("scf", [128, 80], F32)
        self.sel_t = P("sel_t", [128, 4], F32)
        self.modT = P("modT", [128, 2, 144], F32)
        self.normg_t = P("normg_t", [128, 48], F32)
        self.Asc = P("Asc", [128, 3, 2, KC], F32)
        self.CG = P("CG", [128, 3, 2, KC], F32)
        self.fnorm_t = P("fnorm_t", [128, KC], F32)
        self.cmask_t = P("cmask_t", [128, 2], F32)
        self.epsc = P("epsc", [128, 1], F32)
        self.mrow_t = P("mrow_t", [128, 5 * KLOC], F32)

        s0 = self.slot("misc")
        d = [s0.dma(self.sp, self.scf[:, :], self.scin[:, :]),
             s0.dma(self.sp, self.fnorm_t[:, :], self.fnorm[:, :]),
             s0.dma(self.sp, self.cmask_t[:, :], self.cmask[:, :]),
             s0.dma(self.sp, self.mrow_t[:, :], self.mrow[:, :]),
             s0.dma(self.sp, self.sel_t[:, :], self.sel[:, :]),
             s0.dma(self.sp, self.identf[:, :], self.ident_in[:, :])]
        self.pool.mark(nc.gpsimd.memset(self.onesf[:, :], 1.0))
        self.pool.mark(nc.gpsimd.memset(self.epsc[:, :], EPS))
        self.act(self.scT[:, :], self.scf[:, :], AF.Silu, deps=[d[-1]])
        self.cp(self.identb[:, :], self.identf[:, :], deps=[d[-1]])
        self.barrier()

        xsrc = self.xin
        self.prologue_mods()
        self.make_jobs()
        self.pump(10 ** 9, upto_layer=0)
        for li in range(NL):
            self.cur_layer = li
            self.phase_mod(li)
            self.barrier()
            self.phase_ffn(li, 0, xsrc, self.xT)
            xsrc = self.xT
            if self.dbg == "ffn1":
                break
            self.phase_mix(li)
            if self.dbg == "mix":
                break
            self.phase_ffn(li, 2, self.xT, self.xT)
            self.pump(10 ** 9, upto_layer=li + 1)
        if self.final:
            self.phase_final()
            self.barrier()
        return nc

    def phase_mod(self, li):
        nc = self.nc
        with ExitStack() as st:
            ma = self.sb(st, "modall", [128, 8, MODW], F32)
            sm = self.slot("misc")
            sm.dma(self.pool, self.normg_t[:, :], self.normg[li])
            dl = sm.dma(self.pool, ma[:, :, :], self.Fmod[li].ap().rearrange("(c p) f -> p c f", p=128))
            mv = ma[:, :, 0:90].rearrange("p c (j f) -> p c j f", f=5)
            m0 = self.modT[:, 0, :].rearrange("p (c j) -> p c j", j=18)
            m1 = self.modT[:, 1, :].rearrange("p (c j) -> p c j", j=18)
            dd = self.ts(m0, mv[:, :, :, 0], self.sel_t[:, 0:1], None, ALU.mult, deps=[dl])
            for col in range(1, 4):
                dd = self.stt(m0, mv[:, :, :, col], self.sel_t[:, col:col + 1], m0, ALU.mult, ALU.add, deps=[dd])
            dd = self.cp(m1, mv[:, :, :, 4], deps=[dd])
            for j in range(3):
                for tt_ in range(2):
                    sc = self.modT[:, tt_, (3 * j + 1) * 16:(3 * j + 2) * 16]
                    gt = self.modT[:, tt_, (3 * j + 2) * 16:(3 * j + 3) * 16]
                    self.stt(self.Asc[:, j, tt_, :], sc, 1.0, self.normg_t[:, j * 16:(j + 1) * 16], ALU.add, ALU.mult, deps=[dd])
                    self.ts(self.CG[:, j, tt_, :], gt, 0.5 if j != 1 else 1.0, None, ALU.mult, deps=[dd])

    def Bsc(self, j, tt_, k):
        return self.modT[:, tt_, 3 * j * 16 + k:3 * j * 16 + k + 1]

    def phase_norm(self, j, xsrc, hT, final=False):
        nc = self.nc
        NS = 256
        with ExitStack() as st:
            xs = [self.sb(st, "nxs", [128, KC, NS], F32) for _ in range(2)]
            sq = [self.sb(st, "nsq", [128, KC, NS], F32) for _ in range(2)]
            tmp = [self.sb(st, "ntmp", [128, KC, NS], F32) for _ in range(2)]
            std = [self.sb(st, "nstd", [128, NS], F32) for _ in range(2)]
            lsl = [self.slot("nx%d" % i) for i in range(2)]
            ssl = [self.slot("nst%d" % i) for i in range(2)]
            xs_free = [None, None]
            sq_free = [None, None]
            tmp_free = [None, None]
            std_free = [None, None]
            nblk = (TL if final else T) // NS
            for i in range(nblk):
                t0 = i * NS
                tt_ = 0 if t0 < TL else 1
                b = i % 2
                self.sp.wait(xs_free[b])
                ld = lsl[b].dma(self.sp, xs[b][:, :, :], xsrc[:, :, t0:t0 + NS])
                d_sq = self.act(sq[b][:, :, :], xs[b][:, :, :], AF.Square, deps=[ld, sq_free[b]])
                bk = self.next_bank()
                self.pe.wait(self.bank_free[bk], d_sq)
                for k in range(KC):
                    ins = self.mm(self.ps[bk][:, :NS], self.onesf[:, :], sq[b][:, k, :], k == 0, k == KC - 1)
                d_pe = self.pe.mark(ins)
                sq_free[b] = d_pe
                d_std = self.act(std[b][:, :], self.ps[bk][:, :NS], AF.Sqrt, bias=self.epsc[:, 0:1], scale=1.0 / D,
                                 deps=[d_pe, std_free[b]])
                self.bank_free[bk] = d_std
                d_r = self.recip(std[b][:, :], std[b][:, :], deps=[d_std])
                d1 = d2 = None
                for k in range(KC):
                    if final:
                        d1 = self.stt(tmp[b][:, k, :], xs[b][:, k, :], self.fnorm_t[:, k:k + 1], std[b][:, :], ALU.mult, ALU.mult,
                                      deps=[d_r, tmp_free[b]])
                    else:
                        d1 = self.stt(tmp[b][:, k, :], xs[b][:, k, :], self.Asc[:, j, tt_, k:k + 1], std[b][:, :], ALU.mult, ALU.mult,
                                      deps=[d_r, tmp_free[b]])
                        d2 = self.act(hT[:, k, t0:t0 + NS], tmp[b][:, k, :], AF.Identity, bias=self.Bsc(j, tt_, k), deps=[d1])
                xs_free[b] = d1
                std_free[b] = d1
                if final:
                    self.sp.wait(d1)
                    tmp_free[b] = ssl[b].dma(self.sp, self.outd[:, :, t0:t0 + NS], tmp[b][:, :, :])
                else:
                    tmp_free[b] = d2

    def phase_final(self):
        self.phase_norm(0, self.xT, None, final=True)

    def linear_fm(self, wsrc, nslab, slab_cols, units, tbs, epi, nbuf=3, wdep=None):
        if isinstance(units[0], tuple):
            units = [units]
        with ExitStack() as st:
            wb = [self.sb(st, "wslab", [128, slab_cols], BF16) for _ in range(nbuf)]
            slots = [self.slot("w%d" % i) for i in range(nbuf)]
            pe_last = {}
            deps = {}

            def issue(s):
                b = s % nbuf
                self.wq.wait(pe_last.get(b), wdep)
                deps[s] = slots[b].dma(self.wq, wb[b][:, :], wsrc(s))

            for s in range(min(nbuf - 1, nslab)):
                issue(s)
            for s in range(nslab):
                if s + nbuf - 1 < nslab:
                    issue(s + nbuf - 1)
                b = s % nbuf
                self.pe.wait(deps[s])
                pe_dep = None
                for ti, (t0, nt, tt_) in enumerate(tbs):
                    for ui, groups in enumerate(units):
                        banks = []
                        for (src, kcg, off) in groups:
                            bk = self.next_bank()
                            self.pe.wait(self.bank_free[bk])
                            for k in range(kcg):
                                ins = self.mm(self.ps[bk][:, :nt], wb[b][:, off + k * 128:off + (k + 1) * 128],
                                              src[:, k, t0:t0 + nt], k == 0, k == kcg - 1)
                            banks.append(bk)
                        pe_dep = self.pe.mark(ins)
                        rel = epi(s, ti, t0, nt, tt_, ui, [self.ps[bk][:, :nt] for bk in banks], pe_dep)
                        for bk in banks:
                            self.bank_free[bk] = rel
                pe_last[b] = pe_dep

    def linear_tm(self, st, wsrcs, hT, epi, wdep=None):
        wv = [self.sb(st, "wtm", [128, 8192], BF16) for _ in range(2)]
        self.wq.wait(wdep)
        dl = [self.slot("wt%d" % i).dma(self.wq, wv[i][:, :], wsrcs[i]) for i in range(2)]
        self.pe.wait(*dl)
        for tile in range(10):
            banks = []
            for blk in range(2):
                bk = self.next_bank()
                self.pe.wait(self.bank_free[bk])
                for k in range(KC):
                    ins = self.mm(self.ps[bk][:, :], hT[:, k, tile * 128:(tile + 1) * 128], wv[blk][:, k * 512:(k + 1) * 512],
                                  k == 0, k == KC - 1)
                banks.append(bk)
            pe_dep = self.pe.mark(ins)
            rel = epi(tile, [self.ps[bk] for bk in banks], pe_dep)
            for bk in banks:
                self.bank_free[bk] = rel

    def make_residual_epi(self, st, xs_ap, xd_ap, j):
        NR = 4
        xin_t = [self.sb(st, "rxi", [128, 512], F32) for _ in range(NR)]
        xout_t = [self.sb(st, "rxo", [128, 512], F32) for _ in range(NR)]
        lsl = [self.slot("rl%d" % i) for i in range(NR)]
        ssl = [self.slot("rs%d" % i) for i in range(NR)]
        state = {"n": 0, "xin_free": [None] * NR, "st_done": [None] * NR}

        def epi(s, ti, t0, nt, tt_, ui, banks, pe_dep):
            r = state["n"] % NR
            state["n"] += 1
            self.aq.wait(state["xin_free"][r])
            ld = lsl[r].dma(self.aq, xin_t[r][:, :nt], xs_ap[:, s, t0:t0 + nt])
            dd = self.stt(xout_t[r][:, :nt], banks[0], self.CG[:, j, tt_, s:s + 1], xin_t[r][:, :nt], ALU.mult, ALU.add,
                          deps=[pe_dep, ld, state["st_done"][r]])
            state["xin_free"][r] = dd
            self.actE.wait(dd)
            state["st_done"][r] = ssl[r].dma(self.actE, xd_ap[:, s, t0:t0 + nt], xout_t[r][:, :nt])
            return dd
        return epi

    def phase_ffn(self, li, j, xsrc, xdst):
        w = 0 if j == 0 else 1
        with ExitStack() as st:
            hT = self.sb(st, "hT", [128, KC, T], BF16)
            self.phase_norm(j, xsrc, hT)
            self.barrier()
            actT = self.sb(st, "actT", [128, HFC, T], BF16)
            sg = [self.sb(st, "sg", [128, 512], F32) for _ in range(2)]
            for hf in range(2):
                state = {"n": 0, "free": [None, None]}

                def epi_b(s, ti, t0, nt, tt_, ui, banks, pe_dep):
                    b = state["n"] % 2
                    state["n"] += 1
                    d1 = self.act(sg[b][:, :nt], banks[0], AF.Silu, deps=[pe_dep, state["free"][b]])
                    d2 = self.tt(actT[:, s, t0:t0 + nt], sg[b][:, :nt], banks[1], ALU.mult, deps=[d1])
                    state["free"][b] = d2
                    return d2
                self.linear_fm(lambda s: self.slab('wgu%d' % (w + 1), li, hf * HFC + s), HFC, 4096, [(hT, KC, 0), (hT, KC, 2048)], TBS, epi_b, wdep=self.wdep('wgu%d' % (w + 1), li))
                self.barrier()
                with ExitStack() as st2:
                    epi_r = self.make_residual_epi(st2, xsrc if hf == 0 else xdst, xdst, j)
                    self.linear_fm(lambda s: self.slab('wd%d' % (w + 1), li, hf * 16 + s), 16, HFC * 128, [(actT, HFC, 0)], TBS, epi_r, wdep=self.wdep('wd%d' % (w + 1), li))
                    self.barrier()

    def phase_mix(self, li):
        with ExitStack() as stA:
            hT = self.sb(stA, "hT", [128, KC, T], BF16)
            self.phase_norm(1, self.xT, hT)
            self.barrier()
            self.slot("misc").dma(self.sp, self.hT_d[:, :, :], hT[:, :, :])
            self.stage_convin(li, hT)
            self.barrier()
            self.stage_qkv(li, hT)
            self.barrier()
            self.exchange()
            self.stage_gmlp(li, hT)
            self.barrier()
        self.stage_conv(li)
        self.barrier()
        self.stage_attn(li)
        self.barrier()
        self.stage_merge(li)
        self.barrier()
        self.stage_wout(li)
        self.barrier()

    def stage_convin(self, li, hT):
        with ExitStack() as st:
            sgm = [self.sb(st, "cisg", [128, 512], F32) for _ in range(2)]
            ao = [self.sb(st, "ciao", [128, 512], F32) for _ in range(2)]
            ssl = [self.slot("st%d" % i) for i in range(2)]
            state = {"n": 0, "sg_free": [None, None], "st_done": [None, None]}

            def epi(s, ti, t0, nt, tt_, ui, banks, pe_dep):
                b = state["n"] % 2
                state["n"] += 1
                d1 = self.act(sgm[b][:, :nt], banks[1], AF.Sigmoid, deps=[pe_dep, state["sg_free"][b]])
                d2 = self.tt(ao[b][:, :nt], sgm[b][:, :nt], banks[0], ALU.mult, deps=[d1, state["st_done"][b]])
                state["sg_free"][b] = d2
                self.aq.wait(d2)
                state["st_done"][b] = ssl[b].dma(self.aq, self.a_d[:, s, t0:t0 + nt], ao[b][:, :nt])
                return d2
            self.linear_fm(lambda s: self.slab('wconv', li, s), 8, 4096, [(hT, KC, 0), (hT, KC, 2048)], TBS, epi, wdep=self.wdep('wconv', li))

    def stage_qkv(self, li, hT):
        with ExitStack() as st:
            qs = [self.sb(st, "qs", [128, 512], BF16) for _ in range(2)]
            ssl = [self.slot("st%d" % i) for i in range(2)]
            state = {"n": 0, "st_done": [None, None]}

            def epi(s, ti, t0, nt, tt_, ui, banks, pe_dep):
                b = state["n"] % 2
                state["n"] += 1
                isk = s >= 8
                c = s % 8
                if isk:
                    d = self.cp(qs[b][:, :nt], banks[0], deps=[pe_dep, state["st_done"][b]], eng=self.actE)
                else:
                    d = self.ts(qs[b][:, :nt], banks[0], 0.125, None, ALU.mult, deps=[pe_dep, state["st_done"][b]])
                self.aq.wait(d)
                dst = self.k_d if isk else self.q_d
                state["st_done"][b] = ssl[b].dma(self.aq, dst[:, c, t0:t0 + nt], qs[b][:, :nt])
                return d
            self.linear_fm(lambda s: self.slab('wfm1', li, 8 + s), 16, 2048, [(hT, KC, 0)], TBS, epi, wdep=self.wdep('wfm1', li))
            self.barrier()
            vs = [self.sb(st, "vs", [128, 1024], BF16) for _ in range(2)]
            state2 = {"n": 0, "st_done": [None, None]}

            def epi_v(tile, banks, pe_dep):
                b = state2["n"] % 2
                state2["n"] += 1
                d1 = self.cp(vs[b][:, 0:512], banks[0][:, :], deps=[pe_dep, state2["st_done"][b]], eng=self.actE)
                d2 = self.cp(vs[b][:, 512:1024], banks[1][:, :], deps=[pe_dep, state2["st_done"][b]])
                self.aq.wait(d1, d2)
                state2["st_done"][b] = ssl[b].dma(self.aq, self.v_d[tile], vs[b][:, :])
                return [d1, d2]
            self.linear_tm(st, [self.slab('wintm', li, 2), self.slab('wintm', li, 3)], hT, epi_v, wdep=self.wdep('wintm', li))

    def exchange(self):
        nc = self.nc
        s = self.slot("exp")
        EA, EK, EV = self.EA.ap(), self.EK.ap(), self.EV.ap()
        for side, (a0, k0, vt) in enumerate([(0, 0, 0), (TL - 16, TL - 256, 6)]):
            s.dma(self.aq, EA[side * 128:(side + 1) * 128, :].rearrange("p (c j) -> p c j", j=16), self.a_d[:, :, a0:a0 + 16])
            s.dma(self.aq, EK[side * 128:(side + 1) * 128, :].rearrange("p (c j) -> p c j", j=256), self.k_d[:, :, k0:k0 + 256])
            s.dma(self.aq, EV[side * 256:(side + 1) * 256, :].rearrange("(t p) f -> t p f", p=128), self.v_d[vt:vt + 2])
        self.pool.wait(s.dep())
        groups = [[2 * i, 2 * i + 1] for i in range(self.ncores // 2)]
        for (E, I) in ((self.EA, self.IA), (self.EK, self.IK), (self.EV, self.IV)):
            nc.gpsimd.collective_compute("AllGather", ALU.bypass, replica_groups=groups,
                                         ins=[E.ap()[:, :]], outs=[I.ap()[:, :]]).then_inc(self.cc_sem, 1)
            self.ccn += 1
            self.pool.wait((self.cc_sem, self.ccn))

    def stage_gmlp(self, li, hT):
        nc = self.nc
        with ExitStack() as st:
            vln = self.sb(st, "vln", [128, 10, 1024], BF16)
            glng_t = self.sb(st, "glng", [128, 1024], F32)
            glnb_t = self.sb(st, "glnb", [128, 1024], F32)
            wsT_t = self.sb(st, "wsT", [128, 512], BF16)
            bsb_t = self.sb(st, "bsb", [128, 2048], F32)
            sm = self.slot("misc")
            dl = [sm.dma(self.aq, glng_t[:, :], self.glng[li]), sm.dma(self.aq, glnb_t[:, :], self.glnb[li]),
                  sm.dma(self.aq, bsb_t[:, :], self.bsb[li])]
            dws = self.slot("misc2").dma(self.pool, wsT_t[:, :], self.wsT[li])
            with ExitStack() as st1:
                vf = [self.sb(st1, "vf", [128, 1024], F32) for _ in range(2)]
                stats = [self.sb(st1, "vstat", [128, 12], F32) for _ in range(2)]
                mv = [self.sb(st1, "vmv", [128, 2], F32) for _ in range(2)]
                sd = [self.sb(st1, "vsd", [128, 1], F32) for _ in range(2)]
                state = {"n": 0, "free": [None, None]}

                def epi_v(tile, banks, pe_dep):
                    b = state["n"] % 2
                    state["n"] += 1
                    g1 = self.act(vf[b][:, 0:512], banks[0][:, :], AF.Gelu_apprx_tanh, deps=[pe_dep, state["free"][b]])
                    g2 = self.act(vf[b][:, 512:1024], banks[1][:, :], AF.Gelu_apprx_tanh, deps=[pe_dep])
                    self.dve.wait(g1, g2)
                    self.dve.mark(nc.vector.bn_stats(out=stats[b][:, 0:6], in_=vf[b][:, 0:512]))
                    d = self.dve.mark(nc.vector.bn_stats(out=stats[b][:, 6:12], in_=vf[b][:, 512:1024]))
                    self.dve.wait(d)
                    d = self.dve.mark(nc.vector.bn_aggr(out=mv[b][:, :], in_=stats[b][:, :]))
                    d = self.act(sd[b][:, :], mv[b][:, 1:2], AF.Sqrt, bias=self.epsc[:, 0:1], deps=[d])
                    d = self.recip(sd[b][:, :], sd[b][:, :], deps=[d])
                    d = self.ts(vf[b][:, :], vf[b][:, :], mv[b][:, 0:1], sd[b][:, 0:1], ALU.subtract, ALU.mult, deps=[d])
                    d = self.tt(vf[b][:, :], vf[b][:, :], glng_t[:, :], ALU.mult, deps=[d, dl[2]])
                    d = self.tt(vln[:, tile, :], vf[b][:, :], glnb_t[:, :], ALU.add, deps=[d, dl[2]])
                    state["free"][b] = d
                    return [g1, g2]
                self.linear_tm(st1, [self.slab('wintm', li, 0), self.slab('wintm', li, 1)], hT, epi_v, wdep=self.wdep('wintm', li))
                self.barrier()
            uf = [self.sb(st, "guf", [128, 512], F32) for _ in range(2)]
            tmp = [self.sb(st, "gtmp", [128, 512], F32) for _ in range(2)]
            go = [self.sb(st, "ggo", [128, 512], BF16) for _ in range(2)]
            ssl = [self.slot("st%d" % i) for i in range(2)]
            state2 = {"n": 0, "uf_free": [None, None], "st_done": [None, None]}

            def epi_u(s, ti, t0, nt, tt_, ui, banks, pe_dep):
                b = state2["n"] % 2
                state2["n"] += 1
                g = s // 2
                d1 = self.act(uf[b][:, :nt], banks[0], AF.Gelu_apprx_tanh, deps=[pe_dep, state2["uf_free"][b]])
                bk2 = self.next_bank()
                self.pe.wait(self.bank_free[bk2], dws)
                for n_ in range(nt // 128):
                    tile = t0 // 128 + n_
                    ins = self.mm(self.ps[bk2][:, n_ * 128:(n_ + 1) * 128], vln[:, tile, s * 128:(s + 1) * 128],
                                  wsT_t[:, g * 128:(g + 1) * 128], True, True)
                dpe2 = self.pe.mark(ins)
                d2 = self.tt(tmp[b][:, :nt], self.ps[bk2][:, :nt], bsb_t[:, g * 512:g * 512 + nt], ALU.add,
                             deps=[dpe2, dl[2], state2["uf_free"][b]])
                self.bank_free[bk2] = d2
                d3 = self.tt(go[b][:, :nt], tmp[b][:, :nt], uf[b][:, :nt], ALU.mult, deps=[d2, d1, state2["st_done"][b]])
                state2["uf_free"][b] = d3
                self.aq.wait(d3)
                state2["st_done"][b] = ssl[b].dma(self.aq, self.gm_d[:, s, t0:t0 + nt], go[b][:, :nt])
                return d1
            self.linear_fm(lambda s: self.slab('wfm1', li, s), 8, 2048, [(hT, KC, 0)], TBS, epi_u, wdep=self.wdep('wfm1', li))

    def stage_conv(self, li):
        nc = self.nc
        with ExitStack() as st:
            aT = self.sb(st, "caT", [128, 8, AW], BF16)
            y = self.sb(st, "cy", [128, 8, T], F32)
            cT = self.sb(st, "ccT", [128, 8, T], BF16)
            dw = self.sb(st, "cdw", [128, 248], F32)
            cpp = self.sb(st, "ccp", [128, 24], F32)
            sq = self.sb(st, "csq", [128, 8, 512], F32)
            mean = self.sb(st, "cmean", [128, 512], F32)
            m2 = self.sb(st, "cm2", [128, 512], F32)
            rstd = self.sb(st, "crstd", [128, 512], F32)
            zt = [self.sb(st, "czt", [128, 512], F32) for _ in range(2)]
            dg = [self.sb(st, "cdg", [128, 31, 128], BF16) for _ in range(2)]
            sl = self.slot("misc")
            IA = self.IA.ap()
            dl = [sl.dma(self.aq, dw[:, :], self.convdw[li]), sl.dma(self.aq, cpp[:, :], self.convp[li]),
                  sl.dma(self.aq, aT[:, :, 16:16 + TL], self.a_d[:, :, 0:TL]),
                  sl.dma(self.aq, aT[:, :, 1072:1328], self.a_d[:, :, TL:T]),
                  sl.dma(self.aq, aT[:, :, 0:16], IA[128:256, :].rearrange("p (c j) -> p c j", j=16)),
                  sl.dma(self.aq, aT[:, :, 1040:1056], IA[256:384, :].rearrange("p (c j) -> p c j", j=16))]
            dall = dl[-1]
            dz = self.pool.mark(nc.gpsimd.memset(aT[:, :, 1056:1072], 0.0))
            dz = self.pool.mark(nc.gpsimd.memset(aT[:, :, 1328:1344], 0.0))
            dm = self.ts(aT[:, :, 0:16], aT[:, :, 0:16], self.cmask_t[:, 0:1], None, ALU.mult, deps=[dall])
            dm = self.ts(aT[:, :, 1040:1056], aT[:, :, 1040:1056], self.cmask_t[:, 1:2], None, ALU.mult, deps=[dall])
            dg_free = [None, None]
            d = None
            for cc in range(8):
                gb = cc % 2
                dgd = None
                for k in range(31):
                    dgd = self.ts(dg[gb][:, k, :], self.identb[:, :], dw[:, cc * 31 + k:cc * 31 + k + 1], None, ALU.mult,
                                  deps=[dall, dg_free[gb]])
                for (o0, yo, nt) in ((0, 0, 512), (512, 512, 512), (1056, TL, 256)):
                    bk = self.next_bank()
                    self.pe.wait(self.bank_free[bk], dgd, dm, dz)
                    for k in range(31):
                        ins = self.mm(self.ps[bk][:, :nt], dg[gb][:, k, :], aT[:, cc, o0 + 1 + k:o0 + 1 + k + nt], k == 0, k == 30)
                    dpe = self.pe.mark(ins)
                    d = self.act(y[:, cc, yo:yo + nt], self.ps[bk][:, :nt], AF.Identity, bias=cpp[:, cc:cc + 1], deps=[dpe, dall])
                    self.bank_free[bk] = d
                dg_free[gb] = dpe
            dconv = d
            zfree = [None, None]
            zi = 0
            dlast = None
            for (t0, nt, tt_) in TBS:
                dsq = self.act(sq[:, :, :nt], y[:, :, t0:t0 + nt], AF.Square, deps=[dconv, dlast])
                b1 = self.next_bank()
                self.pe.wait(self.bank_free[b1], dconv)
                for cc in range(8):
                    ins = self.mm(self.ps[b1][:, :nt], self.onesf[:, :], y[:, cc, t0:t0 + nt], cc == 0, cc == 7)
                dp1 = self.pe.mark(ins)
                b2 = self.next_bank()
                self.pe.wait(self.bank_free[b2], dsq)
                for cc in range(8):
                    ins = self.mm(self.ps[b2][:, :nt], self.onesf[:, :], sq[:, cc, :nt], cc == 0, cc == 7)
                dp2 = self.pe.mark(ins)
                dmean = self.act(mean[:, :nt], self.ps[b1][:, :nt], AF.Identity, scale=1.0 / 1024, deps=[dp1, dlast])
                self.bank_free[b1] = dmean
                d = self.tt(m2[:, :nt], mean[:, :nt], mean[:, :nt], ALU.mult, deps=[dmean])
                d = self.stt(m2[:, :nt], self.ps[b2][:, :nt], 1.0 / 1024, m2[:, :nt], ALU.mult, ALU.subtract, deps=[d, dp2])
                self.bank_free[b2] = d
                d = self.act(rstd[:, :nt], m2[:, :nt], AF.Sqrt, bias=self.epsc[:, 0:1], deps=[d])
                drs = self.recip(rstd[:, :nt], rstd[:, :nt], deps=[d])
                for cc in range(8):
                    zb = zi % 2
                    zi += 1
                    d = self.tt(zt[zb][:, :nt], y[:, cc, t0:t0 + nt], mean[:, :nt], ALU.subtract, deps=[drs, zfree[zb]])
                    d = self.tt(zt[zb][:, :nt], zt[zb][:, :nt], rstd[:, :nt], ALU.mult, deps=[d])
                    d = self.act(cT[:, cc, t0:t0 + nt], zt[zb][:, :nt], AF.Silu, bias=cpp[:, 16 + cc:17 + cc], scale=cpp[:, 8 + cc:9 + cc],
                                 deps=[d])
                    zfree[zb] = d
                dlast = d
            self.aq.wait(dlast)
            self.slot("misc").dma(self.aq, self.conv_d[:, :, :], cT[:, :, :])

    def stage_attn(self, li):
        nc = self.nc
        with ExitStack() as st:
            Kb = [self.sb(st, "aK", [128, 1792], BF16) for _ in range(2)]
            Qb = [self.sb(st, "aQ", [128, T], BF16) for _ in range(2)]
            Vb = [self.sb(st, "aV", [128, 14, 128], BF16) for _ in range(2)]
            NBI = 10
            bias_t = [self.sb(st, "abias", [128, KLOC], F32) for _ in range(NBI)]
            NRG = 3
            S = [self.sb(st, "aS", [128, 1024], F32) for _ in range(NRG)]
            Pm = [self.sb(st, "aP", [128, 1024], BF16) for _ in range(NRG)]
            PT = [self.sb(st, "aPT", [128, 1024], BF16) for _ in range(NRG)]
            Osb = [self.sb(st, "aO", [128, 10, 128], BF16) for _ in range(2)]
            ao = [self.sb(st, "aao", [128, T], BF16) for _ in range(2)]
            nmx = [self.sb(st, "anmx", [128, 1], F32) for _ in range(NRG)]
            rsum = [self.sb(st, "arsum", [128, 1], F32) for _ in range(NRG)]
            rinv = [self.sb(st, "arinv", [128, 1], F32) for _ in range(NRG)]
            kslot = [self.slot("akqv%d" % i) for i in range(2)]
            bt_t = [self.sb(st, "abt", [128, KLOC], F32) for _ in range(4)]
            bslot = [self.slot("ab%d" % i) for i in range(4)]
            bt_free = [None] * 4
            oslot = [self.slot("st%d" % i) for i in range(2)]
            IK, IV = self.IK.ap(), self.IV.ap()
            kqv_free = [None, None]
            bias_free = [None] * NBI
            s_free = [None] * NRG
            ocnt = 0
            ao_done = [None, None]
            osb_free = [None, None]
            un = 0
            for c in range(8):
                b = c % 2
                self.sp.wait(kqv_free[b])
                sl = kslot[b]
                sl.dma(self.sp, Kb[b][:, 256:1280], self.k_d[:, c, 0:TL])
                sl.dma(self.sp, Kb[b][:, 1536:1792], self.k_d[:, c, TL:T])
                sl.dma(self.sp, Kb[b][:, 0:256], IK[128:256, c * 256:(c + 1) * 256])
                sl.dma(self.sp, Kb[b][:, 1280:1536], IK[256:384, c * 256:(c + 1) * 256])
                sl.dma(self.sp, Qb[b][:, :], self.q_d[:, c, :])
                sl.dma(self.sp, Vb[b][:, 2:10, :], self.v_d[0:8, :, c * 128:(c + 1) * 128].rearrange("t p f -> p t f"))
                sl.dma(self.sp, Vb[b][:, 12:14, :], self.v_d[8:10, :, c * 128:(c + 1) * 128].rearrange("t p f -> p t f"))
                sl.dma(self.sp, Vb[b][:, 0:2, :], IV[256:512, c * 128:(c + 1) * 128].rearrange("(t p) f -> p t f", p=128))
                dkqv = sl.dma(self.sp, Vb[b][:, 10:12, :], IV[512:768, c * 128:(c + 1) * 128].rearrange("(t p) f -> p t f", p=128))
                last_pe = None
                for hh in range(2):
                    h = 2 * c + hh
                    p0 = hh * 64
                    bdeps = []
                    btd = []
                    for v in range(2):
                        ti_ = (h % 2) * 2 + v
                        self.sp.wait(bt_free[ti_])
                        btd.append(bslot[ti_].dma(self.sp, bt_t[ti_][:, :], self.btd[li, (h * 2 + v) * 128:(h * 2 + v + 1) * 128, :]))
                    for slot_i in range(5):
                        bi = (h % 2) * 5 + slot_i
                        v = 1 if slot_i == 4 else 0
                        ti_ = (h % 2) * 2 + v
                        dd_ = self.tt(bias_t[bi][:, :], bt_t[ti_][:, :], self.mrow_t[:, slot_i * KLOC:(slot_i + 1) * KLOC], ALU.add,
                                      deps=[btd[v], bias_free[bi]])
                        bdeps.append(dd_)
                        bt_free[ti_] = dd_
                    for u in range(10):
                        r = un % NRG
                        un += 1
                        lat = u < 8
                        qc = u * 128
                        nk = 1024 if lat else 256
                        sA = self.next_bank()
                        sB = self.next_bank() if lat else None
                        osl = ocnt % 8
                        ocnt += 1
                        oP = self.po[:, osl * 64:(osl + 1) * 64]
                        self.pe.wait(dkqv, self.bank_free[sA])
                        qT = Qb[b][p0:p0 + 64, qc:qc + 128]
                        if lat:
                            koff = WS_ROWS[u] * 64
                            self.mm(self.ps[sA][:, :], qT, Kb[b][p0:p0 + 64, koff:koff + 512], True, True)
                            self.pe.wait(self.bank_free[sB])
                            self.mm(self.ps[sB][:, 0:256], qT, Kb[b][p0:p0 + 64, koff + 512:koff + 768], True, True)
                            ins = self.mm(self.ps[sB][:, 256:512], qT, Kb[b][p0:p0 + 64, 1536:1792], True, True)
                        else:
                            ins = self.mm(self.ps[sA][:, 0:256], qT, Kb[b][p0:p0 + 64, 1536:1792], True, True)
                        dS = self.pe.mark(ins)
                        if lat:
                            bi = (h % 2) * 5 + SLOT_OF_RP[u]
                            d1 = self.tt(S[r][:, 0:512], self.ps[sA][:, :], bias_t[bi][:, 0:512], ALU.add,
                                         deps=[dS, bdeps[SLOT_OF_RP[u]], s_free[r]])
                            d2 = self.tt(S[r][:, 512:768], self.ps[sB][:, 0:256], bias_t[bi][:, 512:768], ALU.add, deps=[dS])
                            bias_free[bi] = d2
                            d3 = self.cp(S[r][:, 768:1024], self.ps[sB][:, 256:512], deps=[dS, s_free[r]], eng=self.actE)
                            self.bank_free[sA] = d1
                            self.bank_free[sB] = [d2, d3]
                            dsc = [d1, d2, d3]
                        else:
                            d3 = self.cp(S[r][:, 0:256], self.ps[sA][:, 0:256], deps=[dS, s_free[r]], eng=self.actE)
                            self.bank_free[sA] = d3
                            dsc = [d3]
                        self.dve.wait(*dsc)
                        dmx = self.dve.mark(nc.vector.tensor_reduce(out=nmx[r][:, :], in_=S[r][:, :nk], axis=AX.X, op=ALU.max, negate=True))
                        dex = self.act(Pm[r][:, :nk], S[r][:, :nk], AF.Exp, bias=nmx[r][:, 0:1], deps=[dmx], accum_out=rsum[r][:, 0:1])
                        pb = 0
                        self.pe.wait(dex, self.pt_free[pb])
                        for jj in range(nk // 128):
                            ins = nc.tensor.transpose(self.pt[pb][:, jj * 128:(jj + 1) * 128], Pm[r][:, jj * 128:(jj + 1) * 128], self.identb[:, :])
                        dT = self.pe.mark(ins)
                        ceng = self.actE if (un % 2 == 0) else self.dve
                        dcp = self.cp(PT[r][:, :nk], self.pt[pb][:, :nk], deps=[dT], eng=ceng)
                        self.pt_free[pb] = dcp
                        self.pe.wait(dcp, self.po_free[osl])
                        nj = nk // 128
                        for jj in range(nj):
                            if lat:
                                vt = (WS_ROWS[u] // 2 + jj) if jj < 6 else (12 + jj - 6)
                            else:
                                vt = 12 + jj
                            ins = self.mm(oP, PT[r][:, jj * 128:(jj + 1) * 128], Vb[b][:, vt, p0:p0 + 64], jj == 0, jj == nj - 1)
                        dO = self.pe.mark(ins)
                        last_pe = dO
                        dri = self.recip(rinv[r][:, :], rsum[r][:, :], deps=[dex])
                        dos = self.ts(Osb[b][:, u, p0:p0 + 64], oP, rinv[r][:, 0:1], None, ALU.mult,
                                      deps=[dO, dri, osb_free[b]])
                        self.po_free[osl] = dos
                        s_free[r] = [dcp, dos, dO]
                kqv_free[b] = last_pe
                for (u0, nu) in ((0, 8), (8, 2)):
                    pb = 0
                    self.pe.wait(dos, self.pt_free[pb])
                    for uu in range(nu):
                        ins = nc.tensor.transpose(self.pt[pb][:, uu * 128:(uu + 1) * 128], Osb[b][:, u0 + uu, :], self.identb[:, :])
                    dT = self.pe.mark(ins)
                    dcp = self.cp(ao[b][:, u0 * 128:(u0 + nu) * 128], self.pt[pb][:, :nu * 128], deps=[dT, ao_done[b]])
                    self.pt_free[pb] = dcp
                osb_free[b] = dT
                self.aq.wait(dcp)
                ao_done[b] = oslot[b].dma(self.aq, self.attn_d[:, c, :], ao[b][:, :])

    def stage_merge(self, li):
        with ExitStack() as st:
            hT = self.sb(st, "mhT", [128, KC, T], BF16)
            bT = [self.sb(st, "mbT", [128, 8, T], BF16) for _ in range(3)]
            sl = self.slot("misc")
            dl = [sl.dma(self.sp, hT[:, :, :], self.hT_d[:, :, :])]
            for i, src in enumerate((self.conv_d, self.gm_d, self.attn_d)):
                dl.append(sl.dma(self.sp, bT[i][:, :, :], src[:, :, :]))
            self.pe.wait(dl[-1])
            sgt = [self.sb(st, "msg", [128, 512], F32) for _ in range(2)]
            tmp = [self.sb(st, "mtmp", [128, 512], F32) for _ in range(2)]
            acc = [self.sb(st, "macc", [128, 512], F32) for _ in range(2)]
            ys = [self.sb(st, "mys", [128, 512], BF16) for _ in range(2)]
            ssl = [self.slot("st%d" % i) for i in range(2)]
            state = {"n": 0, "sg_free": [None, None], "acc_dep": [None, None], "acc_free": [None, None], "st_done": [None, None],
                     "m": 0}

            def epi(s, ti, t0, nt, tt_, ui, banks, pe_dep):
                b = state["n"] % 2
                state["n"] += 1
                a = state["m"] % 2
                d1 = self.act(sgt[b][:, :nt], banks[0], AF.Sigmoid, deps=[pe_dep, state["sg_free"][b]])
                if ui == 0:
                    d2 = self.tt(acc[a][:, :nt], sgt[b][:, :nt], banks[1], ALU.mult, deps=[d1, state["acc_free"][a]])
                    state["acc_dep"][a] = d2
                    state["sg_free"][b] = d2
                    return d2
                d2 = self.tt(tmp[b][:, :nt], sgt[b][:, :nt], banks[1], ALU.mult, deps=[d1])
                if ui == 1:
                    d3 = self.tt(acc[a][:, :nt], acc[a][:, :nt], tmp[b][:, :nt], ALU.add, deps=[d2, state["acc_dep"][a]])
                    state["acc_dep"][a] = d3
                    state["sg_free"][b] = d3
                    return d2
                d3 = self.tt(ys[a][:, :nt], acc[a][:, :nt], tmp[b][:, :nt], ALU.add, deps=[d2, state["acc_dep"][a], state["st_done"][a]])
                state["sg_free"][b] = d3
                state["acc_free"][a] = d3
                state["m"] += 1
                self.aq.wait(d3)
                state["st_done"][a] = ssl[a].dma(self.aq, self.y_d[:, s, t0:t0 + nt], ys[a][:, :nt])
                return d2
            units = [[(hT, KC, br * 2048), (bT[br], 8, 6144 + br * 1024)] for br in range(3)]
            self.linear_fm(lambda s: self.slab('wmix', li, s), 16, 9216, units, TBS, epi, nbuf=2, wdep=self.wdep('wmix', li))

    def stage_wout(self, li):
        with ExitStack() as st:
            yT = self.sb(st, "yT", [128, KC, T], BF16)
            d = self.slot("misc").dma(self.sp, yT[:, :, :], self.y_d[:, :, :])
            self.pe.wait(d)
            epi_r = self.make_residual_epi(st, self.xT, self.xT, 1)
            self.linear_fm(lambda s: self.slab('wout', li, s), 16, 2048, [(yT, KC, 0)], TBS, epi_r, wdep=self.wdep('wout', li))


def fm_slabs(w, kc):
    K_, N_ = w.shape
    return np.ascontiguousarray(w.reshape(kc, 128, N_ // 128, 128).transpose(2, 1, 0, 3)).reshape(N_ // 128, 128, kc * 128)


def fvec(v):
    return np.ascontiguousarray(v.reshape(-1, 128).T)


SLOT_OF_RP = [0, 1, 2, 2, 2, 2, 3, 4]
REP_RP = [0, 1, 2, 6, 7]


def build_bt(rpb):
    ri = np.arange(2)[:, None, None, None]
    c = np.arange(64)[None, :, None, None]
    j = np.arange(12)[None, None, :, None]
    kc = np.arange(64)[None, None, None, :]
    cs = np.clip(c - 8, 0, 48)
    vcol = np.broadcast_to((kc >= cs) & (kc < cs + 16), (2, 64, 12, 64))
    dc = np.broadcast_to(np.clip(kc - c, -15, 15) + 15, (2, 64, 12, 64))
    out = np.empty((16, 2, 128, KLOC), np.float32)
    for v, off in enumerate((3, 1)):
        dr = np.broadcast_to(np.clip(j - ri + off, 0, 14), (2, 64, 12, 64))
        vals = rpb[:, dr, dc]
        vals = np.where(vcol[None], vals, np.float32(-1e9))
        out[:, v] = vals.reshape(16, 128, KLOC)
    return out


def build_mrow(half):
    R0 = 16 * half
    out = np.empty((128, 5, KLOC), np.float32)
    ri = np.arange(2)[:, None, None, None]
    j = np.arange(12)[None, None, :, None]
    for slot, rp in enumerate(REP_RP):
        r = R0 + 2 * rp + ri
        g = WS_ROWS[rp] + j + R0 - 4
        rs = np.clip(r - 4, 0, 24)
        vrow = (g >= 0) & (g < 32) & (g >= rs) & (g < rs + 8)
        m = np.where(np.broadcast_to(vrow, (2, 64, 12, 64)), np.float32(0.0), np.float32(-1e9))
        out[:, slot] = m.reshape(128, KLOC)
    return out.reshape(128, 5 * KLOC)


def prep_shared(inp, layers):
    sh = {}
    L = layers
    st = lambda f: np.stack([f(i) for i in L]).astype(np.float32, copy=False)
    sh["wmod"] = st(lambda i: fm_slabs(inp["w_mod"][i], KC))
    sh["bmod"] = st(lambda i: fvec(inp["b_mod"][i]))
    sh["normg"] = st(lambda i: np.concatenate([fvec(inp["norm_ffn1"][i]), fvec(inp["norm_mix"][i]), fvec(inp["norm_ffn2"][i])], axis=1))
    for nm, src in (("wgu1", "ffn1_w_gu"), ("wgu2", "ffn2_w_gu")):
        sh[nm] = st(lambda i: np.concatenate([fm_slabs(inp[src][i][:, :DFF], KC), fm_slabs(inp[src][i][:, DFF:], KC)], axis=2))
    for nm, src in (("wd1", "ffn1_w_down"), ("wd2", "ffn2_w_down")):
        sh[nm] = st(lambda i: np.concatenate([fm_slabs(inp[src][i][hf * 2816:(hf + 1) * 2816], HFC) for hf in range(2)], axis=0))
    win = inp["w_in"]
    sh["wconv"] = st(lambda i: np.concatenate([fm_slabs(win[i][:, 0:1024], KC), fm_slabs(win[i][:, 1024:2048], KC)], axis=2))
    sh["wfm1"] = st(lambda i: np.concatenate([fm_slabs(win[i][:, 2048:3072], KC), fm_slabs(win[i][:, 4096:5120], KC),
                                              fm_slabs(win[i][:, 5120:6144], KC)], axis=0))

    def tm(w):
        return np.ascontiguousarray(w.reshape(KC, 128, 512).transpose(1, 0, 2)).reshape(128, KC * 512)
    sh["wintm"] = st(lambda i: np.stack([tm(win[i][:, 3072:3584]), tm(win[i][:, 3584:4096]),
                                         tm(win[i][:, 6144:6656]), tm(win[i][:, 6656:7168])]))

    def mix(i):
        g = [fm_slabs(win[i][:, 7168 + b * D:7168 + (b + 1) * D], KC) for b in range(3)]
        o = [fm_slabs(inp[n][i], 8) for n in ("w_conv_out", "w_gmlp_out", "w_attn_out")]
        return np.concatenate(g + o, axis=2)
    sh["wmix"] = st(mix)
    sh["wout"] = st(lambda i: fm_slabs(inp["w_out"][i], KC))
    sh["convdw"] = st(lambda i: np.ascontiguousarray(inp["conv_dw"][i].reshape(31, 8, 128).transpose(2, 1, 0)).reshape(128, 248))
    sh["convp"] = st(lambda i: np.concatenate([fvec(inp["conv_db"][i]), fvec(inp["conv_ln_g"][i]), fvec(inp["conv_ln_b"][i])], axis=1))
    sh["glng"] = st(lambda i: np.broadcast_to(inp["gmlp_ln_g"][i][None, :], (128, 1024)))
    sh["glnb"] = st(lambda i: np.broadcast_to(inp["gmlp_ln_b"][i][None, :], (128, 1024)))
    sh["wsT"] = st(lambda i: np.ascontiguousarray(inp["gmlp_ws"][i].transpose(2, 0, 1)).reshape(128, 512))
    sh["bsb"] = st(lambda i: np.broadcast_to(np.broadcast_to(inp["gmlp_bs"][i][:, None, :], (4, 4, 128)).reshape(1, 2048), (128, 2048)))
    sh["bt"] = st(lambda i: build_bt(inp["attn_rpb"][i])).reshape(len(L), 32 * 128, 768)
    sh["fnorm"] = fvec(inp["final_norm"]).astype(np.float32)
    sh["ident"] = np.eye(128, dtype=np.float32)
    return {k: np.ascontiguousarray(v) for k, v in sh.items()}


def prep_core(inp, layers, core, x_T=None):
    b, half = core // 2, core % 2
    pc = {}
    if x_T is None:
        xl = inp["x"][b, half * TL:(half + 1) * TL, :]
        xc = inp["ctx"][b]
        xa = np.concatenate([xl, xc], axis=0)
        x_T = np.ascontiguousarray(xa.T.reshape(KC, 128, T).transpose(1, 0, 2))
    pc["xin"] = x_T
    sc = np.concatenate([inp["c"].T, inp["c_ctx"][:, None]], axis=1)
    pc["scin"] = np.ascontiguousarray(sc.reshape(KC, 128, 5).transpose(1, 0, 2)).reshape(128, 80)
    sel = np.zeros((128, 4), np.float32)
    sel[:, b] = 1.0
    pc["sel"] = sel
    cm = np.zeros((128, 2), np.float32)
    cm[:, 0] = 1.0 if half == 1 else 0.0
    cm[:, 1] = 1.0 if half == 0 else 0.0
    pc["cmask"] = cm
    pc["mrow"] = build_mrow(half)
    return pc


def shard_flat(sh, nl, core):
    flat = np.concatenate([sh[k].reshape(nl, -1) for k in BIGW], axis=1)
    pad = NBW * BLK - flat.shape[1]
    if pad:
        flat = np.concatenate([flat, np.zeros((nl, pad), np.float32)], axis=1)
    s_, q_ = core // 4, core % 4
    h = flat.reshape(nl, NBW, 2, 512 * 2048)[:, :, s_]
    e = h.reshape(nl, NBW // 2, 4, 256 * 2048)[:, :, q_]
    return np.ascontiguousarray(e).reshape(nl, NBW // 2 * 256, 2048)


def core_maps(inp, layers, cores, xs=None):
    sh = prep_shared(inp, layers)
    nl = len(layers)
    maps = []
    for ci, c in enumerate(cores):
        m = {k: v for k, v in sh.items() if k not in BIGW and k not in ("wmod", "bmod")}
        m["wsh"] = shard_flat(sh, nl, c)
        m["wmodsh"] = np.ascontiguousarray(sh["wmod"][:, 18 * c:18 * (c + 1)])
        m["bmodsh"] = np.ascontiguousarray(sh["bmod"][:, :, 18 * c:18 * (c + 1)])
        m.update(prep_core(inp, layers, c, x_T=None if xs is None else xs[ci]))
        maps.append(m)
    return maps


FUSED = True
_PROGS = {}


def _prog(nl, final):
    key = (nl, final)
    if key not in _PROGS:
        _PROGS[key] = K(list(range(nl)), final, ncores=N_CORES).build()
    return _PROGS[key]


def _assemble(outs):
    B = N_CORES // 2
    y = np.empty((B, 2 * TL, D), np.float32)
    for core in range(N_CORES):
        b, half = core // 2, core % 2
        o = np.asarray(outs[core], dtype=np.float32)
        y[b, half * TL:(half + 1) * TL, :] = o.transpose(2, 1, 0).reshape(TL, D)
    return y


def kernel(**inputs):
    inp = {k: np.asarray(v) for k, v in inputs.items()}
    cores = list(range(N_CORES))
    if FUSED:
        maps = core_maps(inp, list(range(DEPTH)), cores)
        res = run_bass_kernel_spmd(_prog(DEPTH, True), maps, core_ids=cores)
        return _assemble([r["out"] for r in res.results])
    xs = None
    res = None
    for l in range(DEPTH):
        maps = core_maps(inp, [l], cores, xs=xs)
        res = run_bass_kernel_spmd(_prog(1, l == DEPTH - 1), maps, core_ids=cores)
        xs = [np.asarray(r["xT"], dtype=np.float32) for r in res.results]
    return _assemble([r["out"] for r in res.results])
```

```python
import numpy as np
from contextlib import ExitStack
import concourse.bass as bass
import concourse.mybir as mybir
from concourse.bass_utils import run_bass_kernel_spmd

F32 = mybir.dt.float32
BF16 = mybir.dt.bfloat16
BIGW = {"wgu1": (44 * 128, 4096), "wd1": (32 * 128, 2816), "wconv": (8 * 128, 4096), "wfm1": (24 * 128, 2048),
        "wintm": (4 * 128, 8192), "wmix": (16 * 128, 9216), "wout": (16 * 128, 2048), "wgu2": (44 * 128, 4096),
        "wd2": (32 * 128, 2816)}
WOFF = {}
_o = 0
for _k, (_r, _c) in BIGW.items():
    WOFF[_k] = _o
    _o += _r * _c
BLK = 2 * (1 << 20)
NBW = ((_o + BLK - 1) // BLK + 1) // 2 * 2
Q_GROUPS = [[0, 1, 2, 3], [4, 5, 6, 7]]
P4_GROUPS = [[0, 4], [1, 5], [2, 6], [3, 7]]
MODW = 96
PUMP = 7
AF = mybir.ActivationFunctionType
ALU = mybir.AluOpType
AX = mybir.AxisListType

D = 2048
KC = 16
DFF = 5632
HFC = 22
T = 1280
TL = 1024
EPS = 1e-6
TBS = [(0, 512, 0), (512, 512, 0), (1024, 256, 1)]
WS_ROWS = [0, 2, 4, 6, 8, 10, 12, 12]
KLOC = 768
AW = 1344
N_CORES = 8
DEPTH = 4


def _key(sem):
    return id(sem)


class Eng:
    def __init__(self, k, raw, name):
        self.k, self.raw, self.name = k, raw, name
        self.sem = k.es.enter_context(k.nc.semaphore("tick_" + name))
        self.n = 0
        self.seen = {}

    def wait(self, *deps):
        for d in deps:
            if d is None:
                continue
            if isinstance(d, list):
                self.wait(*d)
                continue
            sem, v = d
            kk = _key(sem)
            if self.seen.get(kk, 0) < v:
                self.raw.wait_ge(sem, v)
                self.seen[kk] = v

    def mark(self, ins):
        self.n += 1
        ins.then_inc(self.sem, 1)
        return (self.sem, self.n)


class Slot:
    def __init__(self, k, name):
        self.sem = k.es.enter_context(k.nc.semaphore("dma_" + name))
        self.n = 0

    def dma(self, eng, out, in_):
        ins = eng.raw.dma_start(out=out, in_=in_)
        self.n += 16
        ins.then_inc(self.sem, 16)
        return (self.sem, self.n)

    def dep(self):
        return (self.sem, self.n)


class K:
    def __init__(self, layers, final, dbg=None, ncores=8):
        self.dbg = dbg
        self.ncores = ncores
        self.layers = layers
        self.NL = len(layers)
        self.final = final
        self.nc = bass.Bass("TRN2", target_bir_lowering=False)
        self.es = ExitStack()
        self.uid = 0
        self.slots = {}
        self.bar_n = 0
        self.ccn = 0

    def uniq(self, s):
        self.uid += 1
        return "%s_%d" % (s, self.uid)

    def slot(self, name):
        if name not in self.slots:
            self.slots[name] = Slot(self, name)
        return self.slots[name]

    def sb(self, st, name, shape, dt):
        return st.enter_context(self.nc.sbuf_tensor(self.uniq(name), shape, dt))

    def act(self, out, in_, func, bias=None, scale=None, deps=(), accum_out=None):
        self.actE.wait(*deps)
        kw = {}
        if bias is not None:
            kw["bias"] = bias
        if scale is not None:
            kw["scale"] = scale
        if accum_out is not None:
            kw["accum_out"] = accum_out
        return self.actE.mark(self.nc.scalar.activation(out=out, in_=in_, func=func, **kw))

    def tt(self, out, in0, in1, op, deps=(), eng=None):
        e = eng or self.dve
        e.wait(*deps)
        return e.mark(e.raw.tensor_tensor(out=out, in0=in0, in1=in1, op=op))

    def stt(self, out, in0, scalar, in1, op0, op1, deps=()):
        self.dve.wait(*deps)
        return self.dve.mark(self.nc.vector.scalar_tensor_tensor(out=out, in0=in0, scalar=scalar, in1=in1, op0=op0, op1=op1))

    def ts(self, out, in0, s1, s2, op0, op1=None, deps=(), eng=None):
        e = eng or self.dve
        e.wait(*deps)
        if op1 is None:
            return e.mark(e.raw.tensor_scalar(out=out, in0=in0, scalar1=s1, scalar2=None, op0=op0))
        return e.mark(e.raw.tensor_scalar(out=out, in0=in0, scalar1=s1, scalar2=s2, op0=op0, op1=op1))

    def cp(self, out, in_, deps=(), eng=None):
        e = eng or self.dve
        e.wait(*deps)
        if e is self.actE:
            return e.mark(self.nc.scalar.copy(out=out, in_=in_))
        return e.mark(e.raw.tensor_copy(out=out, in_=in_))

    def recip(self, out, in_, deps=()):
        self.dve.wait(*deps)
        return self.dve.mark(self.nc.vector.reciprocal(out=out, in_=in_))

    def mm(self, out, lhsT, rhs, start, stop):
        return self.nc.tensor.matmul(out, lhsT=lhsT, rhs=rhs, start=start, stop=stop)

    def slab(self, name, li, s):
        cols = BIGW[name][1]
        flat = self.Wf[li].ap().rearrange("r c -> (r c)")
        o = WOFF[name] + s * 128 * cols
        return flat[o:o + 128 * cols].rearrange("(p c) -> p c", c=cols)

    def wdep(self, name, li):
        rows, cols = BIGW[name]
        m_last = min((WOFF[name] + rows * cols - 1) // BLK + 2, NBW - 1)
        return (self.wg_sem, self.s2_seq[(li, m_last)] + 1)

    def make_jobs(self):
        self.s2_seq = {}
        seq = 0
        for li in range(self.NL):
            for i in range(NBW // 2):
                self.jobs.append(("cp", li, i))
            for i in range(NBW // 2):
                self.jobs.append(("s1", li, i))
                seq += 1
            for m in range(NBW):
                self.jobs.append(("s2", li, m))
                self.s2_seq[(li, m)] = seq
                seq += 1
        self.s1_end = {}
        c = 0
        for li in range(self.NL):
            c += NBW // 2
            self.s1_end[li] = c
            c += NBW

    def pump(self, quota, upto_layer=None):
        nc = self.nc
        n = 0
        while self.job_i < len(self.jobs) and n < quota:
            kind, li, i = self.jobs[self.job_i]
            if upto_layer is not None and li > upto_layer:
                break
            if kind == "cp":
                self.slot("wgcp").dma(self.pool, self.Wex[li].ap()[i * 256:(i + 1) * 256, :], self.wsh[li, i * 256:(i + 1) * 256, :])
            elif kind == "s1":
                if i == 0:
                    self.pool.wait(self.slot("wgcp").dep())
                nc.gpsimd.collective_compute("AllGather", ALU.bypass, replica_groups=Q_GROUPS,
                                             ins=[self.Wex[li].ap()[i * 256:(i + 1) * 256, :]],
                                             outs=[self.Wh[li].ap()[i * 1024:(i + 1) * 1024, :]]).then_inc(self.wg_sem, 1)
                n += 1
            else:
                if i == 0:
                    self.pool.wait((self.wg_sem, self.s1_end[li]))
                nc.gpsimd.collective_compute("AllGather", ALU.bypass, replica_groups=P4_GROUPS,
                                             ins=[self.Wh[li].ap()[i * 512:(i + 1) * 512, :]],
                                             outs=[self.Wf[li].ap()[i * 1024:(i + 1) * 1024, :]]).then_inc(self.wg_sem, 1)
                n += 1
            self.job_i += 1

    def prologue_mods(self):
        nc = self.nc
        with ExitStack() as st:
            wb = [self.sb(st, "wmodb", [128, 6 * 2048], BF16) for _ in range(3)]
            slots = [self.slot("w%d" % i) for i in range(3)]
            bm = self.sb(st, "bmodl", [128, self.NL, 18], F32)
            dbm = self.slot("misc").dma(self.sp, bm[:, :, :], self.bmodsh.rearrange("l p c -> p l c"))
            ml = [self.sb(st, "modl", [128, MODW], F32) for _ in range(2)]
            pe_last = {}
            n = 0
            ml_free = [None, None]
            for li in range(self.NL):
                bk = self.next_bank()
                self.pe.wait(self.bank_free[bk])
                ps = self.ps[bk]
                for piece in range(3):
                    b = n % 3
                    n += 1
                    self.pool.wait(pe_last.get(b))
                    d = slots[b].dma(self.pool, wb[b][:, :].rearrange("p (s c) -> p s c", c=2048),
                                     self.wmodsh[li, piece * 6:(piece + 1) * 6].rearrange("s p c -> p s c"))
                    self.pe.wait(d)
                    for o in range(6):
                        j = piece * 6 + o
                        for k in range(KC):
                            ins = self.mm(ps[:, 5 * j:5 * j + 5], wb[b][:, o * 2048 + k * 128:o * 2048 + (k + 1) * 128],
                                          self.scT[:, 5 * k:5 * k + 5], k == 0, k == KC - 1)
                    pe_last[b] = self.pe.mark(ins)
                a = li % 2
                dz = self.pool.mark(nc.gpsimd.memset(ml[a][:, 90:MODW], 0.0)) if li < 2 else None
                self.dve.wait(pe_last[(n - 1) % 3], dbm, ml_free[a], dz)
                for col in range(5):
                    dd = self.dve.mark(nc.vector.tensor_tensor(out=ml[a][:, col:90:5], in0=ps[:, col:90:5], in1=bm[:, li, :], op=ALU.add))
                self.bank_free[bk] = dd
                self.pool.wait(dd)
                dst = self.slot("misc2").dma(self.pool, self.Emod[li].ap()[:, :], ml[a][:, :])
                ml_free[a] = dst
                self.pool.wait(dst)
                nc.gpsimd.collective_compute("AllGather", ALU.bypass, replica_groups=Q_GROUPS,
                                             ins=[self.Emod[li].ap()[:, :]], outs=[self.Hmod[li].ap()[:, :]]).then_inc(self.cc_sem, 1)
                self.ccn += 1
                self.pool.wait((self.cc_sem, self.ccn))
                nc.gpsimd.collective_compute("AllGather", ALU.bypass, replica_groups=P4_GROUPS,
                                             ins=[self.Hmod[li].ap()[:, :]], outs=[self.Fmod[li].ap()[:, :]]).then_inc(self.cc_sem, 1)
                self.ccn += 1
            self.pool.wait((self.cc_sem, self.ccn))
        self.barrier()

    def next_bank(self):
        b = self.bank_i % len(self.ps)
        self.bank_i += 1
        return b

    def barrier(self):
        self.bar_n += 1
        if self.jobs and getattr(self, "cur_layer", None) is not None:
            self.pump(PUMP, upto_layer=self.cur_layer + 1)
        for s in self.slots.values():
            if s.n:
                self.sp.wait(s.dep())
        if self.ccn:
            self.sp.wait((self.cc_sem, self.ccn))
        engs = [self.pe, self.actE, self.dve, self.pool, self.sp]
        for e in engs:
            e.raw.drain().then_inc(self.bar_sem, 1)
        for e in engs:
            e.raw.wait_ge(self.bar_sem, len(engs) * self.bar_n)

    def build(self):
        nc, es = self.nc, self.es
        NL = self.NL
        self.pe = Eng(self, nc.tensor, "pe")
        self.actE = Eng(self, nc.scalar, "act")
        self.dve = Eng(self, nc.vector, "dve")
        self.pool = Eng(self, nc.gpsimd, "pool")
        self.sp = Eng(self, nc.sync, "sp")
        self.wq = self.sp
        self.aq = self.pool
        self.cur_layer = None
        self.bar_sem = es.enter_context(nc.semaphore("bar"))
        self.cc_sem = es.enter_context(nc.semaphore("cc"))

        def din(name, shape):
            return nc.dram_tensor(name, shape, F32, kind="ExternalInput").ap()

        self.xin = din("xin", [128, KC, T])
        self.scin = din("scin", [128, 80])
        self.cmask = din("cmask", [128, 2])
        self.mrow = din("mrow", [128, 5 * KLOC])
        self.normg = din("normg", [NL, 128, 48])
        self.wsh = din("wsh", [NL, NBW // 2 * 256, 2048])
        self.wmodsh = din("wmodsh", [NL, 18, 128, 2048])
        self.bmodsh = din("bmodsh", [NL, 128, 18])
        self.sel = din("sel", [128, 4])
        self.btd = din("bt", [NL, 32 * 128, 768])
        self.Wex = [nc.dram_tensor("w_e%d" % l, [NBW // 2 * 256, 2048], BF16) for l in range(NL)]
        self.Wh = [nc.dram_tensor("w_h%d" % l, [NBW * 512, 2048], BF16) for l in range(NL)]
        self.Wf = [nc.dram_tensor("w_f%d" % l, [NBW * 1024, 2048], BF16) for l in range(NL)]
        self.Emod = [nc.dram_tensor("mod_e%d" % l, [128, MODW], F32) for l in range(NL)]
        self.Hmod = [nc.dram_tensor("mod_h%d" % l, [512, MODW], F32) for l in range(NL)]
        self.Fmod = [nc.dram_tensor("mod_f%d" % l, [1024, MODW], F32) for l in range(NL)]
        self.wg_sem = es.enter_context(nc.semaphore("wg"))
        self.wgn = 0
        self.jobs = []
        self.job_i = 0
        self.convdw = din("convdw", [NL, 128, 248])
        self.convp = din("convp", [NL, 128, 24])
        self.glng = din("glng", [NL, 128, 1024])
        self.glnb = din("glnb", [NL, 128, 1024])
        self.wsT = din("wsT", [NL, 128, 512])
        self.bsb = din("bsb", [NL, 128, 2048])
        self.fnorm = din("fnorm", [128, KC])
        self.ident_in = din("ident", [128, 128])

        self.xT = nc.dram_tensor("xT", [128, KC, T], F32, kind="ExternalOutput").ap()
        if self.final:
            self.outd = nc.dram_tensor("out", [128, KC, TL], F32, kind="ExternalOutput").ap()
        self.a_d = nc.dram_tensor("a_d", [128, 8, T], F32).ap()
        self.q_d = nc.dram_tensor("q_d", [128, 8, T], BF16).ap()
        self.k_d = nc.dram_tensor("k_d", [128, 8, T], BF16).ap()
        self.v_d = nc.dram_tensor("v_d", [10, 128, 1024], BF16).ap()
        self.hT_d = nc.dram_tensor("hT_d", [128, KC, T], BF16).ap()
        dk = {"kind": "ExternalOutput"} if self.dbg else {}
        self.conv_d = nc.dram_tensor("conv_d", [128, 8, T], BF16, **dk).ap()
        self.gm_d = nc.dram_tensor("gm_d", [128, 8, T], BF16, **dk).ap()
        self.attn_d = nc.dram_tensor("attn_d", [128, 8, T], BF16, **dk).ap()
        self.y_d = nc.dram_tensor("y_d", [128, KC, T], BF16, **dk).ap()
        self.EA = nc.dram_tensor("EA", [256, 128], F32)
        self.IA = nc.dram_tensor("IA", [512, 128], F32)
        self.EK = nc.dram_tensor("EK", [256, 2048], BF16)
        self.IK = nc.dram_tensor("IK", [512, 2048], BF16)
        self.EV = nc.dram_tensor("EV", [512, 1024], BF16)
        self.IV = nc.dram_tensor("IV", [1024, 1024], BF16)

        P = lambda name, shape, dt: es.enter_context(nc.sbuf_tensor(name, shape, dt))
        self.ps = [es.enter_context(nc.psum_tensor("ps%d" % i, [128, 512], F32)) for i in range(6)]
        self.pt = [es.enter_context(nc.psum_tensor("pt0", [128, 1024], BF16))]
        self.po = es.enter_context(nc.psum_tensor("po", [128, 512], F32))
        self.po_free = [None] * 8
        self.bank_i = 0
        self.bank_free = [None] * 6
        self.pt_free = [None]
        self.onesf = P("onesf", [128, 128], F32)
        self.identf = P("identf", [128, 128], F32)
        self.identb = P("identb", [128, 128], BF16)
        self.scT = P("scT", [128, 80], BF16)
        self.scf = P("scf", [128, 80], F32)
        self.sel_t = P("sel_t", [128, 4], F32)
        self.modT = P("modT", [128, 2, 144], F32)
        self.normg_t = P("normg_t", [128, 48], F32)
        self.Asc = P("Asc", [128, 3, 2, KC], F32)
        self.CG = P("CG", [128, 3, 2, KC], F32)
        self.fnorm_t = P("fnorm_t", [128, KC], F32)
        self.cmask_t = P("cmask_t", [128, 2], F32)
        self.epsc = P("epsc", [128, 1], F32)
        self.mrow_t = P("mrow_t", [128, 5 * KLOC], F32)

        s0 = self.slot("misc")
        d = [s0.dma(self.sp, self.scf[:, :], self.scin[:, :]),
             s0.dma(self.sp, self.fnorm_t[:, :], self.fnorm[:, :]),
             s0.dma(self.sp, self.cmask_t[:, :], self.cmask[:, :]),
             s0.dma(self.sp, self.mrow_t[:, :], self.mrow[:, :]),
             s0.dma(self.sp, self.sel_t[:, :], self.sel[:, :]),
             s0.dma(self.sp, self.identf[:, :], self.ident_in[:, :])]
        self.pool.mark(nc.gpsimd.memset(self.onesf[:, :], 1.0))
        self.pool.mark(nc.gpsimd.memset(self.epsc[:, :], EPS))
        self.act(self.scT[:, :], self.scf[:, :], AF.Silu, deps=[d[-1]])
        self.cp(self.identb[:, :], self.identf[:, :], deps=[d[-1]])
        self.barrier()

        xsrc = self.xin
        self.prologue_mods()
        self.make_jobs()
        self.pump(10 ** 9, upto_layer=0)
        for li in range(NL):
            self.cur_layer = li
            self.phase_mod(li)
            self.barrier()
            self.phase_ffn(li, 0, xsrc, self.xT)
            xsrc = self.xT
            if self.dbg == "ffn1":
                break
            self.phase_mix(li)
            if self.dbg == "mix":
                break
            self.phase_ffn(li, 2, self.xT, self.xT)
            self.pump(10 ** 9, upto_layer=li + 1)
        if self.final:
            self.phase_final()
            self.barrier()
        return nc

    def phase_mod(self, li):
        nc = self.nc
        with ExitStack() as st:
            ma = self.sb(st, "modall", [128, 8, MODW], F32)
            sm = self.slot("misc")
            sm.dma(self.pool, self.normg_t[:, :], self.normg[li])
            dl = sm.dma(self.pool, ma[:, :, :], self.Fmod[li].ap().rearrange("(c p) f -> p c f", p=128))
            mv = ma[:, :, 0:90].rearrange("p c (j f) -> p c j f", f=5)
            m0 = self.modT[:, 0, :].rearrange("p (c j) -> p c j", j=18)
            m1 = self.modT[:, 1, :].rearrange("p (c j) -> p c j", j=18)
            dd = self.ts(m0, mv[:, :, :, 0], self.sel_t[:, 0:1], None, ALU.mult, deps=[dl])
            for col in range(1, 4):
                dd = self.stt(m0, mv[:, :, :, col], self.sel_t[:, col:col + 1], m0, ALU.mult, ALU.add, deps=[dd])
            dd = self.cp(m1, mv[:, :, :, 4], deps=[dd])
            for j in range(3):
                for tt_ in range(2):
                    sc = self.modT[:, tt_, (3 * j + 1) * 16:(3 * j + 2) * 16]
                    gt = self.modT[:, tt_, (3 * j + 2) * 16:(3 * j + 3) * 16]
                    self.stt(self.Asc[:, j, tt_, :], sc, 1.0, self.normg_t[:, j * 16:(j + 1) * 16], ALU.add, ALU.mult, deps=[dd])
                    self.ts(self.CG[:, j, tt_, :], gt, 0.5 if j != 1 else 1.0, None, ALU.mult, deps=[dd])

    def Bsc(self, j, tt_, k):
        return self.modT[:, tt_, 3 * j * 16 + k:3 * j * 16 + k + 1]

    def phase_norm(self, j, xsrc, hT, final=False):
        nc = self.nc
        NS = 256
        with ExitStack() as st:
            xs = [self.sb(st, "nxs", [128, KC, NS], F32) for _ in range(2)]
            sq = [self.sb(st, "nsq", [128, KC, NS], F32) for _ in range(2)]
            tmp = [self.sb(st, "ntmp", [128, KC, NS], F32) for _ in range(2)]
            std = [self.sb(st, "nstd", [128, NS], F32) for _ in range(2)]
            lsl = [self.slot("nx%d" % i) for i in range(2)]
            ssl = [self.slot("nst%d" % i) for i in range(2)]
            xs_free = [None, None]
            sq_free = [None, None]
            tmp_free = [None, None]
            std_free = [None, None]
            nblk = (TL if final else T) // NS
            for i in range(nblk):
                t0 = i * NS
                tt_ = 0 if t0 < TL else 1
                b = i % 2
                self.sp.wait(xs_free[b])
                ld = lsl[b].dma(self.sp, xs[b][:, :, :], xsrc[:, :, t0:t0 + NS])
                d_sq = self.act(sq[b][:, :, :], xs[b][:, :, :], AF.Square, deps=[ld, sq_free[b]])
                bk = self.next_bank()
                self.pe.wait(self.bank_free[bk], d_sq)
                for k in range(KC):
                    ins = self.mm(self.ps[bk][:, :NS], self.onesf[:, :], sq[b][:, k, :], k == 0, k == KC - 1)
                d_pe = self.pe.mark(ins)
                sq_free[b] = d_pe
                d_std = self.act(std[b][:, :], self.ps[bk][:, :NS], AF.Sqrt, bias=self.epsc[:, 0:1], scale=1.0 / D,
                                 deps=[d_pe, std_free[b]])
                self.bank_free[bk] = d_std
                d_r = self.recip(std[b][:, :], std[b][:, :], deps=[d_std])
                d1 = d2 = None
                for k in range(KC):
                    if final:
                        d1 = self.stt(tmp[b][:, k, :], xs[b][:, k, :], self.fnorm_t[:, k:k + 1], std[b][:, :], ALU.mult, ALU.mult,
                                      deps=[d_r, tmp_free[b]])
                    else:
                        d1 = self.stt(tmp[b][:, k, :], xs[b][:, k, :], self.Asc[:, j, tt_, k:k + 1], std[b][:, :], ALU.mult, ALU.mult,
                                      deps=[d_r, tmp_free[b]])
                        d2 = self.act(hT[:, k, t0:t0 + NS], tmp[b][:, k, :], AF.Identity, bias=self.Bsc(j, tt_, k), deps=[d1])
                xs_free[b] = d1
                std_free[b] = d1
                if final:
                    self.sp.wait(d1)
                    tmp_free[b] = ssl[b].dma(self.sp, self.outd[:, :, t0:t0 + NS], tmp[b][:, :, :])
                else:
                    tmp_free[b] = d2

    def phase_final(self):
        self.phase_norm(0, self.xT, None, final=True)

    def linear_fm(self, wsrc, nslab, slab_cols, units, tbs, epi, nbuf=3, wdep=None):
        if isinstance(units[0], tuple):
            units = [units]
        with ExitStack() as st:
            wb = [self.sb(st, "wslab", [128, slab_cols], BF16) for _ in range(nbuf)]
            slots = [self.slot("w%d" % i) for i in range(nbuf)]
            pe_last = {}
            deps = {}

            def issue(s):
                b = s % nbuf
                self.wq.wait(pe_last.get(b), wdep)
                deps[s] = slots[b].dma(self.wq, wb[b][:, :], wsrc(s))

            for s in range(min(nbuf - 1, nslab)):
                issue(s)
            for s in range(nslab):
                if s + nbuf - 1 < nslab:
                    issue(s + nbuf - 1)
                b = s % nbuf
                self.pe.wait(deps[s])
                pe_dep = None
                for ti, (t0, nt, tt_) in enumerate(tbs):
                    for ui, groups in enumerate(units):
                        banks = []
                        for (src, kcg, off) in groups:
                            bk = self.next_bank()
                            self.pe.wait(self.bank_free[bk])
                            for k in range(kcg):
                                ins = self.mm(self.ps[bk][:, :nt], wb[b][:, off + k * 128:off + (k + 1) * 128],
                                              src[:, k, t0:t0 + nt], k == 0, k == kcg - 1)
                            banks.append(bk)
                        pe_dep = self.pe.mark(ins)
                        rel = epi(s, ti, t0, nt, tt_, ui, [self.ps[bk][:, :nt] for bk in banks], pe_dep)
                        for bk in banks:
                            self.bank_free[bk] = rel
                pe_last[b] = pe_dep

    def linear_tm(self, st, wsrcs, hT, epi, wdep=None):
        wv = [self.sb(st, "wtm", [128, 8192], BF16) for _ in range(2)]
        self.wq.wait(wdep)
        dl = [self.slot("wt%d" % i).dma(self.wq, wv[i][:, :], wsrcs[i]) for i in range(2)]
        self.pe.wait(*dl)
        for tile in range(10):
            banks = []
            for blk in range(2):
                bk = self.next_bank()
                self.pe.wait(self.bank_free[bk])
                for k in range(KC):
                    ins = self.mm(self.ps[bk][:, :], hT[:, k, tile * 128:(tile + 1) * 128], wv[blk][:, k * 512:(k + 1) * 512],
                                  k == 0, k == KC - 1)
                banks.append(bk)
            pe_dep = self.pe.mark(ins)
            rel = epi(tile, [self.ps[bk] for bk in banks], pe_dep)
            for bk in banks:
                self.bank_free[bk] = rel

    def make_residual_epi(self, st, xs_ap, xd_ap, j):
        NR = 4
        xin_t = [self.sb(st, "rxi", [128, 512], F32) for _ in range(NR)]
        xout_t = [self.sb(st, "rxo", [128, 512], F32) for _ in range(NR)]
        lsl = [self.slot("rl%d" % i) for i in range(NR)]
        ssl = [self.slot("rs%d" % i) for i in range(NR)]
        state = {"n": 0, "xin_free": [None] * NR, "st_done": [None] * NR}

        def epi(s, ti, t0, nt, tt_, ui, banks, pe_dep):
            r = state["n"] % NR
            state["n"] += 1
            self.aq.wait(state["xin_free"][r])
            ld = lsl[r].dma(self.aq, xin_t[r][:, :nt], xs_ap[:, s, t0:t0 + nt])
            dd = self.stt(xout_t[r][:, :nt], banks[0], self.CG[:, j, tt_, s:s + 1], xin_t[r][:, :nt], ALU.mult, ALU.add,
                          deps=[pe_dep, ld, state["st_done"][r]])
            state["xin_free"][r] = dd
            self.actE.wait(dd)
            state["st_done"][r] = ssl[r].dma(self.actE, xd_ap[:, s, t0:t0 + nt], xout_t[r][:, :nt])
            return dd
        return epi

    def phase_ffn(self, li, j, xsrc, xdst):
        w = 0 if j == 0 else 1
        with ExitStack() as st:
            hT = self.sb(st, "hT", [128, KC, T], BF16)
            self.phase_norm(j, xsrc, hT)
            self.barrier()
            actT = self.sb(st, "actT", [128, HFC, T], BF16)
            sg = [self.sb(st, "sg", [128, 512], F32) for _ in range(2)]
            for hf in range(2):
                state = {"n": 0, "free": [None, None]}

                def epi_b(s, ti, t0, nt, tt_, ui, banks, pe_dep):
                    b = state["n"] % 2
                    state["n"] += 1
                    d1 = self.act(sg[b][:, :nt], banks[0], AF.Silu, deps=[pe_dep, state["free"][b]])
                    d2 = self.tt(actT[:, s, t0:t0 + nt], sg[b][:, :nt], banks[1], ALU.mult, deps=[d1])
                    state["free"][b] = d2
                    return d2
                self.linear_fm(lambda s: self.slab('wgu%d' % (w + 1), li, hf * HFC + s), HFC, 4096, [(hT, KC, 0), (hT, KC, 2048)], TBS, epi_b, wdep=self.wdep('wgu%d' % (w + 1), li))
                self.barrier()
                with ExitStack() as st2:
                    epi_r = self.make_residual_epi(st2, xsrc if hf == 0 else xdst, xdst, j)
                    self.linear_fm(lambda s: self.slab('wd%d' % (w + 1), li, hf * 16 + s), 16, HFC * 128, [(actT, HFC, 0)], TBS, epi_r, wdep=self.wdep('wd%d' % (w + 1), li))
                    self.barrier()

    def phase_mix(self, li):
        with ExitStack() as stA:
            hT = self.sb(stA, "hT", [128, KC, T], BF16)
            self.phase_norm(1, self.xT, hT)
            self.barrier()
            self.slot("misc").dma(self.sp, self.hT_d[:, :, :], hT[:, :, :])
            self.stage_convin(li, hT)
            self.barrier()
            self.stage_qkv(li, hT)
            self.barrier()
            self.exchange()
            self.stage_gmlp(li, hT)
            self.barrier()
        self.stage_conv(li)
        self.barrier()
        self.stage_attn(li)
        self.barrier()
        self.stage_merge(li)
        self.barrier()
        self.stage_wout(li)
        self.barrier()

    def stage_convin(self, li, hT):
        with ExitStack() as st:
            sgm = [self.sb(st, "cisg", [128, 512], F32) for _ in range(2)]
            ao = [self.sb(st, "ciao", [128, 512], F32) for _ in range(2)]
            ssl = [self.slot("st%d" % i) for i in range(2)]
            state = {"n": 0, "sg_free": [None, None], "st_done": [None, None]}

            def epi(s, ti, t0, nt, tt_, ui, banks, pe_dep):
                b = state["n"] % 2
                state["n"] += 1
                d1 = self.act(sgm[b][:, :nt], banks[1], AF.Sigmoid, deps=[pe_dep, state["sg_free"][b]])
                d2 = self.tt(ao[b][:, :nt], sgm[b][:, :nt], banks[0], ALU.mult, deps=[d1, state["st_done"][b]])
                state["sg_free"][b] = d2
                self.aq.wait(d2)
                state["st_done"][b] = ssl[b].dma(self.aq, self.a_d[:, s, t0:t0 + nt], ao[b][:, :nt])
                return d2
            self.linear_fm(lambda s: self.slab('wconv', li, s), 8, 4096, [(hT, KC, 0), (hT, KC, 2048)], TBS, epi, wdep=self.wdep('wconv', li))

    def stage_qkv(self, li, hT):
        with ExitStack() as st:
            qs = [self.sb(st, "qs", [128, 512], BF16) for _ in range(2)]
            ssl = [self.slot("st%d" % i) for i in range(2)]
            state = {"n": 0, "st_done": [None, None]}

            def epi(s, ti, t0, nt, tt_, ui, banks, pe_dep):
                b = state["n"] % 2
                state["n"] += 1
                isk = s >= 8
                c = s % 8
                if isk:
                    d = self.cp(qs[b][:, :nt], banks[0], deps=[pe_dep, state["st_done"][b]], eng=self.actE)
                else:
                    d = self.ts(qs[b][:, :nt], banks[0], 0.125, None, ALU.mult, deps=[pe_dep, state["st_done"][b]])
                self.aq.wait(d)
                dst = self.k_d if isk else self.q_d
                state["st_done"][b] = ssl[b].dma(self.aq, dst[:, c, t0:t0 + nt], qs[b][:, :nt])
                return d
            self.linear_fm(lambda s: self.slab('wfm1', li, 8 + s), 16, 2048, [(hT, KC, 0)], TBS, epi, wdep=self.wdep('wfm1', li))
            self.barrier()
            vs = [self.sb(st, "vs", [128, 1024], BF16) for _ in range(2)]
            state2 = {"n": 0, "st_done": [None, None]}

            def epi_v(tile, banks, pe_dep):
                b = state2["n"] % 2
                state2["n"] += 1
                d1 = self.cp(vs[b][:, 0:512], banks[0][:, :], deps=[pe_dep, state2["st_done"][b]], eng=self.actE)
                d2 = self.cp(vs[b][:, 512:1024], banks[1][:, :], deps=[pe_dep, state2["st_done"][b]])
                self.aq.wait(d1, d2)
                state2["st_done"][b] = ssl[b].dma(self.aq, self.v_d[tile], vs[b][:, :])
                return [d1, d2]
            self.linear_tm(st, [self.slab('wintm', li, 2), self.slab('wintm', li, 3)], hT, epi_v, wdep=self.wdep('wintm', li))

    def exchange(self):
        nc = self.nc
        s = self.slot("exp")
        EA, EK, EV = self.EA.ap(), self.EK.ap(), self.EV.ap()
        for side, (a0, k0, vt) in enumerate([(0, 0, 0), (TL - 16, TL - 256, 6)]):
            s.dma(self.aq, EA[side * 128:(side + 1) * 128, :].rearrange("p (c j) -> p c j", j=16), self.a_d[:, :, a0:a0 + 16])
            s.dma(self.aq, EK[side * 128:(side + 1) * 128, :].rearrange("p (c j) -> p c j", j=256), self.k_d[:, :, k0:k0 + 256])
            s.dma(self.aq, EV[side * 256:(side + 1) * 256, :].rearrange("(t p) f -> t p f", p=128), self.v_d[vt:vt + 2])
        self.pool.wait(s.dep())
        groups = [[2 * i, 2 * i + 1] for i in range(self.ncores // 2)]
        for (E, I) in ((self.EA, self.IA), (self.EK, self.IK), (self.EV, self.IV)):
            nc.gpsimd.collective_compute("AllGather", ALU.bypass, replica_groups=groups,
                                         ins=[E.ap()[:, :]], outs=[I.ap()[:, :]]).then_inc(self.cc_sem, 1)
            self.ccn += 1
            self.pool.wait((self.cc_sem, self.ccn))

    def stage_gmlp(self, li, hT):
        nc = self.nc
        with ExitStack() as st:
            vln = self.sb(st, "vln", [128, 10, 1024], BF16)
            glng_t = self.sb(st, "glng", [128, 1024], F32)
            glnb_t = self.sb(st, "glnb", [128, 1024], F32)
            wsT_t = self.sb(st, "wsT", [128, 512], BF16)
            bsb_t = self.sb(st, "bsb", [128, 2048], F32)
            sm = self.slot("misc")
            dl = [sm.dma(self.aq, glng_t[:, :], self.glng[li]), sm.dma(self.aq, glnb_t[:, :], self.glnb[li]),
                  sm.dma(self.aq, bsb_t[:, :], self.bsb[li])]
            dws = self.slot("misc2").dma(self.pool, wsT_t[:, :], self.wsT[li])
            with ExitStack() as st1:
                vf = [self.sb(st1, "vf", [128, 1024], F32) for _ in range(2)]
                stats = [self.sb(st1, "vstat", [128, 12], F32) for _ in range(2)]
                mv = [self.sb(st1, "vmv", [128, 2], F32) for _ in range(2)]
                sd = [self.sb(st1, "vsd", [128, 1], F32) for _ in range(2)]
                state = {"n": 0, "free": [None, None]}

                def epi_v(tile, banks, pe_dep):
                    b = state["n"] % 2
                    state["n"] += 1
                    g1 = self.act(vf[b][:, 0:512], banks[0][:, :], AF.Gelu_apprx_tanh, deps=[pe_dep, state["free"][b]])
                    g2 = self.act(vf[b][:, 512:1024], banks[1][:, :], AF.Gelu_apprx_tanh, deps=[pe_dep])
                    self.dve.wait(g1, g2)
                    self.dve.mark(nc.vector.bn_stats(out=stats[b][:, 0:6], in_=vf[b][:, 0:512]))
                    d = self.dve.mark(nc.vector.bn_stats(out=stats[b][:, 6:12], in_=vf[b][:, 512:1024]))
                    self.dve.wait(d)
                    d = self.dve.mark(nc.vector.bn_aggr(out=mv[b][:, :], in_=stats[b][:, :]))
                    d = self.act(sd[b][:, :], mv[b][:, 1:2], AF.Sqrt, bias=self.epsc[:, 0:1], deps=[d])
                    d = self.recip(sd[b][:, :], sd[b][:, :], deps=[d])
                    d = self.ts(vf[b][:, :], vf[b][:, :], mv[b][:, 0:1], sd[b][:, 0:1], ALU.subtract, ALU.mult, deps=[d])
                    d = self.tt(vf[b][:, :], vf[b][:, :], glng_t[:, :], ALU.mult, deps=[d, dl[2]])
                    d = self.tt(vln[:, tile, :], vf[b][:, :], glnb_t[:, :], ALU.add, deps=[d, dl[2]])
                    state["free"][b] = d
                    return [g1, g2]
                self.linear_tm(st1, [self.slab('wintm', li, 0), self.slab('wintm', li, 1)], hT, epi_v, wdep=self.wdep('wintm', li))
                self.barrier()
            uf = [self.sb(st, "guf", [128, 512], F32) for _ in range(2)]
            tmp = [self.sb(st, "gtmp", [128, 512], F32) for _ in range(2)]
            go = [self.sb(st, "ggo", [128, 512], BF16) for _ in range(2)]
            ssl = [self.slot("st%d" % i) for i in range(2)]
            state2 = {"n": 0, "uf_free": [None, None], "st_done": [None, None]}

            def epi_u(s, ti, t0, nt, tt_, ui, banks, pe_dep):
                b = state2["n"] % 2
                state2["n"] += 1
                g = s // 2
                d1 = self.act(uf[b][:, :nt], banks[0], AF.Gelu_apprx_tanh, deps=[pe_dep, state2["uf_free"][b]])
                bk2 = self.next_bank()
                self.pe.wait(self.bank_free[bk2], dws)
                for n_ in range(nt // 128):
                    tile = t0 // 128 + n_
                    ins = self.mm(self.ps[bk2][:, n_ * 128:(n_ + 1) * 128], vln[:, tile, s * 128:(s + 1) * 128],
                                  wsT_t[:, g * 128:(g + 1) * 128], True, True)
                dpe2 = self.pe.mark(ins)
                d2 = self.tt(tmp[b][:, :nt], self.ps[bk2][:, :nt], bsb_t[:, g * 512:g * 512 + nt], ALU.add,
                             deps=[dpe2, dl[2], state2["uf_free"][b]])
                self.bank_free[bk2] = d2
                d3 = self.tt(go[b][:, :nt], tmp[b][:, :nt], uf[b][:, :nt], ALU.mult, deps=[d2, d1, state2["st_done"][b]])
                state2["uf_free"][b] = d3
                self.aq.wait(d3)
                state2["st_done"][b] = ssl[b].dma(self.aq, self.gm_d[:, s, t0:t0 + nt], go[b][:, :nt])
                return d1
            self.linear_fm(lambda s: self.slab('wfm1', li, s), 8, 2048, [(hT, KC, 0)], TBS, epi_u, wdep=self.wdep('wfm1', li))

    def stage_conv(self, li):
        nc = self.nc
        with ExitStack() as st:
            aT = self.sb(st, "caT", [128, 8, AW], BF16)
            y = self.sb(st, "cy", [128, 8, T], F32)
            cT = self.sb(st, "ccT", [128, 8, T], BF16)
            dw = self.sb(st, "cdw", [128, 248], F32)
            cpp = self.sb(st, "ccp", [128, 24], F32)
            sq = self.sb(st, "csq", [128, 8, 512], F32)
            mean = self.sb(st, "cmean", [128, 512], F32)
            m2 = self.sb(st, "cm2", [128, 512], F32)
            rstd = self.sb(st, "crstd", [128, 512], F32)
            zt = [self.sb(st, "czt", [128, 512], F32) for _ in range(2)]
            dg = [self.sb(st, "cdg", [128, 31, 128], BF16) for _ in range(2)]
            sl = self.slot("misc")
            IA = self.IA.ap()
            dl = [sl.dma(self.aq, dw[:, :], self.convdw[li]), sl.dma(self.aq, cpp[:, :], self.convp[li]),
                  sl.dma(self.aq, aT[:, :, 16:16 + TL], self.a_d[:, :, 0:TL]),
                  sl.dma(self.aq, aT[:, :, 1072:1328], self.a_d[:, :, TL:T]),
                  sl.dma(self.aq, aT[:, :, 0:16], IA[128:256, :].rearrange("p (c j) -> p c j", j=16)),
                  sl.dma(self.aq, aT[:, :, 1040:1056], IA[256:384, :].rearrange("p (c j) -> p c j", j=16))]
            dall = dl[-1]
            dz = self.pool.mark(nc.gpsimd.memset(aT[:, :, 1056:1072], 0.0))
            dz = self.pool.mark(nc.gpsimd.memset(aT[:, :, 1328:1344], 0.0))
            dm = self.ts(aT[:, :, 0:16], aT[:, :, 0:16], self.cmask_t[:, 0:1], None, ALU.mult, deps=[dall])
            dm = self.ts(aT[:, :, 1040:1056], aT[:, :, 1040:1056], self.cmask_t[:, 1:2], None, ALU.mult, deps=[dall])
            dg_free = [None, None]
            d = None
            for cc in range(8):
                gb = cc % 2
                dgd = None
                for k in range(31):
                    dgd = self.ts(dg[gb][:, k, :], self.identb[:, :], dw[:, cc * 31 + k:cc * 31 + k + 1], None, ALU.mult,
                                  deps=[dall, dg_free[gb]])
                for (o0, yo, nt) in ((0, 0, 512), (512, 512, 512), (1056, TL, 256)):
                    bk = self.next_bank()
                    self.pe.wait(self.bank_free[bk], dgd, dm, dz)
                    for k in range(31):
                        ins = self.mm(self.ps[bk][:, :nt], dg[gb][:, k, :], aT[:, cc, o0 + 1 + k:o0 + 1 + k + nt], k == 0, k == 30)
                    dpe = self.pe.mark(ins)
                    d = self.act(y[:, cc, yo:yo + nt], self.ps[bk][:, :nt], AF.Identity, bias=cpp[:, cc:cc + 1], deps=[dpe, dall])
                    self.bank_free[bk] = d
                dg_free[gb] = dpe
            dconv = d
            zfree = [None, None]
            zi = 0
            dlast = None
            for (t0, nt, tt_) in TBS:
                dsq = self.act(sq[:, :, :nt], y[:, :, t0:t0 + nt], AF.Square, deps=[dconv, dlast])
                b1 = self.next_bank()
                self.pe.wait(self.bank_free[b1], dconv)
                for cc in range(8):
                    ins = self.mm(self.ps[b1][:, :nt], self.onesf[:, :], y[:, cc, t0:t0 + nt], cc == 0, cc == 7)
                dp1 = self.pe.mark(ins)
                b2 = self.next_bank()
                self.pe.wait(self.bank_free[b2], dsq)
                for cc in range(8):
                    ins = self.mm(self.ps[b2][:, :nt], self.onesf[:, :], sq[:, cc, :nt], cc == 0, cc == 7)
                dp2 = self.pe.mark(ins)
                dmean = self.act(mean[:, :nt], self.ps[b1][:, :nt], AF.Identity, scale=1.0 / 1024, deps=[dp1, dlast])
                self.bank_free[b1] = dmean
                d = self.tt(m2[:, :nt], mean[:, :nt], mean[:, :nt], ALU.mult, deps=[dmean])
                d = self.stt(m2[:, :nt], self.ps[b2][:, :nt], 1.0 / 1024, m2[:, :nt], ALU.mult, ALU.subtract, deps=[d, dp2])
                self.bank_free[b2] = d
                d = self.act(rstd[:, :nt], m2[:, :nt], AF.Sqrt, bias=self.epsc[:, 0:1], deps=[d])
                drs = self.recip(rstd[:, :nt], rstd[:, :nt], deps=[d])
                for cc in range(8):
                    zb = zi % 2
                    zi += 1
                    d = self.tt(zt[zb][:, :nt], y[:, cc, t0:t0 + nt], mean[:, :nt], ALU.subtract, deps=[drs, zfree[zb]])
                    d = self.tt(zt[zb][:, :nt], zt[zb][:, :nt], rstd[:, :nt], ALU.mult, deps=[d])
                    d = self.act(cT[:, cc, t0:t0 + nt], zt[zb][:, :nt], AF.Silu, bias=cpp[:, 16 + cc:17 + cc], scale=cpp[:, 8 + cc:9 + cc],
                                 deps=[d])
                    zfree[zb] = d
                dlast = d
            self.aq.wait(dlast)
            self.slot("misc").dma(self.aq, self.conv_d[:, :, :], cT[:, :, :])

    def stage_attn(self, li):
        nc = self.nc
        with ExitStack() as st:
            Kb = [self.sb(st, "aK", [128, 1792], BF16) for _ in range(2)]
            Qb = [self.sb(st, "aQ", [128, T], BF16) for _ in range(2)]
            Vb = [self.sb(st, "aV", [128, 14, 128], BF16) for _ in range(2)]
            NBI = 10
            bias_t = [self.sb(st, "abias", [128, KLOC], F32) for _ in range(NBI)]
            NRG = 3
            S = [self.sb(st, "aS", [128, 1024], F32) for _ in range(NRG)]
            Pm = [self.sb(st, "aP", [128, 1024], BF16) for _ in range(NRG)]
            PT = [self.sb(st, "aPT", [128, 1024], BF16) for _ in range(NRG)]
            Osb = [self.sb(st, "aO", [128, 10, 128], BF16) for _ in range(2)]
            ao = [self.sb(st, "aao", [128, T], BF16) for _ in range(2)]
            nmx = [self.sb(st, "anmx", [128, 1], F32) for _ in range(NRG)]
            rsum = [self.sb(st, "arsum", [128, 1], F32) for _ in range(NRG)]
            rinv = [self.sb(st, "arinv", [128, 1], F32) for _ in range(NRG)]
            kslot = [self.slot("akqv%d" % i) for i in range(2)]
            bt_t = [self.sb(st, "abt", [128, KLOC], F32) for _ in range(4)]
            bslot = [self.slot("ab%d" % i) for i in range(4)]
            bt_free = [None] * 4
            oslot = [self.slot("st%d" % i) for i in range(2)]
            IK, IV = self.IK.ap(), self.IV.ap()
            kqv_free = [None, None]
            bias_free = [None] * NBI
            s_free = [None] * NRG
            ocnt = 0
            ao_done = [None, None]
            osb_free = [None, None]
            un = 0
            for c in range(8):
                b = c % 2
                self.sp.wait(kqv_free[b])
                sl = kslot[b]
                sl.dma(self.sp, Kb[b][:, 256:1280], self.k_d[:, c, 0:TL])
                sl.dma(self.sp, Kb[b][:, 1536:1792], self.k_d[:, c, TL:T])
                sl.dma(self.sp, Kb[b][:, 0:256], IK[128:256, c * 256:(c + 1) * 256])
                sl.dma(self.sp, Kb[b][:, 1280:1536], IK[256:384, c * 256:(c + 1) * 256])
                sl.dma(self.sp, Qb[b][:, :], self.q_d[:, c, :])
                sl.dma(self.sp, Vb[b][:, 2:10, :], self.v_d[0:8, :, c * 128:(c + 1) * 128].rearrange("t p f -> p t f"))
                sl.dma(self.sp, Vb[b][:, 12:14, :], self.v_d[8:10, :, c * 128:(c + 1) * 128].rearrange("t p f -> p t f"))
                sl.dma(self.sp, Vb[b][:, 0:2, :], IV[256:512, c * 128:(c + 1) * 128].rearrange("(t p) f -> p t f", p=128))
                dkqv = sl.dma(self.sp, Vb[b][:, 10:12, :], IV[512:768, c * 128:(c + 1) * 128].rearrange("(t p) f -> p t f", p=128))
                last_pe = None
                units = []
                for hh in range(2):
                    h = 2 * c + hh
                    p0 = hh * 64
                    bdeps = []
                    btd = []
                    for v in range(2):
                        ti_ = (h % 2) * 2 + v
                        self.sp.wait(bt_free[ti_])
                        btd.append(bslot[ti_].dma(self.sp, bt_t[ti_][:, :], self.btd[li, (h * 2 + v) * 128:(h * 2 + v + 1) * 128, :]))
                    for slot_i in range(5):
                        bi = (h % 2) * 5 + slot_i
                        v = 1 if slot_i == 4 else 0
                        ti_ = (h % 2) * 2 + v
                        dd_ = self.tt(bias_t[bi][:, :], bt_t[ti_][:, :], self.mrow_t[:, slot_i * KLOC:(slot_i + 1) * KLOC], ALU.add,
                                      deps=[btd[v], bias_free[bi]])
                        bdeps.append(dd_)
                        bt_free[ti_] = dd_
                    for u in range(10):
                        units.append({"h": h, "p0": p0, "u": u, "bdeps": bdeps})

                def st_a(U):
                    nonlocal un, ocnt
                    h, p0, u = U["h"], U["p0"], U["u"]
                    r = un % NRG
                    U["ceng"] = self.actE if (un % 2 == 0) else self.dve
                    un += 1
                    lat = u < 8
                    qc = u * 128
                    nk = 1024 if lat else 256
                    sA = self.next_bank()
                    sB = self.next_bank() if lat else None
                    U["osl"] = ocnt % 8
                    ocnt += 1
                    U.update(r=r, lat=lat, nk=nk)
                    self.pe.wait(dkqv, self.bank_free[sA])
                    qT = Qb[b][p0:p0 + 64, qc:qc + 128]
                    if lat:
                        koff = WS_ROWS[u] * 64
                        self.mm(self.ps[sA][:, :], qT, Kb[b][p0:p0 + 64, koff:koff + 512], True, True)
                        self.pe.wait(self.bank_free[sB])
                        self.mm(self.ps[sB][:, 0:256], qT, Kb[b][p0:p0 + 64, koff + 512:koff + 768], True, True)
                        ins = self.mm(self.ps[sB][:, 256:512], qT, Kb[b][p0:p0 + 64, 1536:1792], True, True)
                    else:
                        ins = self.mm(self.ps[sA][:, 0:256], qT, Kb[b][p0:p0 + 64, 1536:1792], True, True)
                    dS = self.pe.mark(ins)
                    if lat:
                        bi = (h % 2) * 5 + SLOT_OF_RP[u]
                        d1 = self.tt(S[r][:, 0:512], self.ps[sA][:, :], bias_t[bi][:, 0:512], ALU.add,
                                     deps=[dS, U["bdeps"][SLOT_OF_RP[u]], s_free[r]])
                        d2 = self.tt(S[r][:, 512:768], self.ps[sB][:, 0:256], bias_t[bi][:, 512:768], ALU.add, deps=[dS])
                        bias_free[bi] = d2
                        d3 = self.cp(S[r][:, 768:1024], self.ps[sB][:, 256:512], deps=[dS, s_free[r]], eng=self.actE)
                        self.bank_free[sA] = d1
                        self.bank_free[sB] = [d2, d3]
                        dsc = [d1, d2, d3]
                    else:
                        d3 = self.cp(S[r][:, 0:256], self.ps[sA][:, 0:256], deps=[dS, s_free[r]], eng=self.actE)
                        self.bank_free[sA] = d3
                        dsc = [d3]
                    self.dve.wait(*dsc)
                    dmx = self.dve.mark(nc.vector.tensor_reduce(out=nmx[r][:, :], in_=S[r][:, :nk], axis=AX.X, op=ALU.max, negate=True))
                    U["dex"] = self.act(Pm[r][:, :nk], S[r][:, :nk], AF.Exp, bias=nmx[r][:, 0:1], deps=[dmx], accum_out=rsum[r][:, 0:1])

                def st_b(U):
                    nonlocal last_pe, dos
                    r, nk, u, p0, lat, osl = U["r"], U["nk"], U["u"], U["p0"], U["lat"], U["osl"]
                    self.pe.wait(U["dex"], self.pt_free[0])
                    for jj in range(nk // 128):
                        ins = nc.tensor.transpose(self.pt[0][:, jj * 128:(jj + 1) * 128], Pm[r][:, jj * 128:(jj + 1) * 128], self.identb[:, :])
                    dT = self.pe.mark(ins)
                    dcp = self.cp(PT[r][:, :nk], self.pt[0][:, :nk], deps=[dT], eng=U["ceng"])
                    self.pt_free[0] = dcp
                    oP = self.po[:, osl * 64:(osl + 1) * 64]
                    self.pe.wait(dcp, self.po_free[osl], dos)
                    nj = nk // 128
                    for jj in range(nj):
                        if lat:
                            vt = (WS_ROWS[u] // 2 + jj) if jj < 6 else (12 + jj - 6)
                        else:
                            vt = 12 + jj
                        ins = self.mm(oP, PT[r][:, jj * 128:(jj + 1) * 128], Vb[b][:, vt, p0:p0 + 64], jj == 0, jj == nj - 1)
                    dO = self.pe.mark(ins)
                    last_pe = dO
                    dri = self.recip(rinv[r][:, :], rsum[r][:, :], deps=[U["dex"]])
                    dos = self.ts(Osb[b][:, u, p0:p0 + 64], oP, rinv[r][:, 0:1], None, ALU.mult, deps=[dO, dri, osb_free[b]])
                    self.po_free[osl] = dos
                    s_free[r] = [dcp, dos, dO]

                dos = None
                NU = len(units)
                for it in range(NU + 1):
                    if it < NU:
                        st_a(units[it])
                    if it >= 1:
                        st_b(units[it - 1])
                kqv_free[b] = last_pe
                for (u0, nu) in ((0, 8), (8, 2)):
                    pb = 0
                    self.pe.wait(dos, self.pt_free[pb])
                    for uu in range(nu):
                        ins = nc.tensor.transpose(self.pt[pb][:, uu * 128:(uu + 1) * 128], Osb[b][:, u0 + uu, :], self.identb[:, :])
                    dT = self.pe.mark(ins)
                    dcp = self.cp(ao[b][:, u0 * 128:(u0 + nu) * 128], self.pt[pb][:, :nu * 128], deps=[dT, ao_done[b]])
                    self.pt_free[pb] = dcp
                osb_free[b] = dT
                self.aq.wait(dcp)
                ao_done[b] = oslot[b].dma(self.aq, self.attn_d[:, c, :], ao[b][:, :])

    def stage_merge(self, li):
        with ExitStack() as st:
            hT = self.sb(st, "mhT", [128, KC, T], BF16)
            bT = [self.sb(st, "mbT", [128, 8, T], BF16) for _ in range(3)]
            sl = self.slot("misc")
            dl = [sl.dma(self.sp, hT[:, :, :], self.hT_d[:, :, :])]
            for i, src in enumerate((self.conv_d, self.gm_d, self.attn_d)):
                dl.append(sl.dma(self.sp, bT[i][:, :, :], src[:, :, :]))
            self.pe.wait(dl[-1])
            sgt = [self.sb(st, "msg", [128, 512], F32) for _ in range(2)]
            tmp = [self.sb(st, "mtmp", [128, 512], F32) for _ in range(2)]
            acc = [self.sb(st, "macc", [128, 512], F32) for _ in range(2)]
            ys = [self.sb(st, "mys", [128, 512], BF16) for _ in range(2)]
            ssl = [self.slot("st%d" % i) for i in range(2)]
            state = {"n": 0, "sg_free": [None, None], "acc_dep": [None, None], "acc_free": [None, None], "st_done": [None, None],
                     "m": 0}

            def epi(s, ti, t0, nt, tt_, ui, banks, pe_dep):
                b = state["n"] % 2
                state["n"] += 1
                a = state["m"] % 2
                d1 = self.act(sgt[b][:, :nt], banks[0], AF.Sigmoid, deps=[pe_dep, state["sg_free"][b]])
                if ui == 0:
                    d2 = self.tt(acc[a][:, :nt], sgt[b][:, :nt], banks[1], ALU.mult, deps=[d1, state["acc_free"][a]])
                    state["acc_dep"][a] = d2
                    state["sg_free"][b] = d2
                    return d2
                d2 = self.tt(tmp[b][:, :nt], sgt[b][:, :nt], banks[1], ALU.mult, deps=[d1])
                if ui == 1:
                    d3 = self.tt(acc[a][:, :nt], acc[a][:, :nt], tmp[b][:, :nt], ALU.add, deps=[d2, state["acc_dep"][a]])
                    state["acc_dep"][a] = d3
                    state["sg_free"][b] = d3
                    return d2
                d3 = self.tt(ys[a][:, :nt], acc[a][:, :nt], tmp[b][:, :nt], ALU.add, deps=[d2, state["acc_dep"][a], state["st_done"][a]])
                state["sg_free"][b] = d3
                state["acc_free"][a] = d3
                state["m"] += 1
                self.aq.wait(d3)
                state["st_done"][a] = ssl[a].dma(self.aq, self.y_d[:, s, t0:t0 + nt], ys[a][:, :nt])
                return d2
            units = [[(hT, KC, br * 2048), (bT[br], 8, 6144 + br * 1024)] for br in range(3)]
            self.linear_fm(lambda s: self.slab('wmix', li, s), 16, 9216, units, TBS, epi, nbuf=2, wdep=self.wdep('wmix', li))

    def stage_wout(self, li):
        with ExitStack() as st:
            yT = self.sb(st, "yT", [128, KC, T], BF16)
            d = self.slot("misc").dma(self.sp, yT[:, :, :], self.y_d[:, :, :])
            self.pe.wait(d)
            epi_r = self.make_residual_epi(st, self.xT, self.xT, 1)
            self.linear_fm(lambda s: self.slab('wout', li, s), 16, 2048, [(yT, KC, 0)], TBS, epi_r, wdep=self.wdep('wout', li))


def fm_slabs(w, kc):
    K_, N_ = w.shape
    return np.ascontiguousarray(w.reshape(kc, 128, N_ // 128, 128).transpose(2, 1, 0, 3)).reshape(N_ // 128, 128, kc * 128)


def fvec(v):
    return np.ascontiguousarray(v.reshape(-1, 128).T)


SLOT_OF_RP = [0, 1, 2, 2, 2, 2, 3, 4]
REP_RP = [0, 1, 2, 6, 7]


def build_bt(rpb):
    ri = np.arange(2)[:, None, None, None]
    c = np.arange(64)[None, :, None, None]
    j = np.arange(12)[None, None, :, None]
    kc = np.arange(64)[None, None, None, :]
    cs = np.clip(c - 8, 0, 48)
    vcol = np.broadcast_to((kc >= cs) & (kc < cs + 16), (2, 64, 12, 64))
    dc = np.broadcast_to(np.clip(kc - c, -15, 15) + 15, (2, 64, 12, 64))
    out = np.empty((16, 2, 128, KLOC), np.float32)
    for v, off in enumerate((3, 1)):
        dr = np.broadcast_to(np.clip(j - ri + off, 0, 14), (2, 64, 12, 64))
        vals = rpb[:, dr, dc]
        vals = np.where(vcol[None], vals, np.float32(-1e9))
        out[:, v] = vals.reshape(16, 128, KLOC)
    return out


def build_mrow(half):
    R0 = 16 * half
    out = np.empty((128, 5, KLOC), np.float32)
    ri = np.arange(2)[:, None, None, None]
    j = np.arange(12)[None, None, :, None]
    for slot, rp in enumerate(REP_RP):
        r = R0 + 2 * rp + ri
        g = WS_ROWS[rp] + j + R0 - 4
        rs = np.clip(r - 4, 0, 24)
        vrow = (g >= 0) & (g < 32) & (g >= rs) & (g < rs + 8)
        m = np.where(np.broadcast_to(vrow, (2, 64, 12, 64)), np.float32(0.0), np.float32(-1e9))
        out[:, slot] = m.reshape(128, KLOC)
    return out.reshape(128, 5 * KLOC)


def prep_shared(inp, layers):
    sh = {}
    L = layers
    st = lambda f: np.stack([f(i) for i in L]).astype(np.float32, copy=False)
    sh["wmod"] = st(lambda i: fm_slabs(inp["w_mod"][i], KC))
    sh["bmod"] = st(lambda i: fvec(inp["b_mod"][i]))
    sh["normg"] = st(lambda i: np.concatenate([fvec(inp["norm_ffn1"][i]), fvec(inp["norm_mix"][i]), fvec(inp["norm_ffn2"][i])], axis=1))
    for nm, src in (("wgu1", "ffn1_w_gu"), ("wgu2", "ffn2_w_gu")):
        sh[nm] = st(lambda i: np.concatenate([fm_slabs(inp[src][i][:, :DFF], KC), fm_slabs(inp[src][i][:, DFF:], KC)], axis=2))
    for nm, src in (("wd1", "ffn1_w_down"), ("wd2", "ffn2_w_down")):
        sh[nm] = st(lambda i: np.concatenate([fm_slabs(inp[src][i][hf * 2816:(hf + 1) * 2816], HFC) for hf in range(2)], axis=0))
    win = inp["w_in"]
    sh["wconv"] = st(lambda i: np.concatenate([fm_slabs(win[i][:, 0:1024], KC), fm_slabs(win[i][:, 1024:2048], KC)], axis=2))
    sh["wfm1"] = st(lambda i: np.concatenate([fm_slabs(win[i][:, 2048:3072], KC), fm_slabs(win[i][:, 4096:5120], KC),
                                              fm_slabs(win[i][:, 5120:6144], KC)], axis=0))

    def tm(w):
        return np.ascontiguousarray(w.reshape(KC, 128, 512).transpose(1, 0, 2)).reshape(128, KC * 512)
    sh["wintm"] = st(lambda i: np.stack([tm(win[i][:, 3072:3584]), tm(win[i][:, 3584:4096]),
                                         tm(win[i][:, 6144:6656]), tm(win[i][:, 6656:7168])]))

    def mix(i):
        g = [fm_slabs(win[i][:, 7168 + b * D:7168 + (b + 1) * D], KC) for b in range(3)]
        o = [fm_slabs(inp[n][i], 8) for n in ("w_conv_out", "w_gmlp_out", "w_attn_out")]
        return np.concatenate(g + o, axis=2)
    sh["wmix"] = st(mix)
    sh["wout"] = st(lambda i: fm_slabs(inp["w_out"][i], KC))
    sh["convdw"] = st(lambda i: np.ascontiguousarray(inp["conv_dw"][i].reshape(31, 8, 128).transpose(2, 1, 0)).reshape(128, 248))
    sh["convp"] = st(lambda i: np.concatenate([fvec(inp["conv_db"][i]), fvec(inp["conv_ln_g"][i]), fvec(inp["conv_ln_b"][i])], axis=1))
    sh["glng"] = st(lambda i: np.broadcast_to(inp["gmlp_ln_g"][i][None, :], (128, 1024)))
    sh["glnb"] = st(lambda i: np.broadcast_to(inp["gmlp_ln_b"][i][None, :], (128, 1024)))
    sh["wsT"] = st(lambda i: np.ascontiguousarray(inp["gmlp_ws"][i].transpose(2, 0, 1)).reshape(128, 512))
    sh["bsb"] = st(lambda i: np.broadcast_to(np.broadcast_to(inp["gmlp_bs"][i][:, None, :], (4, 4, 128)).reshape(1, 2048), (128, 2048)))
    sh["bt"] = st(lambda i: build_bt(inp["attn_rpb"][i])).reshape(len(L), 32 * 128, 768)
    sh["fnorm"] = fvec(inp["final_norm"]).astype(np.float32)
    sh["ident"] = np.eye(128, dtype=np.float32)
    return {k: np.ascontiguousarray(v) for k, v in sh.items()}


def prep_core(inp, layers, core, x_T=None):
    b, half = core // 2, core % 2
    pc = {}
    if x_T is None:
        xl = inp["x"][b, half * TL:(half + 1) * TL, :]
        xc = inp["ctx"][b]
        xa = np.concatenate([xl, xc], axis=0)
        x_T = np.ascontiguousarray(xa.T.reshape(KC, 128, T).transpose(1, 0, 2))
    pc["xin"] = x_T
    sc = np.concatenate([inp["c"].T, inp["c_ctx"][:, None]], axis=1)
    pc["scin"] = np.ascontiguousarray(sc.reshape(KC, 128, 5).transpose(1, 0, 2)).reshape(128, 80)
    sel = np.zeros((128, 4), np.float32)
    sel[:, b] = 1.0
    pc["sel"] = sel
    cm = np.zeros((128, 2), np.float32)
    cm[:, 0] = 1.0 if half == 1 else 0.0
    cm[:, 1] = 1.0 if half == 0 else 0.0
    pc["cmask"] = cm
    pc["mrow"] = build_mrow(half)
    return pc


def shard_flat(sh, nl, core):
    flat = np.concatenate([sh[k].reshape(nl, -1) for k in BIGW], axis=1)
    pad = NBW * BLK - flat.shape[1]
    if pad:
        flat = np.concatenate([flat, np.zeros((nl, pad), np.float32)], axis=1)
    s_, q_ = core // 4, core % 4
    h = flat.reshape(nl, NBW, 2, 512 * 2048)[:, :, s_]
    e = h.reshape(nl, NBW // 2, 4, 256 * 2048)[:, :, q_]
    return np.ascontiguousarray(e).reshape(nl, NBW // 2 * 256, 2048)


def core_maps(inp, layers, cores, xs=None):
    sh = prep_shared(inp, layers)
    nl = len(layers)
    maps = []
    for ci, c in enumerate(cores):
        m = {k: v for k, v in sh.items() if k not in BIGW and k not in ("wmod", "bmod")}
        m["wsh"] = shard_flat(sh, nl, c)
        m["wmodsh"] = np.ascontiguousarray(sh["wmod"][:, 18 * c:18 * (c + 1)])
        m["bmodsh"] = np.ascontiguousarray(sh["bmod"][:, :, 18 * c:18 * (c + 1)])
        m.update(prep_core(inp, layers, c, x_T=None if xs is None else xs[ci]))
        maps.append(m)
    return maps


FUSED = True
_PROGS = {}


def _prog(nl, final):
    key = (nl, final)
    if key not in _PROGS:
        _PROGS[key] = K(list(range(nl)), final, ncores=N_CORES).build()
    return _PROGS[key]


def _assemble(outs):
    B = N_CORES // 2
    y = np.empty((B, 2 * TL, D), np.float32)
    for core in range(N_CORES):
        b, half = core // 2, core % 2
        o = np.asarray(outs[core], dtype=np.float32)
        y[b, half * TL:(half + 1) * TL, :] = o.transpose(2, 1, 0).reshape(TL, D)
    return y


def kernel(**inputs):
    inp = {k: np.asarray(v) for k, v in inputs.items()}
    cores = list(range(N_CORES))
    if FUSED:
        maps = core_maps(inp, list(range(DEPTH)), cores)
        res = run_bass_kernel_spmd(_prog(DEPTH, True), maps, core_ids=cores)
        return _assemble([r["out"] for r in res.results])
    xs = None
    res = None
    for l in range(DEPTH):
        maps = core_maps(inp, [l], cores, xs=xs)
        res = run_bass_kernel_spmd(_prog(1, l == DEPTH - 1), maps, core_ids=cores)
        xs = [np.asarray(r["xT"], dtype=np.float32) for r in res.results]
    return _assemble([r["out"] for r in res.results])
```
